# Optimizing a Trainium2 kernel written in Bass

```python
import math
import jax
import jax.numpy as jnp
from jax import lax
import numpy as np


D_MODEL = 1024
BATCH = 8
SEQ = 4096
DEPTH = 2

GRID_W = 64
CTX_LEN = 256
N_BRANCH = 3
MIX_W = 512
SHORT_CONV = 4
DN_H = 4
DN_DK = 128
DN_DV = MIX_W // DN_H
DN_W = DN_H * DN_DK
DN_CHUNK = 64
LRU_W = MIX_W
LRU_G = 8
LRU_BS = LRU_W // LRU_G
LRU_C = 8.0
DA_H = 4
DA_D = 64
DA_DV = MIX_W // DA_H
DA_QW = DA_H * 2 * DA_D
ATTN_BLOCK = 128
ROPE_BASE = 10000.0
ROPE_NF = DA_D // 4
D_FF = -(-8 * D_MODEL // (3 * 256)) * 256
SPLIT = (DN_W, DN_W, DN_H * DN_DV, DN_H * DN_DV, 2 * DN_H, 2 * DN_H, LRU_W, LRU_W, DA_QW, DA_QW, DA_H * DA_DV, N_BRANCH * D_MODEL)
IN_COLS = sum(SPLIT)

kernel_name = 'hybrid_gdn_rglru_diffattn_prefix_dit'


def rmsnorm(x, w, eps=1e-6):
    xf = x.astype(jnp.float32)
    y = xf * lax.rsqrt(jnp.mean(xf * xf, axis=-1, keepdims=True) + eps) * w.astype(jnp.float32)
    return y.astype(x.dtype)


def l2norm(x, eps=1e-6):
    xf = x.astype(jnp.float32)
    return xf * lax.rsqrt(jnp.sum(xf * xf, axis=-1, keepdims=True) + eps)


def modulate(h, shift, scale):
    return h * (1.0 + scale[:, None]) + shift[:, None]


def split_cols(p):
    out, start = [], 0
    for size in SPLIT:
        out.append(p[..., start:start + size])
        start += size
    return out


def flip_t(t):
    return jnp.flip(t, axis=1)


def ident_t(t):
    return t


def centred_dwconv(x, w):
    K = w.shape[0]
    left = (K - 1) // 2
    T = x.shape[1]
    xp = jnp.pad(x, ((0, 0), (left, K - 1 - left), (0, 0)))
    y = xp[:, 0:T] * w[0]
    for j in range(1, K):
        y = y + xp[:, j:j + T] * w[j]
    return y


def axial_angles(n_tokens):
    rows = n_tokens // GRID_W
    row = jnp.repeat(jnp.arange(rows, dtype=jnp.float32), GRID_W)
    col = jnp.tile(jnp.arange(GRID_W, dtype=jnp.float32), rows)
    inv = ROPE_BASE ** (-jnp.arange(ROPE_NF, dtype=jnp.float32) / ROPE_NF)
    return jnp.stack([row[:, None] * inv, col[:, None] * inv], axis=1)


def rope_2d(x, rope_cs):
    cos, sin = rope_cs
    B, T, H, Dh = x.shape
    xr = x.astype(jnp.float32).reshape(B, T, H, 2, 2, Dh // 4)
    x1, x2 = xr[..., 0, :], xr[..., 1, :]
    cos = cos[None, :, None]
    sin = sin[None, :, None]
    out = jnp.stack([x1 * cos - x2 * sin, x2 * cos + x1 * sin], axis=-2)
    return out.reshape(B, T, H, Dh).astype(x.dtype)


def gated_delta_chunked(q, k, v, g, beta, s0):
    B, T, H, DK = q.shape
    DV = v.shape[-1]
    C = DN_CHUNK
    N = T // C

    def to_chunks(t):
        t = t.astype(jnp.float32).reshape((B, N, C, H) + t.shape[3:])
        return jnp.moveaxis(t, (1, 3), (0, 2))

    q = to_chunks(q) * (DK ** -0.5)
    k = to_chunks(k)
    v = to_chunks(v)
    beta = to_chunks(beta)
    g = jnp.cumsum(to_chunks(g), axis=-1)
    incl = jnp.tril(jnp.ones((C, C), dtype=bool))
    strict = jnp.tril(jnp.ones((C, C), dtype=bool), -1)
    diff = g[..., :, None] - g[..., None, :]
    decay = jnp.where(incl, jnp.exp(jnp.where(incl, diff, 0.0)), 0.0)
    kb = k * beta[..., None]
    lower = jnp.where(strict, jnp.einsum('nbhid,nbhjd->nbhij', kb, k) * decay, 0.0)
    a_mat = lower + jnp.eye(C, dtype=jnp.float32)
    w = lax.linalg.triangular_solve(a_mat, kb * jnp.exp(g)[..., None], left_side=True, lower=True, unit_diagonal=True)
    u = lax.linalg.triangular_solve(a_mat, v * beta[..., None], left_side=True, lower=True, unit_diagonal=True)
    qk = jnp.where(incl, jnp.einsum('nbhid,nbhjd->nbhij', q, k) * decay, 0.0)
    q_in = q * jnp.exp(g)[..., None]
    k_out = k * jnp.exp(g[..., -1:] - g)[..., None]
    g_end = jnp.exp(g[..., -1])

    def step(s, inp):
        q_n, w_n, u_n, qk_n, ko_n, ge_n = inp
        v_new = u_n - jnp.einsum('bhck,bhkv->bhcv', w_n, s)
        o = jnp.einsum('bhck,bhkv->bhcv', q_n, s) + jnp.einsum('bhij,bhjv->bhiv', qk_n, v_new)
        s = s * ge_n[..., None, None] + jnp.einsum('bhck,bhcv->bhkv', ko_n, v_new)
        return s, o

    s_fin, o = lax.scan(step, s0.astype(jnp.float32), (q_in, w, u, qk, k_out, g_end))
    o = jnp.moveaxis(o, (0, 2), (1, 3)).reshape(B, T, H, DV)
    return o, s_fin


def deltanet_branch(pc, pl, conv_w, a_log, dt_bias, out_norm, need_ctx):
    def prep(q, k, v, a, b):
        B, T, _ = q.shape
        qkv = jax.nn.silu(centred_dwconv(jnp.concatenate([q, k, v], axis=-1), conv_w))
        q, k, v = jnp.split(qkv, [DN_W, 2 * DN_W], axis=-1)
        q = l2norm(q.reshape(B, T, DN_H, DN_DK))
        k = l2norm(k.reshape(B, T, DN_H, DN_DK))
        v = v.reshape(B, T, DN_H, DN_DV)
        g = -jnp.exp(a_log.astype(jnp.float32)) * jax.nn.softplus(a.reshape(B, T, 2, DN_H).astype(jnp.float32) + dt_bias.astype(jnp.float32))
        bt = jax.nn.sigmoid(b.reshape(B, T, 2, DN_H).astype(jnp.float32))
        return q, k, v, g, bt

    qc, kc, vc, gc, bc = prep(pc[0], pc[1], pc[2], pc[4], pc[5])
    ql, kl, vl, gl, bl = prep(pl[0], pl[1], pl[2], pl[4], pl[5])
    B = ql.shape[0]
    outs_c, outs_l = [], []
    for d in range(2):
        f = flip_t if d else ident_t
        s0 = jnp.zeros((B, DN_H, DN_DK, DN_DV), jnp.float32)
        oc, s_ctx = gated_delta_chunked(f(qc), f(kc), f(vc), f(gc[:, :, d]), f(bc[:, :, d]), s0)
        ol, _ = gated_delta_chunked(f(ql), f(kl), f(vl), f(gl[:, :, d]), f(bl[:, :, d]), s_ctx)
        outs_c.append(f(oc))
        outs_l.append(f(ol))

    def finish(o, z):
        B, T = z.shape[:2]
        o = rmsnorm(o, out_norm) * jax.nn.silu(z.reshape(B, T, DN_H, DN_DV).astype(jnp.float32))
        return o.reshape(B, T, DN_H * DN_DV).astype(z.dtype)

    y_l = finish(outs_l[0] + outs_l[1], pl[3])
    y_c = finish(outs_c[0] + outs_c[1], pc[3]) if need_ctx else None
    return y_c, y_l


def rglru_coeffs(xb, wa, ba, wi, bi, lam):
    B, T, W = xb.shape
    xg = xb.reshape(B, T, LRU_G, LRU_BS)
    r = jax.nn.sigmoid((jnp.einsum('btgi,gij->btgj', xg, wa).reshape(B, T, W) + ba).astype(jnp.float32))
    i = jax.nn.sigmoid((jnp.einsum('btgi,gij->btgj', xg, wi).reshape(B, T, W) + bi).astype(jnp.float32))
    log_a = -LRU_C * r * jax.nn.softplus(-lam.astype(jnp.float32))
    a = jnp.exp(log_a)
    b = jnp.sqrt(1.0 - jnp.exp(2.0 * log_a)) * (i * xb.astype(jnp.float32))
    return a, b


def linear_scan(a, b, h0):
    def combine(e1, e2):
        a1, b1 = e1
        a2, b2 = e2
        return a1 * a2, a2 * b1 + b2
    a_cum, b_cum = lax.associative_scan(combine, (a, b), axis=1)
    return b_cum + a_cum * h0[:, None, :]


def rglru_branch(pc, pl, conv_w, conv_b, wa, ba, wi, bi, lam, need_ctx):
    xc = centred_dwconv(pc[0], conv_w) + conv_b
    xl = centred_dwconv(pl[0], conv_w) + conv_b
    B = xl.shape[0]
    hs_c, hs_l = [], []
    for d in range(2):
        f = flip_t if d else ident_t
        a, b = rglru_coeffs(f(xc), wa[d], ba[d], wi[d], bi[d], lam[d])
        h_c = linear_scan(a, b, jnp.zeros((B, LRU_W), jnp.float32))
        a, b = rglru_coeffs(f(xl), wa[d], ba[d], wi[d], bi[d], lam[d])
        h_l = linear_scan(a, b, h_c[:, -1])
        hs_c.append(f(h_c))
        hs_l.append(f(h_l))

    def finish(h, y):
        return (h * jax.nn.gelu(y.astype(jnp.float32))).astype(y.dtype)

    y_l = finish(hs_l[0] + hs_l[1], pl[1])
    y_c = finish(hs_c[0] + hs_c[1], pc[1]) if need_ctx else None
    return y_c, y_l


def diff_attend(q1, q2, k1, k2, v, lam):
    scale = DA_D ** -0.5
    s1 = jnp.einsum('bqhd,bkhd->bhqk', q1.astype(jnp.float32), k1) * scale
    s2 = jnp.einsum('bqhd,bkhd->bhqk', q2.astype(jnp.float32), k2) * scale
    p = jax.nn.softmax(s1, axis=-1) - lam * jax.nn.softmax(s2, axis=-1)
    return jnp.einsum('bhqk,bkhv->bqhv', p, v)


def diffattn_branch(pc, pl, rope_cs, lam_vecs, sub_norm, lam_init, need_ctx):
    def heads(q, k, v):
        B, T, _ = q.shape
        q = q.reshape(B, T, DA_H, 2, DA_D)
        k = k.reshape(B, T, DA_H, 2, DA_D)
        return q[..., 0, :], q[..., 1, :], k[..., 0, :], k[..., 1, :], v.reshape(B, T, DA_H, DA_DV)

    q1c, q2c, k1c, k2c, vc = heads(pc[0], pc[1], pc[2])
    q1l, q2l, k1l, k2l, vl = heads(pl[0], pl[1], pl[2])
    q1l = rope_2d(q1l, rope_cs)
    q2l = rope_2d(q2l, rope_cs)
    k1l = rope_2d(k1l, rope_cs)
    k2l = rope_2d(k2l, rope_cs)
    lv = lam_vecs.astype(jnp.float32)
    lam = jnp.exp(jnp.sum(lv[0] * lv[1])) - jnp.exp(jnp.sum(lv[2] * lv[3])) + lam_init
    k1c32, k2c32, vc32 = k1c.astype(jnp.float32), k2c.astype(jnp.float32), vc.astype(jnp.float32)
    k1 = jnp.concatenate([k1c32, k1l.astype(jnp.float32)], axis=1)
    k2 = jnp.concatenate([k2c32, k2l.astype(jnp.float32)], axis=1)
    v = jnp.concatenate([vc32, vl.astype(jnp.float32)], axis=1)
    B, T = q1l.shape[:2]
    nb = T // ATTN_BLOCK

    def blocks(t):
        return jnp.moveaxis(t.reshape(B, nb, ATTN_BLOCK, DA_H, DA_D), 1, 0)

    o_l = lax.map(lambda qq: diff_attend(qq[0], qq[1], k1, k2, v, lam), (blocks(q1l), blocks(q2l)))
    o_l = jnp.moveaxis(o_l, 0, 1).reshape(B, T, DA_H, DA_DV)

    def finish(o, like):
        o = rmsnorm(o, sub_norm, 1e-5) * (1.0 - lam_init)
        return o.reshape(o.shape[0], o.shape[1], DA_H * DA_DV).astype(like.dtype)

    y_l = finish(o_l, pl[2])
    if need_ctx:
        y_c = finish(diff_attend(q1c, q2c, k1c32, k2c32, vc32, lam), pc[2])
    else:
        y_c = None
    return y_c, y_l


def merge_branches(ya, yb, yc, gate_logits, w_branch, w_out):
    B, T, _ = ya.shape
    up = jnp.einsum('btnm,nmd->btnd', jnp.stack([ya, yb, yc], axis=2), w_branch)
    gates = jax.nn.sigmoid(gate_logits.reshape(B, T, N_BRANCH, D_MODEL))
    return jnp.einsum('btnd,btnd->btd', gates, up) @ w_out


def mixer_sublayer(u_c, u_l, rope_cs, lam_init, need_ctx, w_in, dn_conv, dn_a_log, dn_dt_bias, dn_norm, lru_conv_w, lru_conv_b, lru_wa, lru_ba, lru_wi, lru_bi, lru_lambda, da_lambda, da_norm, w_branch, w_out):
    pc = split_cols(u_c @ w_in)
    pl = split_cols(u_l @ w_in)
    ya_c, ya_l = deltanet_branch(pc[0:6], pl[0:6], dn_conv, dn_a_log, dn_dt_bias, dn_norm, need_ctx)
    yb_c, yb_l = rglru_branch(pc[6:8], pl[6:8], lru_conv_w, lru_conv_b, lru_wa, lru_ba, lru_wi, lru_bi, lru_lambda, need_ctx)
    yc_c, yc_l = diffattn_branch(pc[8:11], pl[8:11], rope_cs, da_lambda, da_norm, lam_init, need_ctx)
    y_l = merge_branches(ya_l, yb_l, yc_l, pl[11], w_branch, w_out)
    y_c = merge_branches(ya_c, yb_c, yc_c, pc[11], w_branch, w_out) if need_ctx else None
    return y_c, y_l


def swiglu(h, wg, wu, wd):
    return (jax.nn.silu(h @ wg) * (h @ wu)) @ wd


def setup_inputs(seed: int = 0) -> dict:
    key = jax.random.key(seed)
    ks = jax.random.split(key, 28)
    L, D = DEPTH, D_MODEL
    f32 = jnp.float32

    def nrm(k, shape, scale):
        return jax.random.normal(k, shape, f32) * scale

    dt = jnp.exp(jax.random.uniform(ks[11], (L, 2, DN_H), f32, minval=math.log(1e-3), maxval=math.log(1e-1)))
    a_pow = jax.random.uniform(ks[19], (L, 2, LRU_W), f32, minval=0.9, maxval=0.999)
    a_base = a_pow ** (1.0 / LRU_C)
    return {
        'x': nrm(ks[0], (BATCH, SEQ, D), 1.0),
        'c': nrm(ks[1], (BATCH, D), 1.0),
        'ctx': nrm(ks[2], (BATCH, CTX_LEN, D), 1.0),
        'c_ctx': nrm(ks[3], (D,), 1.0),
        'w_mod': nrm(ks[4], (L, D, 6 * D), D ** -0.5),
        'b_mod': nrm(ks[5], (L, 6 * D), 0.02),
        'norm_mix': 1.0 + nrm(ks[6], (L, D), 0.02),
        'norm_ffn': 1.0 + nrm(ks[7], (L, D), 0.02),
        'w_in': nrm(ks[8], (L, D, IN_COLS), D ** -0.5),
        'dn_conv': nrm(ks[9], (L, SHORT_CONV, 3 * DN_W), SHORT_CONV ** -0.5),
        'dn_a_log': jnp.log(jax.random.uniform(ks[10], (L, 2, DN_H), f32, minval=1.0, maxval=16.0)),
        'dn_dt_bias': dt + jnp.log(-jnp.expm1(-dt)),
        'dn_norm': 1.0 + nrm(ks[12], (L, DN_DV), 0.02),
        'lru_conv_w': nrm(ks[13], (L, SHORT_CONV, LRU_W), SHORT_CONV ** -0.5),
        'lru_conv_b': nrm(ks[14], (L, LRU_W), 0.01),
        'lru_wa': nrm(ks[15], (L, 2, LRU_G, LRU_BS, LRU_BS), LRU_BS ** -0.5),
        'lru_ba': nrm(ks[16], (L, 2, LRU_W), 0.01),
        'lru_wi': nrm(ks[17], (L, 2, LRU_G, LRU_BS, LRU_BS), LRU_BS ** -0.5),
        'lru_bi': nrm(ks[18], (L, 2, LRU_W), 0.01),
        'lru_lambda': jnp.log(a_base) - jnp.log1p(-a_base),
        'da_lambda': nrm(ks[20], (L, 4, DA_D), 0.1),
        'da_norm': 1.0 + nrm(ks[21], (L, DA_DV), 0.02),
        'w_branch': nrm(ks[22], (L, N_BRANCH, MIX_W, D), MIX_W ** -0.5),
        'w_out': nrm(ks[23], (L, D, D), D ** -0.5),
        'w_ffn_gate': nrm(ks[24], (L, D, D_FF), D ** -0.5),
        'w_ffn_up': nrm(ks[25], (L, D, D_FF), D ** -0.5),
        'w_ffn_down': nrm(ks[26], (L, D_FF, D), D_FF ** -0.5),
        'norm_final': 1.0 + nrm(ks[27], (D,), 0.02),
    }


def reference(x, c, ctx, c_ctx, w_mod, b_mod, norm_mix, norm_ffn, w_in, dn_conv, dn_a_log, dn_dt_bias, dn_norm, lru_conv_w, lru_conv_b, lru_wa, lru_ba, lru_wi, lru_bi, lru_lambda, da_lambda, da_norm, w_branch, w_out, w_ffn_gate, w_ffn_up, w_ffn_down, norm_final):
    ang = axial_angles(x.shape[1])
    rope_cs = (jnp.cos(ang), jnp.sin(ang))
    s_lat = jax.nn.silu(c)
    s_ctx = jax.nn.silu(c_ctx)[None]
    h_lat, h_ctx = x, ctx
    for l in range(DEPTH):
        need_ctx = l < DEPTH - 1
        lam_init = 0.8 - 0.6 * math.exp(-0.3 * l)
        mod_l = jnp.split(s_lat @ w_mod[l] + b_mod[l], 6, axis=-1)
        mod_c = jnp.split(s_ctx @ w_mod[l] + b_mod[l], 6, axis=-1)
        u_l = modulate(rmsnorm(h_lat, norm_mix[l]), mod_l[0], mod_l[1])
        u_c = modulate(rmsnorm(h_ctx, norm_mix[l]), mod_c[0], mod_c[1])
        y_c, y_l = mixer_sublayer(u_c, u_l, rope_cs, lam_init, need_ctx, w_in[l], dn_conv[l], dn_a_log[l], dn_dt_bias[l], dn_norm[l], lru_conv_w[l], lru_conv_b[l], lru_wa[l], lru_ba[l], lru_wi[l], lru_bi[l], lru_lambda[l], da_lambda[l], da_norm[l], w_branch[l], w_out[l])
        h_lat = h_lat + mod_l[2][:, None] * y_l
        h_lat = h_lat + mod_l[5][:, None] * swiglu(modulate(rmsnorm(h_lat, norm_ffn[l]), mod_l[3], mod_l[4]), w_ffn_gate[l], w_ffn_up[l], w_ffn_down[l])
        if need_ctx:
            h_ctx = h_ctx + mod_c[2][:, None] * y_c
            h_ctx = h_ctx + mod_c[5][:, None] * swiglu(modulate(rmsnorm(h_ctx, norm_ffn[l]), mod_c[3], mod_c[4]), w_ffn_gate[l], w_ffn_up[l], w_ffn_down[l])
    return rmsnorm(h_lat, norm_final)
```

```python
import math
import numpy as np
from contextlib import ExitStack
import concourse.bass as bass
import concourse.mybir as mybir
from concourse.bass_utils import run_bass_kernel_spmd
from concourse.alu_op_type import AluOpType as ALU

AF = mybir.ActivationFunctionType
F32 = mybir.dt.float32
BF16 = mybir.dt.bfloat16

D = 1024
NCTX = 256
NLAT = 4096
T = NCTX + NLAT
DEPTH = 2
DFF = 2816
INC = 7696
BLOCKS = [(0, 256)] + [(256 + 512 * j, 512) for j in range(8)]
NEG = -30000.0


class Buf:
    def __init__(self, ap, name, share=None):
        self.ap = ap
        self.name = name
        self.st = share.st if share is not None else [None, []]

    @property
    def w(self):
        return self.st[0]

    @w.setter
    def w(self, v):
        self.st[0] = v

    @property
    def r(self):
        return self.st[1]

    @r.setter
    def r(self, v):
        self.st[1] = v

    def __getitem__(self, k):
        return self.ap[k]


class K:
    NDMA = 8

    def __init__(self, nc, es):
        self.nc = nc
        self.es = es
        self.eng = {"pe": nc.tensor, "act": nc.scalar, "dve": nc.vector,
                    "pool": nc.gpsimd, "sp": nc.sync}
        self.sem = {}
        self.cnt = {}
        for e in self.eng:
            self.sem[e] = es.enter_context(nc.semaphore("s_" + e))
            self.cnt[e] = 0
        self.dring = {}
        for q in ("sp", "pool"):
            ring = []
            for i in range(self.NDMA):
                key = "d_%s%d" % (q, i)
                self.sem[key] = es.enter_context(nc.semaphore(key))
                self.cnt[key] = 0
                ring.append(key)
            self.dring[q] = ring
        self.dpos = {q: 0 for q in self.dring}
        self.seen = {e: {} for e in self.eng}
        self.nbuf = 0
        self.ninst = 0
        self.rr = 0

    def sb(self, shape, dt, name=None):
        self.nbuf += 1
        name = (name or "sb") + "_%d" % self.nbuf
        t = self.es.enter_context(self.nc.sbuf_tensor(name, list(shape), dt))
        return Buf(t, name)

    def ps(self, shape, dt, name=None):
        self.nbuf += 1
        name = (name or "ps") + "_%d" % self.nbuf
        t = self.es.enter_context(self.nc.psum_tensor(name, list(shape), dt))
        return Buf(t, name)

    def _deps(self, R, W):
        d = {}

        def add(ev):
            if ev is None:
                return
            k, v = ev
            if d.get(k, 0) < v:
                d[k] = v
        for b in R:
            add(b.w)
        for b in W:
            add(b.w)
            for ev in b.r:
                add(ev)
        return d

    def _wait(self, e, d):
        eng = self.eng[e]
        seen = self.seen[e]
        for k, v in d.items():
            if e == "pe" and k == "pe":
                continue
            if seen.get(k, 0) >= v:
                continue
            eng.wait_ge(self.sem[k], v)
            seen[k] = v

    def _mark(self, ev, R, W):
        for b in R:
            b.r = [x for x in b.r if x[0] != ev[0]] + [ev]
        for b in W:
            b.w = ev
            b.r = []

    def op(self, e, fn, R=(), W=()):
        d = self._deps(R, W)
        self._wait(e, d)
        inst = fn(self.eng[e])
        self.cnt[e] += 1
        inst.then_inc(self.sem[e], 1)
        self._mark((e, self.cnt[e]), R, W)
        self.ninst += 1

    def dma(self, q, out, in_, R=(), W=(), **kw):
        ring = self.dring[q]
        key = ring[self.dpos[q] % self.NDMA]
        self.dpos[q] += 1
        d = self._deps(R, W)
        if self.cnt[key] > 0 and d.get(key, 0) < self.cnt[key]:
            d[key] = self.cnt[key]
        self._wait(q, d)
        inst = self.eng[q].dma_start(out=out, in_=in_, **kw)
        self.cnt[key] += 16
        inst.then_inc(self.sem[key], 16)
        self._mark((key, self.cnt[key]), R, W)
        self.ninst += 1

    def barrier(self):
        d = {k: v for k, v in self.cnt.items() if v > 0}
        for e in self.eng:
            self._wait(e, dict(d))

    def alt(self):
        self.rr += 1
        return "act" if self.rr % 2 else "dve"


def tt(k, e, out, a, b, op, R, W):
    k.op(e, lambda g: g.tensor_tensor(out=out, in0=a, in1=b, op=op), R=R, W=W)


def ts(k, e, out, a, s1, s2, op0, op1, R, W):
    if op1 is None:
        k.op(e, lambda g: g.tensor_scalar(out=out, in0=a, scalar1=s1, scalar2=None, op0=op0), R=R, W=W)
    else:
        k.op(e, lambda g: g.tensor_scalar(out=out, in0=a, scalar1=s1, scalar2=s2, op0=op0, op1=op1), R=R, W=W)


def stt(k, out, a, s, b, op0, op1, R, W):
    k.op("dve", lambda g: g.scalar_tensor_tensor(out=out, in0=a, scalar=s, in1=b, op0=op0, op1=op1), R=R, W=W)


def act(k, out, a, func, R, W, scale=None, bias=None):
    kw = {}
    if scale is not None:
        kw["scale"] = scale
    if bias is not None:
        kw["bias"] = bias
    k.op("act", lambda g: g.activation(out=out, in_=a, func=func, **kw), R=R, W=W)


def cp(k, e, out, a, R, W):
    if e == "act":
        k.op("act", lambda g: g.activation(out=out, in_=a, func=AF.Copy), R=R, W=W)
    else:
        k.op(e, lambda g: g.tensor_copy(out=out, in_=a), R=R, W=W)


def mm(k, out, lhsT, rhs, start, stop, R, W):
    k.op("pe", lambda g: g.matmul(out, lhsT=lhsT, rhs=rhs, start=start, stop=stop), R=R, W=W)


def tr(k, out, in_, ident, R, W):
    k.op("pe", lambda g: g.transpose(out=out, in_=in_, identity=ident), R=R, W=W)


class Rot:
    def __init__(self, bufs):
        self.bufs = bufs
        self.i = 0

    def get(self):
        b = self.bufs[self.i % len(self.bufs)]
        self.i += 1
        return b


def build(stop_after=None, debug=False):
    nc = bass.Bass("TRN2", target_bir_lowering=False)
    SK = "ExternalOutput" if debug else "Internal"

    def din(name, shape, dt=F32):
        return Buf(nc.dram_tensor(name, list(shape), dt, kind="ExternalInput").ap(), name)

    def dsc(name, shape, dt):
        return Buf(nc.dram_tensor(name, list(shape), dt, kind=SK).ap(), name)

    xin = din("xin", [T, D])
    cT_d = din("cT", [128, 8, 2])
    w_mod = din("w_mod", [DEPTH, D, 6 * D])
    b_modT = din("b_modT", [DEPTH, 128, 48])
    normsT = din("normsT", [128, DEPTH, 2, 8])
    normfT = din("normfT", [128, 8])
    w_in = din("w_in", [DEPTH, D, INC])
    dn_convT = din("dn_convT", [DEPTH, 128, 12, 4])
    dnab = din("dnab", [DEPTH, 4, 4])
    dn_normT = din("dn_normT", [DEPTH, 128, 1])
    lru_cw = din("lru_cw", [DEPTH, 128, 4, 4])
    lru_cb = din("lru_cb", [DEPTH, 128, 4])
    lru_vec = din("lru_vec", [DEPTH, 128, 3, 2, 4])
    lru_wblk = din("lru_wblk", [DEPTH, 2, 2, 4, 128, 128])
    da_lam = din("da_lam", [DEPTH, 256])
    da_normT = din("da_normT", [DEPTH, 128, 1])
    w_branch = din("w_branch", [DEPTH, 3, 512, D])
    w_out = din("w_out", [DEPTH, D, D])
    w_g = din("w_ffn_gate", [DEPTH, D, DFF])
    w_u = din("w_ffn_up", [DEPTH, D, DFF])
    w_d = din("w_ffn_down", [DEPTH, DFF, D])
    c_ident = din("c_ident", [128, 128])
    c_rope = din("c_rope", [2, 128, T])
    c_perm = din("c_perm", [128, 128])
    c_mb = din("c_mb", [2, 128, 128])
    c_nod = din("c_nod", [128, 128])
    c_m01 = din("c_m01", [4, 2, 512])
    c_sel = din("c_sel", [4, 4, 128])
    c_blk = din("c_blk", [5, 128, 128])
    out_d = Buf(nc.dram_tensor("out", [NLAT, D], F32, kind="ExternalOutput").ap(), "out")

    hT = dsc("hT", [D, T], F32)
    DNQKV = dsc("DNQKV", [3, 512, T], BF16)
    ZS = dsc("ZS", [512, T], BF16)
    GCB = dsc("GCB", [4, 4, T], F32)
    LX = dsc("LX", [512, T], F32)
    LG = dsc("LG", [512, T], BF16)
    QR = dsc("QR", [512, T], BF16)
    KR = dsc("KR", [512, T], BF16)
    VT = dsc("VT", [T, 512], BF16)
    SG = dsc("SG", [3072, T], BF16)
    OT = dsc("OT", [2, 512, T], F32)
    YB = dsc("YB", [512, T], BF16)
    YC = dsc("YC", [512, T], BF16)
    hTv = hT.ap.rearrange("(k p) t -> p k t", p=128)
    WGT = dsc("WGT", [11, 128, 8, 256], BF16)
    WUT = dsc("WUT", [11, 128, 8, 256], BF16)
    WDT = dsc("WDT", [8, 128, 22, 128], BF16)

    def precast_ffn(l):
        wgv = w_g.ap[l].rearrange("(k p) c -> p k c", p=128)
        wuv = w_u.ap[l].rearrange("(k p) c -> p k c", p=128)
        wdv = w_d.ap[l].rearrange("(fc p) c -> p fc c", p=128)
        for f2 in range(11):
            k.dma("pool", WGT.ap[f2], wgv[:, :, f2 * 256:(f2 + 1) * 256], R=[w_g], W=[WGT])
            k.dma("pool", WUT.ap[f2], wuv[:, :, f2 * 256:(f2 + 1) * 256], R=[w_u], W=[WUT])
        for dc in range(8):
            k.dma("pool", WDT.ap[dc], wdv[:, :, dc * 128:(dc + 1) * 128], R=[w_d], W=[WDT])

    es0 = ExitStack()
    k = K(nc, es0)
    identF = k.sb([128, 128], F32, "identF")
    identB = k.sb([128, 128], BF16, "identB")
    onesB = k.sb([128, 128], BF16, "onesB")
    modT = k.sb([128, DEPTH, 48, 2], F32, "modT")
    AB = k.sb([128, DEPTH, 6, 8, 2], F32, "AB")
    nrmT = k.sb([128, DEPTH, 2, 8], F32, "nrmT")
    nrmF = k.sb([128, 8], F32, "nrmF")
    cst = k.sb([128, 4], F32, "cst")
    k.dma("sp", identF[:], c_ident.ap, R=[c_ident], W=[identF])
    k.dma("sp", nrmT[:], normsT.ap, R=[normsT], W=[nrmT])
    k.dma("sp", nrmF[:], normfT.ap, R=[normfT], W=[nrmF])
    cp(k, "dve", identB[:], identF[:], [identF], [identB])
    k.op("dve", lambda g: g.memset(onesB[:], 1.0), W=[onesB])
    k.op("dve", lambda g: g.memset(cst[:, 0:1], 1e-6), W=[cst])
    k.op("dve", lambda g: g.memset(cst[:, 1:2], 1e-5), W=[cst])
    k.op("dve", lambda g: g.memset(cst[:, 2:3], 1.0), W=[cst])
    k.op("dve", lambda g: g.memset(cst[:, 3:4], 0.0), W=[cst])
    EPS6, EPS5, ONE = cst[:, 0:1], cst[:, 1:2], cst[:, 2:3]

    def norm_block(l, which, hb, n, j, sq, pss, rs, tmp, ubuf, uout):
        act(k, sq[:, :, :n], hb[:, :, :n], AF.Square, [hb], [sq])
        for kk in range(8):
            mm(k, pss[:, :n], onesB[:], sq[:, kk, :n], kk == 0, kk == 7, [onesB, sq], [pss])
        act(k, rs[:, :n], pss[:, :n], AF.Sqrt, [pss, cst], [rs], scale=1.0 / D, bias=EPS6)
        k.op("dve", lambda g: g.reciprocal(out=rs[:, :n], in_=rs[:, :n]), R=[rs], W=[rs])
        for kk in range(8):
            tt(k, "dve", tmp[:, kk, :n], hb[:, kk, :n], rs[:, :n], ALU.mult, [hb, rs], [tmp])
            act(k, uout(kk), tmp[:, kk, :n], AF.Identity, [tmp, AB], [ubuf],
                scale=AB[:, l, 3 * which + 0, kk, j:j + 1], bias=AB[:, l, 3 * which + 1, kk, j:j + 1])

    with ExitStack() as es:
        k.es = es
        cTs = k.sb([128, 8, 2], F32, "cTs")
        sTs = k.sb([128, 8, 2], F32, "sTs")
        bm = k.sb([128, DEPTH, 48], F32, "bm")
        k.dma("sp", cTs[:], cT_d.ap, R=[cT_d], W=[cTs])
        for l in range(DEPTH):
            k.dma("sp", bm[:, l, :], b_modT.ap[l], R=[b_modT], W=[bm])
        act(k, sTs[:], cTs[:], AF.Silu, [cTs], [sTs])
        wm = Rot([k.sb([128, 8, 512], F32, "wm") for _ in range(2)])
        pmod = k.ps([128, 48, 2], F32, "pmod")
        for l in range(DEPTH):
            for g in range(12):
                wt = wm.get()
                k.dma("sp", wt[:], w_mod.ap[l].rearrange("(k p) c -> p k c", p=128)[:, :, g * 512:(g + 1) * 512], R=[w_mod], W=[wt])
                for c4 in range(4):
                    ch = g * 4 + c4
                    for kk in range(8):
                        mm(k, pmod[:, ch, :], wt[:, kk, c4 * 128:(c4 + 1) * 128], sTs[:, kk, :], kk == 0, kk == 7, [wt, sTs], [pmod])
            for j in range(2):
                tt(k, "dve", modT[:, l, :, j], pmod[:, :, j], bm[:, l, :], ALU.add, [pmod, bm], [modT])
            for s, (ish, isc, ig) in enumerate([(0, 1, 2), (3, 4, 5)]):
                for j in range(2):
                    stt(k, AB[:, l, 3 * s + 0, :, j], modT[:, l, isc * 8:(isc + 1) * 8, j], 1.0, nrmT[:, l, s, :], ALU.add, ALU.mult, [modT, nrmT], [AB])
                    cp(k, "dve", AB[:, l, 3 * s + 1, :, j], modT[:, l, ish * 8:(ish + 1) * 8, j], [modT], [AB])
                    cp(k, "dve", AB[:, l, 3 * s + 2, :, j], modT[:, l, ig * 8:(ig + 1) * 8, j], [modT], [AB])
        k.barrier()
    k.es = es0

    with ExitStack() as es:
        k.es = es
        xt = Rot([k.sb([128, D], F32, "xt") for _ in range(2)])
        ht = Rot([k.sb([128, 8, 128], F32, "ht") for _ in range(2)])
        pp = Rot([k.ps([128, 4, 128], F32, "ppT") for _ in range(4)])
        for t in range(T // 128):
            x_ = xt.get()
            k.dma("sp", x_[:], xin.ap[t * 128:(t + 1) * 128, :], R=[xin], W=[x_])
            h_ = ht.get()
            for half in range(2):
                p_ = pp.get()
                for q in range(4):
                    kk = half * 4 + q
                    tr(k, p_[:, q, :], x_[:, kk * 128:(kk + 1) * 128], identF[:], [x_, identF], [p_])
                cp(k, k.alt(), h_[:, half * 4:half * 4 + 4, :], p_[:], [p_], [h_])
            k.dma("sp", hTv[:, :, t * 128:(t + 1) * 128], h_[:], R=[h_], W=[hT])
        k.barrier()
    k.es = es0

    def off(t):
        return t + 1 if t < NCTX else t + 4

    def phase_A(l):
        with ExitStack() as esA:
            k.es = esA
            UT = k.sb([128, 8, T], BF16, "UT")
            with ExitStack() as es1:
                k.es = es1
                hbR = Rot([k.sb([128, 8, 512], F32, "hbA") for _ in range(2)])
                sq = k.sb([128, 8, 512], BF16, "sqA")
                tmp = k.sb([128, 8, 512], F32, "tmpA")
                rs = k.sb([128, 512], F32, "rsA")
                pss = k.ps([128, 512], F32, "pssA")
                for (t0, n) in BLOCKS:
                    j = 0 if t0 < NCTX else 1
                    h_ = hbR.get()
                    k.dma("sp", h_[:, :, :n], hTv[:, :, t0:t0 + n], R=[hT], W=[h_])
                    norm_block(l, 0, h_, n, j, sq, pss, rs, tmp, UT, lambda kk: UT[:, kk, t0:t0 + n])
                k.barrier()
            k.es = esA
            permB = k.sb([128, 128], BF16, "permB")
            cw = k.sb([128, 12, 4], F32, "cw")
            lcw = k.sb([128, 4, 4], F32, "lcw")
            lcb = k.sb([128, 4], F32, "lcb")
            dnab_s = k.sb([4, 4], F32, "dnab_s")
            nA = k.sb([4, 2], F32, "nA")
            m01 = k.sb([4, 2, 512], F32, "m01")
            k.dma("pool", permB[:], c_perm.ap, R=[c_perm], W=[permB])
            k.dma("sp", cw[:], dn_convT.ap[l], R=[dn_convT], W=[cw])
            k.dma("sp", lcw[:], lru_cw.ap[l], R=[lru_cw], W=[lcw])
            k.dma("sp", lcb[:], lru_cb.ap[l], R=[lru_cb], W=[lcb])
            k.dma("sp", dnab_s[:], dnab.ap[l], R=[dnab], W=[dnab_s])
            k.dma("sp", m01[:], c_m01.ap, R=[c_m01], W=[m01])
            act(k, nA[:], dnab_s[:, 0:2], AF.Exp, [dnab_s], [nA])
            ts(k, "dve", nA[:], nA[:], -1.0, None, ALU.mult, None, [nA], [nA])
            WG = Rot([k.sb([128, 8, 512], BF16, "WG") for _ in range(2)])
            OB = Rot([k.sb([128, T], BF16, "OB") for _ in range(2)])
            pg = Rot([k.ps([128, 512], F32, "pg") for _ in range(4)])
            p2 = Rot([k.ps([128, 512], F32, "p2") for _ in range(2)])

            def rot(shape, dt, name, n=2):
                return Rot([k.sb(shape, dt, name) for _ in range(n)])
            w_inv = w_in.ap[l].rearrange("(k p) c -> p k c", p=128)
            groups_conv = [("dnq", 0, 512), ("dnk", 512, 512), ("dnv", 1024, 512), ("lrux", 2064, 512)]
            groups_rest = [("z", 1536, 512), ("ab", 2048, 16), ("lrug", 2576, 512), ("daq", 3088, 512),
                           ("dak", 3600, 512), ("dav", 4112, 512)] + [("gate%d" % g, 4624 + 512 * g, 512) for g in range(6)]
            def process(groups):
                for (kind, c0, ncol) in groups:
                    wt = WG.get()
                    k.dma("pool", wt[:, :, :ncol], w_inv[:, :, c0:c0 + ncol], R=[w_in], W=[wt])
                    if kind == "dav":
                        for ti in range(T // 128):
                            p_ = pg.get()
                            for kk in range(8):
                                mm(k, p_[:, :], UT[:, kk, ti * 128:(ti + 1) * 128], wt[:, kk, :], kk == 0, kk == 7, [UT, wt], [p_])
                            v_ = vbR.get()
                            cp(k, k.alt(), v_[:], p_[:], [p_], [v_])
                            k.dma("sp", VT.ap[ti * 128:(ti + 1) * 128, :], v_[:], R=[v_], W=[VT])
                        continue
                    if kind == "ab":
                        for (t0, n) in BLOCKS:
                            abt = abR.get()
                            for d in range(2):
                                p_ = pg.get()
                                for kk in range(8):
                                    mm(k, p_[0:4, :n], wt[:, kk, 4 * d:4 * d + 4], UT[:, kk, t0:t0 + n], kk == 0, kk == 7, [UT, wt], [p_])
                                e_ = eR.get()
                                act(k, e_[:, :n], p_[0:4, :n], AF.Exp, [p_, dnab_s], [e_], bias=dnab_s[:, 2 + d:3 + d])
                                act(k, e_[:, :n], e_[:, :n], AF.Ln, [e_, cst], [e_], bias=cst[0:4, 2:3])
                                ts(k, "dve", e_[:, :n], e_[:, :n], nA[:, d:d + 1], None, ALU.mult, None, [e_, nA], [e_])
                                if d == 0:
                                    k.op("dve", lambda g: g.tensor_tensor_scan(out=abt[:, 0, :n], data0=m01[:, 0, :n], data1=e_[:, :n], initial=0.0, op0=ALU.mult, op1=ALU.add), R=[m01, e_], W=[abt])
                                else:
                                    k.op("dve", lambda g: g.tensor_tensor_scan(out=abt[:, 1, :n][:, ::-1], data0=m01[:, 1, :n][:, ::-1], data1=e_[:, :n][:, ::-1], initial=0.0, op0=ALU.mult, op1=ALU.add), R=[m01, e_], W=[abt])
                                p_ = pg.get()
                                for kk in range(8):
                                    mm(k, p_[0:4, :n], wt[:, kk, 8 + 4 * d:12 + 4 * d], UT[:, kk, t0:t0 + n], kk == 0, kk == 7, [UT, wt], [p_])
                                act(k, abt[:, 2 + d, :n], p_[0:4, :n], AF.Sigmoid, [p_], [abt])
                            k.dma("sp", GCB.ap[:, :, t0:t0 + n], abt[:, :, :n], R=[abt], W=[GCB])
                        continue
                    for c4 in range(4):
                        conv = kind in ("dnq", "dnk", "dnv", "lrux")
                        xp = XP.get() if conv else None
                        ob = OB.get() if kind != "lrux" else None
                        for (t0, n) in BLOCKS:
                            p_ = pg.get()
                            for kk in range(8):
                                mm(k, p_[:, :n], wt[:, kk, c4 * 128:(c4 + 1) * 128], UT[:, kk, t0:t0 + n], kk == 0, kk == 7, [UT, wt], [p_])
                            if conv:
                                cp(k, k.alt(), xp[:, off(t0):off(t0) + n], p_[:, :n], [p_], [xp])
                            elif kind == "z":
                                act(k, ob[:, t0:t0 + n], p_[:, :n], AF.Silu, [p_], [ob])
                            elif kind == "lrug":
                                act(k, ob[:, t0:t0 + n], p_[:, :n], AF.Gelu, [p_], [ob])
                            elif kind.startswith("gate"):
                                act(k, ob[:, t0:t0 + n], p_[:, :n], AF.Sigmoid, [p_], [ob])
                            else:
                                qr_ = qrawR.get()
                                cp(k, "act", qr_[:, :n], p_[:, :n], [p_], [qr_])
                                q2 = p2.get()
                                mm(k, q2[:, :n], permB[:], qr_[:, :n], True, True, [permB, qr_], [q2])
                                a1 = t1R.get()
                                tt(k, "pool", a1[:, :n], qr_[:, :n], cosT[:, t0:t0 + n], ALU.mult, [qr_, cosT], [a1])
                                a2 = t2R.get()
                                tt(k, "dve", a2[:, :n], q2[:, :n], sinT[:, t0:t0 + n], ALU.mult, [q2, sinT], [a2])
                                tt(k, "pool", ob[:, t0:t0 + n], a1[:, :n], a2[:, :n], ALU.add, [a1, a2], [ob])
                        if conv:
                            if kind == "lrux":
                                wv = lambda jj, c4=c4: lcw[:, c4, jj:jj + 1]
                            else:
                                ci = {"dnq": 0, "dnk": 4, "dnv": 8}[kind] + c4
                                wv = lambda jj, ci=ci: cw[:, ci, jj:jj + 1]
                            wbuf = lcw if kind == "lrux" else cw
                            Dg = DgR.get()
                            for jj in range(4):
                                ts(k, "dve", Dg[:, jj, :], identB[:], wv(jj), None, ALU.mult, None, [identB, wbuf], [Dg])

                            def post(kind=kind, c4=c4, xp=xp, ob=ob, Dg=Dg):
                                ar = accrow.get() if kind != "dnv" else None
                                for (t0, n) in BLOCKS:
                                    c = off(t0)
                                    pc = pg.get()
                                    for idx, (jj, sh) in enumerate(((0, -1), (1, 0), (2, 1), (3, 2))):
                                        mm(k, pc[:, :n], Dg[:, jj, :], xp[:, c + sh:c + sh + n], idx == 0, idx == 3, [Dg, xp], [pc])
                                    if kind == "lrux":
                                        act(k, ar[:, t0:t0 + n], pc[:, :n], AF.Identity, [pc, lcb], [ar], bias=lcb[:, c4:c4 + 1])
                                    elif kind == "dnv":
                                        act(k, ob[:, t0:t0 + n], pc[:, :n], AF.Silu, [pc], [ob])
                                    else:
                                        act(k, ar[:, t0:t0 + n], pc[:, :n], AF.Silu, [pc], [ar])
                                if kind == "lrux":
                                    k.dma("sp", LX.ap[c4 * 128:(c4 + 1) * 128, :], ar[:], R=[ar], W=[LX])
                                    return
                                if kind != "dnv":
                                    for (t0, n) in BLOCKS:
                                        tt(k, "pool", sqrow[:, t0:t0 + n], ar[:, t0:t0 + n], ar[:, t0:t0 + n], ALU.mult, [ar], [sqrow])
                                    for (t0, n) in BLOCKS:
                                        q2 = p2.get()
                                        mm(k, q2[:, :n], onesB[:], sqrow[:, t0:t0 + n], True, True, [onesB, sqrow], [q2])
                                        act(k, lnrow[:, t0:t0 + n], q2[:, :n], AF.Ln, [q2, cst], [lnrow], bias=EPS6)
                                    for (t0, n) in BLOCKS:
                                        act(k, lnrow[:, t0:t0 + n], lnrow[:, t0:t0 + n], AF.Exp, [lnrow], [lnrow], scale=-0.5)
                                    for (t0, n) in BLOCKS:
                                        if kind == "dnq":
                                            stt(k, ob[:, t0:t0 + n], ar[:, t0:t0 + n], 128.0 ** -0.5, lnrow[:, t0:t0 + n], ALU.mult, ALU.mult, [ar, lnrow], [ob])
                                        else:
                                            tt(k, "dve", ob[:, t0:t0 + n], ar[:, t0:t0 + n], lnrow[:, t0:t0 + n], ALU.mult, [ar, lnrow], [ob])
                                dst = DNQKV.ap[{"dnq": 0, "dnk": 1, "dnv": 2}[kind], c4 * 128:(c4 + 1) * 128, :]
                                k.dma("sp", dst, ob[:], R=[ob], W=[DNQKV])
                            while pending:
                                pending.pop(0)()
                            pending.append(post)
                            continue
                        if ob is not None:
                            if kind in ("dnq", "dnk", "dnv"):
                                dst = DNQKV.ap[{"dnq": 0, "dnk": 1, "dnv": 2}[kind], c4 * 128:(c4 + 1) * 128, :]
                                dbuf = DNQKV
                            elif kind == "z":
                                dst, dbuf = ZS.ap[c4 * 128:(c4 + 1) * 128, :], ZS
                            elif kind == "lrug":
                                dst, dbuf = LG.ap[c4 * 128:(c4 + 1) * 128, :], LG
                            elif kind == "daq":
                                dst, dbuf = QR.ap[c4 * 128:(c4 + 1) * 128, :], QR
                            elif kind == "dak":
                                dst, dbuf = KR.ap[c4 * 128:(c4 + 1) * 128, :], KR
                            else:
                                g = int(kind[4:])
                                r0 = (g * 4 + c4) * 128
                                dst, dbuf = SG.ap[r0:r0 + 128, :], SG
                            k.dma("sp", dst, ob[:], R=[ob], W=[dbuf])
            with ExitStack() as e2:
                k.es = e2
                XPs = [k.sb([128, T + 6], BF16, "XP") for _ in range(2)]
                for x_ in XPs:
                    k.op("pool", lambda g: g.memset(x_[:], 0.0), W=[x_])
                XP = Rot(XPs)
                accrow = rot([128, T], F32, "accrow", 2)
                DgR = rot([128, 4, 128], BF16, "Dg", 2)
                pending = []
                sqrow = k.sb([128, T], BF16, "sqrow")
                lnrow = k.sb([128, T], F32, "lnrow")
                process(groups_conv)
                while pending:
                    pending.pop(0)()
                k.barrier()
            with ExitStack() as e3:
                k.es = e3
                cosT = k.sb([128, T], BF16, "cosT")
                sinT = k.sb([128, T], BF16, "sinT")
                k.dma("pool", cosT[:], c_rope.ap[0], R=[c_rope], W=[cosT])
                k.dma("pool", sinT[:], c_rope.ap[1], R=[c_rope], W=[sinT])
                qrawR = rot([128, 512], BF16, "qraw")
                t1R = rot([128, 512], F32, "t1")
                t2R = rot([128, 512], F32, "t2")
                vbR = rot([128, 512], BF16, "vb")
                abR = rot([4, 4, 512], F32, "abt", 2)
                eR = rot([4, 512], F32, "eab")
                process(groups_rest)
                k.barrier()
            k.es = esA
            k.barrier()
        k.es = es0

    def phase_C(l):
        precast_ffn(l)
        with ExitStack() as es:
            k.es = es
            lv = k.sb([128, 3, 2, 4], F32, "lv")
            cneg = k.sb([128, 2, 4], F32, "cneg")
            k.dma("sp", lv[:], lru_vec.ap[l], R=[lru_vec], W=[lv])
            act(k, cneg[:], lv[:, 2], AF.Exp, [lv], [cneg], scale=-1.0)
            act(k, cneg[:], cneg[:], AF.Ln, [cneg, cst], [cneg], bias=ONE)
            ts(k, "dve", cneg[:], cneg[:], -8.0, None, ALU.mult, None, [cneg], [cneg])
            xc = k.sb([128, T], F32, "xc")
            xcb = k.sb([128, T], BF16, "xcb")
            gg = k.sb([128, T], BF16, "gg")
            A_ = k.sb([128, T], F32, "lruA")
            IG = k.sb([128, T], F32, "lruIG")
            W_ = k.sb([128, T], F32, "lruW")
            H = [k.sb([128, T], F32, "lruH%d" % d) for d in range(2)]
            yb = k.sb([128, T], BF16, "lruY")
            wb = k.sb([128, 2, 2, 128], BF16, "lruwb")
            pg = Rot([k.ps([128, 512], F32, "pgC") for _ in range(4)])
            for cc in range(4):
                k.dma("sp", xc[:], LX.ap[cc * 128:(cc + 1) * 128, :], R=[LX], W=[xc])
                k.dma("sp", gg[:], LG.ap[cc * 128:(cc + 1) * 128, :], R=[LG], W=[gg])
                k.dma("pool", wb[:], lru_wblk.ap[l, :, :, cc].rearrange("a d i j -> i a d j"), R=[lru_wblk], W=[wb])
                cp(k, "act", xcb[:], xc[:], [xc], [xcb])
                for d in range(2):
                    for (t0, n) in BLOCKS:
                        p_ = pg.get()
                        mm(k, p_[:, :n], wb[:, 0, d, :], xcb[:, t0:t0 + n], True, True, [wb, xcb], [p_])
                        act(k, A_[:, t0:t0 + n], p_[:, :n], AF.Sigmoid, [p_, lv], [A_], bias=lv[:, 0, d, cc:cc + 1])
                        p_ = pg.get()
                        mm(k, p_[:, :n], wb[:, 1, d, :], xcb[:, t0:t0 + n], True, True, [wb, xcb], [p_])
                        act(k, IG[:, t0:t0 + n], p_[:, :n], AF.Sigmoid, [p_, lv], [IG], bias=lv[:, 1, d, cc:cc + 1])
                    act(k, A_[:], A_[:], AF.Exp, [A_, cneg], [A_], scale=cneg[:, d, cc:cc + 1])
                    act(k, W_[:], A_[:], AF.Square, [A_], [W_])
                    ts(k, "dve", W_[:], W_[:], -1.0, 1.0, ALU.mult, ALU.add, [W_], [W_])
                    act(k, W_[:], W_[:], AF.Sqrt, [W_], [W_])
                    tt(k, "pool", IG[:], IG[:], xc[:], ALU.mult, [IG, xc], [IG])
                    tt(k, "dve", IG[:], IG[:], W_[:], ALU.mult, [IG, W_], [IG])
                    Hd = H[d]
                    if d == 0:
                        k.op("dve", lambda g: g.tensor_tensor_scan(out=Hd[:], data0=A_[:], data1=IG[:], initial=0.0, op0=ALU.mult, op1=ALU.add), R=[A_, IG], W=[Hd])
                    else:
                        k.op("dve", lambda g: g.tensor_tensor_scan(out=Hd[:, 0:NCTX][:, ::-1], data0=A_[:, 0:NCTX][:, ::-1], data1=IG[:, 0:NCTX][:, ::-1], initial=0.0, op0=ALU.mult, op1=ALU.add), R=[A_, IG], W=[Hd])
                        k.op("dve", lambda g: g.tensor_tensor_scan(out=Hd[:, NCTX:T][:, ::-1], data0=A_[:, NCTX:T][:, ::-1], data1=IG[:, NCTX:T][:, ::-1], initial=Hd[:, 0:1], op0=ALU.mult, op1=ALU.add), R=[A_, IG, Hd], W=[Hd])
                tt(k, "pool", H[0][:], H[0][:], H[1][:], ALU.add, [H[0], H[1]], [H[0]])
                tt(k, "dve", yb[:], H[0][:], gg[:], ALU.mult, [H[0], gg], [yb])
                k.dma("sp", YB.ap[cc * 128:(cc + 1) * 128, :], yb[:], R=[yb], W=[YB])
            k.barrier()
        k.es = es0

    def phase_B(l):
        with ExitStack() as es:
            k.es = es
            mb = k.sb([128, 2, 128], F32, "mb")
            nod = k.sb([128, 128], F32, "nod")
            sel = k.sb([4, 4, 128], F32, "sel")
            blk = k.sb([128, 5, 128], F32, "blk")
            k.dma("sp", mb[:], c_mb.ap.rearrange("d j i -> j d i"), R=[c_mb], W=[mb])
            k.dma("sp", nod[:], c_nod.ap, R=[c_nod], W=[nod])
            k.dma("sp", sel[:], c_sel.ap, R=[c_sel], W=[sel])
            k.dma("sp", blk[:], c_blk.ap.rearrange("m j i -> j m i"), R=[c_blk], W=[blk])
            qv = DNQKV.ap.rearrange("w (h f) t -> w f h t", h=4)
            otv = OT.ap.rearrange("d (h f) t -> d f h t", h=4)
            gcv = GCB.ap
            order = {0: list(range(len(BLOCKS))), 1: [0] + list(range(len(BLOCKS) - 1, 0, -1))}
            B4 = [128, 4, 128]
            bc_h = lambda ap2: ap2.unsqueeze(1).broadcast_to(B4)
            bc_i = lambda ap2: ap2.unsqueeze(2).broadcast_to(B4)

            def dn_dir(d):
                def rot(shape, dt, name, n=1):
                    return Rot([k.sb(shape, dt, name + "%d" % d) for _ in range(n)])
                QTr = rot([128, 4, 512], BF16, "QT")
                KTr = rot([128, 4, 512], BF16, "KT")
                VTr = rot([128, 4, 512], BF16, "VTt")
                GBr = rot([4, 2, 512], F32, "GB")
                OBr = rot([128, 4, 512], F32, "OTb")
                S_f = k.sb(B4, F32, "S32_%d" % d)
                S_b = k.sb(B4, BF16, "Sb_%d" % d)
                k.op("pool", lambda g: g.memset(S_f[:], 0.0), W=[S_f])
                k.op("pool", lambda g: g.memset(S_b[:], 0.0), W=[S_b])
                pf = Rot([k.ps(B4, F32, "pfB%d" % d) for _ in range(3)])
                pb = Rot([k.ps(B4, BF16, "pbB%d" % d) for _ in range(1)])
                gbc = rot([128, 8], F32, "gbc", 2)
                sc4 = rot([128, 4, 4], F32, "sc4", 2)
                D1 = rot(B4, F32, "D1")
                E = rot(B4, F32, "E", 2)
                nE = rot(B4, F32, "nE")
                EG = rot(B4, F32, "EG")
                Nr = rot(B4, F32, "N")
                L0r = rot(B4, F32, "L0")
                Ndr = rot(B4, F32, "Nd")
                Ldr = rot(B4, F32, "Ld")
                N2r = rot(B4, F32, "N2")
                L2r = rot(B4, F32, "L2")
                L4r = rot(B4, F32, "L4")
                Xr = rot(B4, F32, "X", 2)
                Lbr = rot(B4, F32, "Lb")
                XTr = rot(B4, F32, "XT")
                Yr = rot(B4, F32, "Y")
                Rr = rot(B4, BF16, "R", 2)
                QK = rot(B4, BF16, "QK", 2)
                kbg = rot(B4, BF16, "kbg", 2)
                kout = rot(B4, BF16, "kout", 2)
                vbt = rot(B4, BF16, "vbt", 2)
                wTr = rot(B4, BF16, "wT", 2)
                ur = rot(B4, F32, "u", 2)
                qin = rot(B4, BF16, "qin", 2)
                vnew = rot(B4, BF16, "vnew", 2)
                stmp = rot(B4, F32, "stmp")
                bm_ = lambda i_: bc_h(blk[:, i_, :])

                def mm4(p, lf, rf, R):
                    for h in range(4):
                        mm(k, p[:, h, :], lf(h), rf(h), True, True, R, [p])

                def tr4(p, src, ident, R):
                    for h in range(4):
                        tr(k, p[:, h, :], src(h), ident, R, [p])

                for step in range(len(BLOCKS)):
                    t0, n = BLOCKS[order[d][step]]
                    QT, KT, VTt, GB, OTb = QTr.get(), KTr.get(), VTr.get(), GBr.get(), OBr.get()
                    k.dma("sp", QT[:, :, :n], qv[0, :, :, t0:t0 + n], R=[DNQKV], W=[QT])
                    k.dma("sp", KT[:, :, :n], qv[1, :, :, t0:t0 + n], R=[DNQKV], W=[KT])
                    k.dma("sp", VTt[:, :, :n], qv[2, :, :, t0:t0 + n], R=[DNQKV], W=[VTt])
                    k.dma("sp", GB[:, 0, :n], gcv[:, d, t0:t0 + n], R=[GCB], W=[GB])
                    k.dma("sp", GB[:, 1, :n], gcv[:, 2 + d, t0:t0 + n], R=[GCB], W=[GB])
                    yield
                    nch = n // 128
                    chs = list(range(nch)) if d == 0 else list(range(nch - 1, -1, -1))
                    lastc = 127 if d == 0 else 0
                    for c in chs:
                        o = c * 128
                        sl = slice(o, o + 128)
                        p0 = pf.get()
                        tr(k, p0[:, 0, 0:4], GB[:, 0, sl], identF[0:4, 0:4], [GB, identF], [p0])
                        tr(k, p0[:, 0, 4:8], GB[:, 1, sl], identF[0:4, 0:4], [GB, identF], [p0])
                        g8 = gbc.get()
                        cp(k, "dve", g8[:], p0[:, 0, 0:8], [p0], [g8])
                        yield
                        GR = pf.get()
                        mm4(GR, lambda h: sel[:, h, :], lambda h: GB[:, 0, sl], [sel, GB])
                        d1 = D1.get()
                        tt(k, "dve", d1[:], GR[:], bc_i(g8[:, 0:4]), ALU.subtract, [GR, g8], [d1])
                        s4 = sc4.get()
                        act(k, s4[:, 0, :], g8[:, 0:4], AF.Exp, [g8], [s4])
                        yield
                        tt(k, "dve", s4[:, 1, :], s4[:, 0, :], g8[:, 4:8], ALU.mult, [s4, g8], [s4])
                        tt(k, "dve", s4[:, 2, :], GR[:, :, lastc], g8[:, 0:4], ALU.subtract, [GR, g8], [s4])
                        act(k, s4[:, 2, :], s4[:, 2, :], AF.Exp, [s4], [s4])
                        act(k, s4[:, 3, :], GR[:, :, lastc], AF.Exp, [GR], [s4])
                        eg_ = EG.get()
                        act(k, eg_[:], GR[:], AF.Exp, [GR], [eg_])
                        yield
                        tt(k, "pool", d1[:], d1[:], bc_h(mb[:, d, :]), ALU.add, [d1, mb], [d1])
                        e_ = E.get()
                        act(k, e_[:], d1[:], AF.Exp, [d1], [e_])
                        yield
                        ne_ = nE.get()
                        tt(k, "pool", ne_[:], e_[:], bc_h(nod[:]), ALU.mult, [e_, nod], [ne_])
                        BR = pf.get()
                        mm4(BR, lambda h: sel[:, h, :], lambda h: GB[:, 1, sl], [sel, GB])
                        tt(k, "dve", ne_[:], BR[:], ne_[:], ALU.mult, [BR, ne_], [ne_])
                        yield
                        KK = pf.get()
                        mm4(KK, lambda h: KT[:, h, sl], lambda h: KT[:, h, sl], [KT])
                        N_ = Nr.get()
                        tt(k, "dve", N_[:], KK[:], ne_[:], ALU.mult, [KK, ne_], [N_])
                        yield
                        KQ = pf.get()
                        mm4(KQ, lambda h: KT[:, h, sl], lambda h: QT[:, h, sl], [KT, QT])
                        qk = QK.get()
                        tt(k, "dve", qk[:], KQ[:], e_[:], ALU.mult, [KQ, e_], [qk])
                        yield
                        pL = pf.get()
                        tr4(pL, lambda h: N_[:, h, :], identF[:], [N_, identF])
                        L0 = L0r.get()
                        cp(k, "act", L0[:], pL[:], [pL], [L0])
                        Nd = Ndr.get()
                        tt(k, "pool", Nd[:], N_[:], bm_(0), ALU.mult, [N_, blk], [Nd])
                        yield
                        Ld = Ldr.get()
                        tt(k, "pool", Ld[:], L0[:], bm_(0), ALU.mult, [L0, blk], [Ld])
                        yield
                        pA = pf.get()
                        mm4(pA, lambda h: Ld[:, h, :], lambda h: Nd[:, h, :], [Ld, Nd])
                        N2 = N2r.get()
                        cp(k, "act", N2[:], pA[:], [pA], [N2])
                        pA = pf.get()
                        mm4(pA, lambda h: Nd[:, h, :], lambda h: Ld[:, h, :], [Ld, Nd])
                        L2 = L2r.get()
                        cp(k, "dve", L2[:], pA[:], [pA], [L2])
                        X = Xr.get()
                        tt(k, "pool", X[:], Nd[:], bc_h(identF[:]), ALU.add, [Nd, identF], [X])
                        yield
                        pA = pf.get()
                        mm4(pA, lambda h: N2[:, h, :], lambda h: L2[:, h, :], [N2, L2])
                        L4 = L4r.get()
                        cp(k, "act", L4[:], pA[:], [pA], [L4])
                        yield
                        for Lk in (L2, L4):
                            pA = pf.get()
                            mm4(pA, lambda h: Lk[:, h, :], lambda h: X[:, h, :], [Lk, X])
                            X2 = Xr.get()
                            tt(k, "dve", X2[:], pA[:], X[:], ALU.add, [pA, X], [X2])
                            X = X2
                            yield
                        for lev in range(1, 5):
                            Lb = Lbr.get()
                            tt(k, "pool", Lb[:], L0[:], bm_(lev), ALU.mult, [L0, blk], [Lb])
                            pA = pf.get()
                            tr4(pA, lambda h: X[:, h, :], identF[:], [X, identF])
                            XT = XTr.get()
                            cp(k, "act", XT[:], pA[:], [pA], [XT])
                            yield
                            pA = pf.get()
                            mm4(pA, lambda h: Lb[:, h, :], lambda h: X[:, h, :], [Lb, X])
                            Y = Yr.get()
                            cp(k, "dve", Y[:], pA[:], [pA], [Y])
                            yield
                            pA = pf.get()
                            mm4(pA, lambda h: XT[:, h, :], lambda h: Y[:, h, :], [XT, Y])
                            if lev < 4:
                                X2 = Xr.get()
                                tt(k, "dve", X2[:], pA[:], X[:], ALU.add, [pA, X], [X2])
                                X = X2
                            else:
                                R_ = Rr.get()
                                tt(k, "dve", R_[:], pA[:], X[:], ALU.add, [pA, X], [R_])
                            yield
                        pK = pb.get()
                        tr4(pK, lambda h: KT[:, h, sl], identB[:], [KT, identB])
                        kb_ = kbg.get()
                        tt(k, "dve", kb_[:], pK[:], bc_i(s4[:, 1, :]), ALU.mult, [pK, s4], [kb_])
                        ko_ = kout.get()
                        tt(k, "dve", ko_[:], pK[:], bc_i(s4[:, 2, :]), ALU.mult, [pK, s4], [ko_])
                        yield
                        pV = pb.get()
                        tr4(pV, lambda h: VTt[:, h, sl], identB[:], [VTt, identB])
                        vb_ = vbt.get()
                        tt(k, "dve", vb_[:], pV[:], bc_i(g8[:, 4:8]), ALU.mult, [pV, g8], [vb_])
                        yield
                        pW = pf.get()
                        mm4(pW, lambda h: kb_[:, h, :], lambda h: R_[:, h, :], [kb_, R_])
                        wT = wTr.get()
                        cp(k, "act", wT[:], pW[:], [pW], [wT])
                        yield
                        pUu = pf.get()
                        mm4(pUu, lambda h: R_[:, h, :], lambda h: vb_[:, h, :], [vb_, R_])
                        u_ = ur.get()
                        cp(k, "dve", u_[:], pUu[:], [pUu], [u_])
                        qi = qin.get()
                        tt(k, "pool", qi[:], QT[:, :, sl], eg_[:], ALU.mult, [QT, eg_], [qi])
                        yield
                        pWS = pf.get()
                        mm4(pWS, lambda h: wT[:, h, :], lambda h: S_b[:, h, :], [wT, S_b])
                        vn = vnew.get()
                        tt(k, "dve", vn[:], u_[:], pWS[:], ALU.subtract, [u_, pWS], [vn])
                        yield
                        pO = pf.get()
                        for h in range(4):
                            mm(k, pO[:, h, :], S_b[:, h, :], qi[:, h, :], True, False, [S_b, qi], [pO])
                            mm(k, pO[:, h, :], vn[:, h, :], qk[:, h, :], False, True, [vn, qk], [pO])
                        cp(k, "act", OTb[:, :, sl], pO[:], [pO], [OTb])
                        yield
                        pS = pf.get()
                        mm4(pS, lambda h: ko_[:, h, :], lambda h: vn[:, h, :], [ko_, vn])
                        st_ = stmp.get()
                        tt(k, "pool", st_[:], S_f[:], bc_i(s4[:, 3, :]), ALU.mult, [S_f, s4], [st_])
                        tt(k, "dve", S_f[:], st_[:], pS[:], ALU.add, [st_, pS], [S_f])
                        cp(k, "act", S_b[:], S_f[:], [S_f], [S_b])
                        yield
                    k.dma("sp", otv[d, :, :, t0:t0 + n], OTb[:, :, :n], R=[OTb], W=[OT])
                    yield

            gens = [dn_dir(0), dn_dir(1)]
            alive = [True, True]
            while any(alive):
                for gi in range(2):
                    if alive[gi]:
                        try:
                            next(gens[gi])
                        except StopIteration:
                            alive[gi] = False
            k.barrier()
        k.es = es0

    def phase_D(l):
        lam_init = 0.8 - 0.6 * math.exp(-0.3 * l)
        with ExitStack() as es:
            k.es = es
            krv = KR.ap.rearrange("(h f) t -> f h t", h=4)
            KS = [k.sb([128, 4, T], BF16, "KS%d" % s_) for s_ in range(2)]
            QRa = k.sb([128, 4, T], BF16, "QRa")
            Va = k.sb([128, T // 128, 512], BF16, "Va")
            k.op("pool", lambda g: g.memset(KS[0][64:128], 0.0), W=[KS[0]])
            k.op("pool", lambda g: g.memset(KS[1][0:64], 0.0), W=[KS[1]])
            k.dma("sp", KS[0][0:64], krv[0:64], R=[KR], W=[KS[0]])
            k.dma("sp", KS[1][64:128], krv[64:128], R=[KR], W=[KS[1]])
            k.dma("sp", QRa[:], QR.ap.rearrange("(h f) t -> f h t", h=4), R=[QR], W=[QRa])
            vtv = VT.ap.rearrange("(kt p) v -> p kt v", p=128)
            for g in range(0, T // 128, 6):
                g1 = min(g + 6, T // 128)
                k.dma("sp", Va[:, g:g1, :], vtv[:, g:g1, :], R=[VT], W=[Va])
            dl = k.sb([128, 2, 2, 64], F32, "dl")
            k.dma("sp", dl[:].rearrange("p a b f -> p (a b f)"), da_lam.ap[l:l + 1, :].partition_broadcast(128) if False else da_lam.ap[l].partition_broadcast(128), R=[da_lam], W=[dl])
            pr = k.sb([128, 2, 64], F32, "pr")
            sv = k.sb([128, 4], F32, "sv")
            dnw = k.sb([128, 1], F32, "dnw")
            k.dma("sp", dnw[:], da_normT.ap[l], R=[da_normT], W=[dnw])
            tt(k, "dve", pr[:], dl[:, :, 0, :], dl[:, :, 1, :], ALU.mult, [dl], [pr])
            k.op("dve", lambda g: g.tensor_reduce(out=sv[:, 0:2], in_=pr[:], axis=mybir.AxisListType.X, op=ALU.add), R=[pr], W=[sv])
            act(k, sv[:, 0:2], sv[:, 0:2], AF.Exp, [sv], [sv])
            tt(k, "dve", sv[:, 2:3], sv[:, 1:2], sv[:, 0:1], ALU.subtract, [sv], [sv])
            ts(k, "dve", sv[:, 2:3], sv[:, 2:3], -lam_init, None, ALU.add, None, [sv], [sv])
            ts(k, "dve", dnw[:], dnw[:], 1.0 - lam_init, None, ALU.mult, None, [dnw], [dnw])
            pst = Rot([k.ps([128, 512], F32, "pst") for _ in range(3)])
            poS = [k.ps([128, 512], F32, "po%d" % s_) for s_ in range(2)]
            plS = [k.ps([128, 512], F32, "pl%d" % s_) for s_ in range(2)]
            pn = k.ps([128, 512], F32, "pn")
            ptR = Rot([k.sb([128, 512], BF16, "pt") for _ in range(12)])
            sA = Rot([k.sb([128, 512], F32, "sA") for _ in range(2)])
            sB = Rot([k.sb([128, 512], F32, "sB") for _ in range(2)])
            sC = Rot([k.sb([128, 512], BF16, "sC") for _ in range(4)])
            oS = [Rot([k.sb([128, 512], F32, "oS%d" % s_) for _ in range(2)]) for s_ in range(2)]
            lS = [Rot([k.sb([128, 512], F32, "lS%d" % s_) for _ in range(2)]) for s_ in range(2)]
            accR = Rot([k.sb([128, 512], F32, "accD") for _ in range(2)])
            sqdR = Rot([k.sb([128, 512], BF16, "sqD") for _ in range(2)])
            rnR = Rot([k.sb([128, 512], F32, "rnD") for _ in range(2)])
            deferred = []
            ycR = Rot([k.sb([128, T], BF16, "ycrow") for _ in range(1)])
            LOOK = 2
            for h in range(4):
                yc = ycR.get()
                for (t0, n) in BLOCKS:
                    kts = [0, 1] if t0 < NCTX else list(range(T // 128))
                    items = [(s_, i_, kt) for i_, kt in enumerate(kts) for s_ in range(2)]
                    pend = []

                    grp = [[], []]
                    ngrp = [0, 0]
                    lq = []
                    ngroups = (len(kts) + 3) // 4

                    def close_group(s_):
                        g_ = grp[s_]
                        grp[s_] = []
                        gi_ = ngrp[s_]
                        ngrp[s_] += 1
                        if len(g_) == 1:
                            src = g_[0]
                        else:
                            a1 = sA.get()
                            tt(k, "dve", a1[:, :n], g_[0][:, :n], g_[1][:, :n], ALU.add, [g_[0], g_[1]], [a1])
                            if len(g_) > 2:
                                a2 = sB.get()
                                if len(g_) == 4:
                                    tt(k, "pool", a2[:, :n], g_[2][:, :n], g_[3][:, :n], ALU.add, [g_[2], g_[3]], [a2])
                                    src2 = a2
                                else:
                                    src2 = g_[2]
                                src = sC.get()
                                tt(k, "dve", src[:, :n], a1[:, :n], src2[:, :n], ALU.add, [a1, src2], [src])
                            else:
                                src = sC.get()
                                cp(k, "pool", src[:, :n], a1[:, :n], [a1], [src])

                        def lmm(src=src, gi_=gi_, s_=s_):
                            mm(k, plS[s_][:, :n], onesB[:], src[:, :n], gi_ == 0, gi_ == ngroups - 1, [onesB, src], [plS[s_]])
                        lq.append([6, lmm])

                    def flush_one():
                        s_, i_, kt, pt_ = pend.pop(0)
                        mm(k, poS[s_][:, :n], Va[:, kt, h * 128:(h + 1) * 128], pt_[:, :n], i_ == 0, i_ == len(kts) - 1, [Va, pt_], [poS[s_]])
                        grp[s_].append(pt_)
                        if len(grp[s_]) == 4 or i_ == len(kts) - 1:
                            close_group(s_)
                        for e_ in lq:
                            e_[0] -= 1
                        while lq and lq[0][0] <= 0:
                            lq.pop(0)[1]()
                    for it_, (s_, i_, kt) in enumerate(items):
                        st = pst.get()
                        mm(k, st[:, :n], KS[s_][:, h, kt * 128:(kt + 1) * 128], QRa[:, h, t0:t0 + n], True, True, [KS[s_], QRa], [st])
                        pt_ = ptR.get()
                        act(k, pt_[:, :n], st[:, :n], AF.Exp, [st], [pt_], scale=0.125)
                        pend.append((s_, i_, kt, pt_))
                        if len(pend) > LOOK:
                            flush_one()
                        if it_ == 12:
                            while deferred:
                                deferred.pop(0)()
                    while pend:
                        flush_one()
                    while lq:
                        lq.pop(0)[1]()
                    while deferred:
                        deferred.pop(0)()
                    os_, ls_ = [], []
                    for s_ in range(2):
                        o_ = oS[s_].get()
                        l_ = lS[s_].get()
                        cp(k, "dve", o_[:, :n], poS[s_][:, :n], [poS[s_]], [o_])
                        cp(k, "dve", l_[:, :n], plS[s_][:, :n], [plS[s_]], [l_])
                        os_.append(o_)
                        ls_.append(l_)
                    for s_ in range(2):
                        k.op("dve", lambda g: g.reciprocal(out=ls_[s_][:, :n], in_=ls_[s_][:, :n]), R=[ls_[s_]], W=[ls_[s_]])
                    acc, sqd, rn = accR.get(), sqdR.get(), rnR.get()
                    tt(k, "dve", acc[:, :n], os_[0][:, :n], ls_[0][:, :n], ALU.mult, [os_[0], ls_[0]], [acc])
                    tt(k, "pool", os_[1][:, :n], os_[1][:, :n], ls_[1][:, :n], ALU.mult, [os_[1], ls_[1]], [os_[1]])
                    stt(k, acc[:, :n], os_[1][:, :n], sv[:, 2:3], acc[:, :n], ALU.mult, ALU.add, [os_[1], sv, acc], [acc])
                    k.op("pool", lambda g: g.tensor_tensor(out=sqd[:, :n], in0=acc[:, :n], in1=acc[:, :n], op=ALU.mult), R=[acc], W=[sqd])

                    def part2(acc=acc, sqd=sqd, rn=rn, yc=yc, t0=t0, n=n):
                        mm(k, pn[:, :n], onesB[:], sqd[:, :n], True, True, [onesB, sqd], [pn])
                        act(k, rn[:, :n], pn[:, :n], AF.Sqrt, [pn, cst], [rn], scale=1.0 / 128, bias=EPS5)
                        k.op("dve", lambda g: g.reciprocal(out=rn[:, :n], in_=rn[:, :n]), R=[rn], W=[rn])
                        stt(k, yc[:, t0:t0 + n], acc[:, :n], dnw[:, 0:1], rn[:, :n], ALU.mult, ALU.mult, [acc, dnw, rn], [yc])
                    deferred.append(part2)
                while deferred:
                    deferred.pop(0)()
                k.dma("sp", YC.ap[h * 128:(h + 1) * 128, :], yc[:], R=[yc], W=[YC])
            k.barrier()
        k.es = es0

    def phase_E(l, last):
        with ExitStack() as es:
            k.es = es
            Wbr = k.sb([128, 3, 4, D], BF16, "Wbr")
            Wo = k.sb([128, 8, D], BF16, "Wo")
            for nb in range(3):
                k.dma("pool", Wbr[:, nb], w_branch.ap[l, nb].rearrange("(k p) c -> p k c", p=128), R=[w_branch], W=[Wbr])
            for h2 in range(2):
                k.dma("pool", Wo[:, h2 * 4:h2 * 4 + 4, :], w_out.ap[l].rearrange("(k p) c -> p k c", p=128)[:, h2 * 4:h2 * 4 + 4, :], R=[w_out], W=[Wo])
            dnw = k.sb([128, 1], F32, "dnwE")
            k.dma("sp", dnw[:], dn_normT.ap[l], R=[dn_normT], W=[dnw])
            WguR = Rot([k.sb([128, 8, 256], BF16, "Wgu") for _ in range(4)])
            WdR = Rot([k.sb([128, 22, 128], BF16, "Wd") for _ in range(2)])
            tmp = k.sb([128, 8, 512], F32, "tmpE")
            sq = k.sb([128, 8, 512], BF16, "sqE")
            zs = k.sb([128, 4, 512], BF16, "zsE")
            ya = k.sb([128, 4, 512], BF16, "yaE")
            yb = k.sb([128, 4, 512], BF16, "ybE")
            yc = k.sb([128, 4, 512], BF16, "ycE")
            sgR = Rot([k.sb([128, 3, 512], BF16, "sgE") for _ in range(2)])
            merged = k.sb([128, 8, 512], BF16, "mergedE")
            HB = [k.sb([128, 8, 512], F32, "hbE%d" % i) for i in range(2)]
            U2 = [k.sb([128, 8, 512], BF16, "u2E%d" % i) for i in range(2)]
            hid = k.sb([128, 22, 512], BF16, "hidE")
            rs = k.sb([128, 512], F32, "rsE")
            rs2 = k.sb([128, 512], F32, "rs2E")
            macc = k.sb([128, 512], F32, "maccE")
            mt = k.sb([128, 512], F32, "mtE")
            sgl = Rot([k.sb([128, 512], F32, "sglE") for _ in range(2)])
            otbufs = [Buf(hid.ap[:, 8 + 4 * i:12 + 4 * i, :].bitcast(F32).rearrange("p a b -> p (a b)"), "otE%d" % i) for i in range(2)]
            otile = Rot(otbufs)
            pg1 = Rot([k.ps([128, 512], F32, "pgE1") for _ in range(3)])
            pg2 = Rot([k.ps([128, 512], F32, "pgE2") for _ in range(3)])
            pss = k.ps([128, 512], F32, "pssE")
            ppT = k.ps([128, 4, 128], F32, "ppTE")
            ytv = lambda Y: Y.ap.rearrange("(k p) t -> p k t", p=128)
            otv = OT.ap.rearrange("d (h f) t -> d f h t", h=4)
            sgv = SG.ap.rearrange("(nb dc p) t -> p nb dc t", nb=3, dc=8)
            wgv = w_g.ap[l].rearrange("(k p) c -> p k c", p=128)
            wuv = w_u.ap[l].rearrange("(k p) c -> p k c", p=128)
            wdv = w_d.ap[l].rearrange("(fc p) c -> p fc c", p=128)
            blocks = [bk for bk in BLOCKS if not (last and bk[0] < NCTX)]

            def stage1(bi):
                t0, n = blocks[bi]
                j = 0 if t0 < NCTX else 1
                hb, u2 = HB[bi % 2], U2[bi % 2]
                k.dma("sp", tmp[:, 0:4, :n], otv[0, :, :, t0:t0 + n], R=[OT], W=[tmp])
                k.dma("sp", tmp[:, 4:8, :n], otv[1, :, :, t0:t0 + n], R=[OT], W=[tmp])
                k.dma("sp", zs[:, :, :n], ytv(ZS)[:, :, t0:t0 + n], R=[ZS], W=[zs])
                k.dma("sp", yb[:, :, :n], ytv(YB)[:, :, t0:t0 + n], R=[YB], W=[yb])
                k.dma("sp", yc[:, :, :n], ytv(YC)[:, :, t0:t0 + n], R=[YC], W=[yc])
                k.dma("sp", hb[:, :, :n], hTv[:, :, t0:t0 + n], R=[hT], W=[hb])
                yield
                tt(k, "pool", tmp[:, 0:4, :n], tmp[:, 0:4, :n], tmp[:, 4:8, :n], ALU.add, [tmp], [tmp])
                act(k, sq[:, 0:4, :n], tmp[:, 0:4, :n], AF.Square, [tmp], [sq])
                yield
                for h in range(4):
                    mm(k, pss[:, :n], onesB[:], sq[:, h, :n], True, True, [onesB, sq], [pss])
                    act(k, rs[:, :n], pss[:, :n], AF.Sqrt, [pss, cst], [rs], scale=1.0 / 128, bias=EPS6)
                    k.op("dve", lambda g: g.reciprocal(out=rs[:, :n], in_=rs[:, :n]), R=[rs], W=[rs])
                    tt(k, "dve", tmp[:, 4 + h, :n], tmp[:, h, :n], rs[:, :n], ALU.mult, [tmp, rs], [tmp])
                    stt(k, ya[:, h, :n], tmp[:, 4 + h, :n], dnw[:, 0:1], zs[:, h, :n], ALU.mult, ALU.mult, [tmp, dnw, zs], [ya])
                    yield
                for _ in range(4):
                    yield
                ys = [ya, yb, yc]
                for dc in range(8):
                    sg = sgR.get()
                    k.dma("sp", sg[:, :, :n], sgv[:, :, dc, t0:t0 + n], R=[SG], W=[sg])
                    for nb in range(3):
                        pu = pg1.get()
                        for k4 in range(4):
                            mm(k, pu[:, :n], Wbr[:, nb, k4, dc * 128:(dc + 1) * 128], ys[nb][:, k4, :n], k4 == 0, k4 == 3, [Wbr, ys[nb]], [pu])
                        if nb == 0:
                            tt(k, "dve", macc[:, :n], pu[:, :n], sg[:, 0, :n], ALU.mult, [pu, sg], [macc])
                        else:
                            tt(k, "dve", mt[:, :n], pu[:, :n], sg[:, nb, :n], ALU.mult, [pu, sg], [mt])
                            if nb == 1:
                                tt(k, "pool", macc[:, :n], macc[:, :n], mt[:, :n], ALU.add, [macc, mt], [macc])
                            else:
                                tt(k, "pool", merged[:, dc, :n], macc[:, :n], mt[:, :n], ALU.add, [macc, mt], [merged])
                        yield
                for dc in range(8):
                    py = pg1.get()
                    for k8 in range(8):
                        mm(k, py[:, :n], Wo[:, k8, dc * 128:(dc + 1) * 128], merged[:, k8, :n], k8 == 0, k8 == 7, [Wo, merged], [py])
                    stt(k, hb[:, dc, :n], py[:, :n], AB[:, l, 2, dc, j:j + 1], hb[:, dc, :n], ALU.mult, ALU.add, [py, AB, hb], [hb])
                    yield
                act(k, sq[:, :, :n], hb[:, :, :n], AF.Square, [hb], [sq])
                yield
                for kk in range(8):
                    mm(k, pss[:, :n], onesB[:], sq[:, kk, :n], kk == 0, kk == 7, [onesB, sq], [pss])
                act(k, rs[:, :n], pss[:, :n], AF.Sqrt, [pss, cst], [rs], scale=1.0 / D, bias=EPS6)
                k.op("dve", lambda g: g.reciprocal(out=rs[:, :n], in_=rs[:, :n]), R=[rs], W=[rs])
                yield
                for kk in range(8):
                    tt(k, "dve", tmp[:, kk, :n], hb[:, kk, :n], rs[:, :n], ALU.mult, [hb, rs], [tmp])
                    act(k, u2[:, kk, :n], tmp[:, kk, :n], AF.Identity, [tmp, AB], [u2],
                        scale=AB[:, l, 3, kk, j:j + 1], bias=AB[:, l, 4, kk, j:j + 1])
                    if kk % 2 == 1:
                        yield

            def stage2(bi):
                t0, n = blocks[bi]
                j = 0 if t0 < NCTX else 1
                hb, u2 = HB[bi % 2], U2[bi % 2]
                for f2 in range(11):
                    wg_ = WguR.get()
                    k.dma("sp", wg_[:], WGT.ap[f2], R=[WGT], W=[wg_])
                    wu_ = WguR.get()
                    k.dma("sp", wu_[:], WUT.ap[f2], R=[WUT], W=[wu_])
                    for c2 in range(2):
                        fc = f2 * 2 + c2
                        pg_ = pg2.get()
                        for kk in range(8):
                            mm(k, pg_[:, :n], wg_[:, kk, c2 * 128:(c2 + 1) * 128], u2[:, kk, :n], kk == 0, kk == 7, [wg_, u2], [pg_])
                        pu_ = pg2.get()
                        for kk in range(8):
                            mm(k, pu_[:, :n], wu_[:, kk, c2 * 128:(c2 + 1) * 128], u2[:, kk, :n], kk == 0, kk == 7, [wu_, u2], [pu_])
                        sg_ = sgl.get()
                        act(k, sg_[:, :n], pg_[:, :n], AF.Silu, [pg_], [sg_])
                        tt(k, "dve", hid[:, fc, :n], sg_[:, :n], pu_[:, :n], ALU.mult, [sg_, pu_], [hid] + (otbufs if (last and 8 <= fc < 16) else []))
                        yield
                for dc in range(8):
                    wd_ = WdR.get()
                    k.dma("sp", wd_[:], WDT.ap[dc], R=[WDT], W=[wd_])
                    py = pg2.get()
                    for fc in range(22):
                        mm(k, py[:, :n], wd_[:, fc, :], hid[:, fc, :n], fc == 0, fc == 21, [wd_, hid], [py])
                    stt(k, hb[:, dc, :n], py[:, :n], AB[:, l, 5, dc, j:j + 1], hb[:, dc, :n], ALU.mult, ALU.add, [py, AB, hb], [hb])
                    yield
                if not last:
                    k.dma("sp", hTv[:, :, t0:t0 + n], hb[:, :, :n], R=[hb], W=[hT])
                    yield
                else:
                    sq2 = hid
                    act(k, sq2[:, 0:8, :n], hb[:, :, :n], AF.Square, [hb], [sq2])
                    for kk in range(8):
                        mm(k, ppT[:].rearrange("p a b -> p (a b)")[:, :n], onesB[:], sq2[:, kk, :n], kk == 0, kk == 7, [onesB, sq2], [ppT])
                    act(k, rs2[:, :n], ppT[:].rearrange("p a b -> p (a b)")[:, :n], AF.Sqrt, [ppT, cst], [rs2], scale=1.0 / D, bias=EPS6)
                    k.op("dve", lambda g: g.reciprocal(out=rs2[:, :n], in_=rs2[:, :n]), R=[rs2], W=[rs2])
                    yield
                    for kk in range(8):
                        stt(k, hb[:, kk, :n], hb[:, kk, :n], nrmF[:, kk:kk + 1], rs2[:, :n], ALU.mult, ALU.mult, [hb, nrmF, rs2], [hb])
                    yield
                    for q in range(n // 128):
                        ot = otile.get()
                        for half in range(2):
                            for qq in range(4):
                                kk = half * 4 + qq
                                tr(k, ppT[:, qq, :], hb[:, kk, q * 128:(q + 1) * 128], identF[:], [hb, identF], [ppT])
                            cp(k, k.alt(), ot[:, half * 512:(half + 1) * 512], ppT[:].rearrange("p a b -> p (a b)"), [ppT], [ot, hid])
                        r0 = t0 - NCTX + q * 128
                        k.dma("sp", out_d.ap[r0:r0 + 128, :], ot[:], R=[ot], W=[out_d])
                        yield

            def run_pair(g1, g2):
                gens = [g for g in (g1, g2) if g is not None]
                alive = [True] * len(gens)
                while any(alive):
                    for gi in range(len(gens)):
                        if alive[gi]:
                            try:
                                next(gens[gi])
                            except StopIteration:
                                alive[gi] = False
            nbk = len(blocks)
            run_pair(stage1(0), None)
            for bi in range(nbk):
                run_pair(stage2(bi), stage1(bi + 1) if bi + 1 < nbk else None)
            k.barrier()
        k.es = es0

    done = False
    for l in range(DEPTH):
        last = (l == DEPTH - 1)
        for name, fn in (("A", lambda: phase_A(l)), ("C", lambda: phase_C(l)), ("B", lambda: phase_B(l)),
                         ("D", lambda: phase_D(l)), ("E", lambda: phase_E(l, last))):
            fn()
            if stop_after == "%s%d" % (name, l):
                done = True
                break
        if done:
            break
    k.barrier()
    es0.close()
    return nc, k


def _consts():
    c = {}
    c["c_ident"] = np.eye(128, dtype=np.float32)
    inv = (10000.0 ** (-np.arange(16, dtype=np.float32) / 16)).astype(np.float32)
    lt = np.arange(NLAT)
    row = (lt // 64).astype(np.float32)
    col = (lt % 64).astype(np.float32)
    cos = np.ones((128, T), np.float32)
    sin = np.zeros((128, T), np.float32)
    perm = np.zeros((128, 128), np.float32)
    for p in range(128):
        e = p % 64
        axis = e // 32
        half = (e % 32) // 16
        f = e % 16
        pos = row if axis == 0 else col
        ang = (pos * inv[f]).astype(np.float32)
        cos[p, NCTX:] = np.cos(ang)
        sn = np.sin(ang)
        sin[p, NCTX:] = -sn if half == 0 else sn
        partner = p + 16 if half == 0 else p - 16
        perm[partner, p] = 1.0
    c["c_rope"] = np.stack([cos, sin]).astype(np.float32)
    c["c_perm"] = perm
    jj, ii = np.meshgrid(np.arange(128), np.arange(128), indexing="ij")
    mb = np.stack([np.where(jj <= ii, 0.0, NEG), np.where(jj >= ii, 0.0, NEG)]).astype(np.float32)
    c["c_mb"] = mb
    c["c_nod"] = (-(1.0 - np.eye(128))).astype(np.float32)
    m01 = np.ones((4, 2, 512), np.float32)
    m01[:, 0, 0::128] = 0.0
    m01[:, 1, 127::128] = 0.0
    c["c_m01"] = m01
    sel = np.zeros((4, 4, 128), np.float32)
    for h in range(4):
        sel[h, h, :] = 1.0
    c["c_sel"] = sel
    bl = np.zeros((5, 128, 128), np.float32)
    bl[0] = (jj // 8 == ii // 8)
    for m_, b_ in enumerate((8, 16, 32, 64)):
        bl[m_ + 1] = (jj // (2 * b_) == ii // (2 * b_)) & (jj // b_ != ii // b_)
    c["c_blk"] = bl
    return c


def _prep_shared(inp):
    f = lambda a: np.ascontiguousarray(a, dtype=np.float32)
    s = {}
    s["w_mod"] = f(inp["w_mod"])
    s["b_modT"] = f(inp["b_mod"].reshape(DEPTH, 48, 128).transpose(0, 2, 1))
    nm = np.stack([inp["norm_mix"], inp["norm_ffn"]], 1)
    s["normsT"] = f(nm.reshape(DEPTH, 2, 8, 128).transpose(3, 0, 1, 2))
    s["normfT"] = f(inp["norm_final"].reshape(8, 128).T)
    s["w_in"] = f(inp["w_in"])
    s["dn_convT"] = f(inp["dn_conv"].reshape(DEPTH, 4, 12, 128).transpose(0, 3, 2, 1))
    s["dnab"] = f(np.concatenate([inp["dn_a_log"].transpose(0, 2, 1), inp["dn_dt_bias"].transpose(0, 2, 1)], -1))
    s["dn_normT"] = f(inp["dn_norm"].reshape(DEPTH, 128, 1))
    s["lru_cw"] = f(inp["lru_conv_w"].reshape(DEPTH, 4, 4, 128).transpose(0, 3, 2, 1))
    s["lru_cb"] = f(inp["lru_conv_b"].reshape(DEPTH, 4, 128).transpose(0, 2, 1))
    lv = np.stack([inp["lru_ba"], inp["lru_bi"], inp["lru_lambda"]], 1)
    s["lru_vec"] = f(lv.reshape(DEPTH, 3, 2, 4, 128).transpose(0, 4, 1, 2, 3))
    wb = np.zeros((DEPTH, 2, 2, 4, 128, 128), np.float32)
    for ai, w in enumerate([inp["lru_wa"], inp["lru_wi"]]):
        for cc in range(4):
            for g2 in range(2):
                wb[:, ai, :, cc, g2 * 64:(g2 + 1) * 64, g2 * 64:(g2 + 1) * 64] = w[:, :, cc * 2 + g2]
    s["lru_wblk"] = wb
    s["da_lam"] = f(inp["da_lambda"].reshape(DEPTH, 256))
    s["da_normT"] = f(inp["da_norm"].reshape(DEPTH, 128, 1))
    s["w_branch"] = f(inp["w_branch"])
    s["w_out"] = f(inp["w_out"])
    s["w_ffn_gate"] = f(inp["w_ffn_gate"])
    s["w_ffn_up"] = f(inp["w_ffn_up"])
    s["w_ffn_down"] = f(inp["w_ffn_down"])
    s.update(_consts())
    return s


def _prep_core(inp, b):
    m = {}
    m["xin"] = np.ascontiguousarray(np.concatenate([inp["ctx"][b], inp["x"][b]], 0), dtype=np.float32)
    cv = np.stack([inp["c_ctx"], inp["c"][b]], 0)
    m["cT"] = np.ascontiguousarray(cv.reshape(2, 8, 128).transpose(2, 1, 0), dtype=np.float32)
    return m


_CACHE = {}


def kernel(**inputs):
    inp = {k_: np.asarray(v) for k_, v in inputs.items()}
    if "nc" not in _CACHE:
        _CACHE["nc"] = build()[0]
    nc = _CACHE["nc"]
    shared = _prep_shared(inp)
    in_maps = []
    for b in range(8):
        m = dict(shared)
        m.update(_prep_core(inp, b))
        in_maps.append(m)
    res = run_bass_kernel_spmd(nc, in_maps, core_ids=list(range(8)))
    return np.stack([np.asarray(r["out"], dtype=np.float32) for r in res.results], 0)
```

```python
import math
import numpy as np
from contextlib import ExitStack
import concourse.bass as bass
import concourse.mybir as mybir
from concourse.bass_utils import run_bass_kernel_spmd
from concourse.alu_op_type import AluOpType as ALU

AF = mybir.ActivationFunctionType
F32 = mybir.dt.float32
BF16 = mybir.dt.bfloat16

D = 1024
NCTX = 256
NLAT = 4096
T = NCTX + NLAT
DEPTH = 2
DFF = 2816
INC = 7696
BLOCKS = [(0, 256)] + [(256 + 512 * j, 512) for j in range(8)]
NEG = -30000.0


class Buf:
    def __init__(self, ap, name, share=None):
        self.ap = ap
        self.name = name
        self.st = share.st if share is not None else [None, []]

    @property
    def w(self):
        return self.st[0]

    @w.setter
    def w(self, v):
        self.st[0] = v

    @property
    def r(self):
        return self.st[1]

    @r.setter
    def r(self, v):
        self.st[1] = v

    def __getitem__(self, k):
        return self.ap[k]


class K:
    NDMA = 8

    def __init__(self, nc, es):
        self.nc = nc
        self.es = es
        self.eng = {"pe": nc.tensor, "act": nc.scalar, "dve": nc.vector,
                    "pool": nc.gpsimd, "sp": nc.sync}
        self.sem = {}
        self.cnt = {}
        for e in self.eng:
            self.sem[e] = es.enter_context(nc.semaphore("s_" + e))
            self.cnt[e] = 0
        self.dring = {}
        for q in ("sp", "pool"):
            ring = []
            for i in range(self.NDMA):
                key = "d_%s%d" % (q, i)
                self.sem[key] = es.enter_context(nc.semaphore(key))
                self.cnt[key] = 0
                ring.append(key)
            self.dring[q] = ring
        self.dpos = {q: 0 for q in self.dring}
        self.seen = {e: {} for e in self.eng}
        self.nbuf = 0
        self.ninst = 0
        self.rr = 0

    def sb(self, shape, dt, name=None):
        self.nbuf += 1
        name = (name or "sb") + "_%d" % self.nbuf
        t = self.es.enter_context(self.nc.sbuf_tensor(name, list(shape), dt))
        return Buf(t, name)

    def ps(self, shape, dt, name=None):
        self.nbuf += 1
        name = (name or "ps") + "_%d" % self.nbuf
        t = self.es.enter_context(self.nc.psum_tensor(name, list(shape), dt))
        return Buf(t, name)

    def _deps(self, R, W):
        d = {}

        def add(ev):
            if ev is None:
                return
            k, v = ev
            if d.get(k, 0) < v:
                d[k] = v
        for b in R:
            add(b.w)
        for b in W:
            add(b.w)
            for ev in b.r:
                add(ev)
        return d

    def _wait(self, e, d):
        eng = self.eng[e]
        seen = self.seen[e]
        for k, v in d.items():
            if e == "pe" and k == "pe":
                continue
            if seen.get(k, 0) >= v:
                continue
            eng.wait_ge(self.sem[k], v)
            seen[k] = v

    def _mark(self, ev, R, W):
        for b in R:
            b.r = [x for x in b.r if x[0] != ev[0]] + [ev]
        for b in W:
            b.w = ev
            b.r = []

    def op(self, e, fn, R=(), W=()):
        d = self._deps(R, W)
        self._wait(e, d)
        inst = fn(self.eng[e])
        self.cnt[e] += 1
        inst.then_inc(self.sem[e], 1)
        self._mark((e, self.cnt[e]), R, W)
        self.ninst += 1

    def dma(self, q, out, in_, R=(), W=(), **kw):
        ring = self.dring[q]
        key = ring[self.dpos[q] % self.NDMA]
        self.dpos[q] += 1
        d = self._deps(R, W)
        if self.cnt[key] > 0 and d.get(key, 0) < self.cnt[key]:
            d[key] = self.cnt[key]
        self._wait(q, d)
        inst = self.eng[q].dma_start(out=out, in_=in_, **kw)
        self.cnt[key] += 16
        inst.then_inc(self.sem[key], 16)
        self._mark((key, self.cnt[key]), R, W)
        self.ninst += 1

    def barrier(self):
        d = {k: v for k, v in self.cnt.items() if v > 0}
        for e in self.eng:
            self._wait(e, dict(d))

    def alt(self):
        self.rr += 1
        return "act" if self.rr % 2 else "dve"


def tt(k, e, out, a, b, op, R, W):
    k.op(e, lambda g: g.tensor_tensor(out=out, in0=a, in1=b, op=op), R=R, W=W)


def ts(k, e, out, a, s1, s2, op0, op1, R, W):
    if op1 is None:
        k.op(e, lambda g: g.tensor_scalar(out=out, in0=a, scalar1=s1, scalar2=None, op0=op0), R=R, W=W)
    else:
        k.op(e, lambda g: g.tensor_scalar(out=out, in0=a, scalar1=s1, scalar2=s2, op0=op0, op1=op1), R=R, W=W)


def stt(k, out, a, s, b, op0, op1, R, W):
    k.op("dve", lambda g: g.scalar_tensor_tensor(out=out, in0=a, scalar=s, in1=b, op0=op0, op1=op1), R=R, W=W)


def act(k, out, a, func, R, W, scale=None, bias=None):
    kw = {}
    if scale is not None:
        kw["scale"] = scale
    if bias is not None:
        kw["bias"] = bias
    k.op("act", lambda g: g.activation(out=out, in_=a, func=func, **kw), R=R, W=W)


def cp(k, e, out, a, R, W):
    if e == "act":
        k.op("act", lambda g: g.activation(out=out, in_=a, func=AF.Copy), R=R, W=W)
    else:
        k.op(e, lambda g: g.tensor_copy(out=out, in_=a), R=R, W=W)


def mm(k, out, lhsT, rhs, start, stop, R, W):
    k.op("pe", lambda g: g.matmul(out, lhsT=lhsT, rhs=rhs, start=start, stop=stop), R=R, W=W)


def tr(k, out, in_, ident, R, W):
    k.op("pe", lambda g: g.transpose(out=out, in_=in_, identity=ident), R=R, W=W)


class Rot:
    def __init__(self, bufs):
        self.bufs = bufs
        self.i = 0

    def get(self):
        b = self.bufs[self.i % len(self.bufs)]
        self.i += 1
        return b


def build(stop_after=None, debug=False):
    nc = bass.Bass("TRN2", target_bir_lowering=False)
    SK = "ExternalOutput" if debug else "Internal"

    def din(name, shape, dt=F32):
        return Buf(nc.dram_tensor(name, list(shape), dt, kind="ExternalInput").ap(), name)

    def dsc(name, shape, dt):
        return Buf(nc.dram_tensor(name, list(shape), dt, kind=SK).ap(), name)

    xin = din("xin", [T, D])
    cT_d = din("cT", [128, 8, 2])
    w_mod = din("w_mod", [DEPTH, D, 6 * D])
    b_modT = din("b_modT", [DEPTH, 128, 48])
    normsT = din("normsT", [128, DEPTH, 2, 8])
    normfT = din("normfT", [128, 8])
    w_in = din("w_in", [DEPTH, D, INC])
    dn_convT = din("dn_convT", [DEPTH, 128, 12, 4])
    dnab = din("dnab", [DEPTH, 4, 4])
    dn_normT = din("dn_normT", [DEPTH, 128, 1])
    lru_cw = din("lru_cw", [DEPTH, 128, 4, 4])
    lru_cb = din("lru_cb", [DEPTH, 128, 4])
    lru_vec = din("lru_vec", [DEPTH, 128, 3, 2, 4])
    lru_wblk = din("lru_wblk", [DEPTH, 2, 2, 4, 128, 128])
    da_lam = din("da_lam", [DEPTH, 256])
    da_normT = din("da_normT", [DEPTH, 128, 1])
    w_branch = din("w_branch", [DEPTH, 3, 512, D])
    w_out = din("w_out", [DEPTH, D, D])
    w_g = din("w_ffn_gate", [DEPTH, D, DFF])
    w_u = din("w_ffn_up", [DEPTH, D, DFF])
    w_d = din("w_ffn_down", [DEPTH, DFF, D])
    c_ident = din("c_ident", [128, 128])
    c_rope = din("c_rope", [2, 128, T])
    c_perm = din("c_perm", [128, 128])
    c_mb = din("c_mb", [2, 128, 128])
    c_nod = din("c_nod", [128, 128])
    c_m01 = din("c_m01", [4, 2, 512])
    c_sel = din("c_sel", [4, 4, 128])
    c_blk = din("c_blk", [5, 128, 128])
    out_d = Buf(nc.dram_tensor("out", [NLAT, D], F32, kind="ExternalOutput").ap(), "out")

    hT = dsc("hT", [D, T], F32)
    DNQKV = dsc("DNQKV", [3, 512, T], BF16)
    ZS = dsc("ZS", [512, T], BF16)
    GCB = dsc("GCB", [4, 4, T], F32)
    LX = dsc("LX", [512, T], F32)
    LG = dsc("LG", [512, T], BF16)
    QR = dsc("QR", [512, T], BF16)
    KR = dsc("KR", [512, T], BF16)
    VT = dsc("VT", [T, 512], BF16)
    SG = dsc("SG", [3072, T], BF16)
    OT = dsc("OT", [2, 512, T], F32)
    YB = dsc("YB", [512, T], BF16)
    YC = dsc("YC", [512, T], BF16)
    hTv = hT.ap.rearrange("(k p) t -> p k t", p=128)
    WGT = dsc("WGT", [11, 128, 8, 256], BF16)
    WUT = dsc("WUT", [11, 128, 8, 256], BF16)
    WDT = dsc("WDT", [8, 128, 22, 128], BF16)

    def precast_ffn(l):
        wgv = w_g.ap[l].rearrange("(k p) c -> p k c", p=128)
        wuv = w_u.ap[l].rearrange("(k p) c -> p k c", p=128)
        wdv = w_d.ap[l].rearrange("(fc p) c -> p fc c", p=128)
        for f2 in range(11):
            k.dma("pool", WGT.ap[f2], wgv[:, :, f2 * 256:(f2 + 1) * 256], R=[w_g], W=[WGT])
            k.dma("pool", WUT.ap[f2], wuv[:, :, f2 * 256:(f2 + 1) * 256], R=[w_u], W=[WUT])
        for dc in range(8):
            k.dma("pool", WDT.ap[dc], wdv[:, :, dc * 128:(dc + 1) * 128], R=[w_d], W=[WDT])

    es0 = ExitStack()
    k = K(nc, es0)
    identF = k.sb([128, 128], F32, "identF")
    identB = k.sb([128, 128], BF16, "identB")
    onesB = k.sb([128, 128], BF16, "onesB")
    modT = k.sb([128, DEPTH, 48, 2], F32, "modT")
    AB = k.sb([128, DEPTH, 6, 8, 2], F32, "AB")
    nrmT = k.sb([128, DEPTH, 2, 8], F32, "nrmT")
    nrmF = k.sb([128, 8], F32, "nrmF")
    cst = k.sb([128, 4], F32, "cst")
    k.dma("sp", identF[:], c_ident.ap, R=[c_ident], W=[identF])
    k.dma("sp", nrmT[:], normsT.ap, R=[normsT], W=[nrmT])
    k.dma("sp", nrmF[:], normfT.ap, R=[normfT], W=[nrmF])
    cp(k, "dve", identB[:], identF[:], [identF], [identB])
    k.op("dve", lambda g: g.memset(onesB[:], 1.0), W=[onesB])
    k.op("dve", lambda g: g.memset(cst[:, 0:1], 1e-6), W=[cst])
    k.op("dve", lambda g: g.memset(cst[:, 1:2], 1e-5), W=[cst])
    k.op("dve", lambda g: g.memset(cst[:, 2:3], 1.0), W=[cst])
    k.op("dve", lambda g: g.memset(cst[:, 3:4], 0.0), W=[cst])
    EPS6, EPS5, ONE = cst[:, 0:1], cst[:, 1:2], cst[:, 2:3]

    def norm_block(l, which, hb, n, j, sq, pss, rs, tmp, ubuf, uout):
        act(k, sq[:, :, :n], hb[:, :, :n], AF.Square, [hb], [sq])
        for kk in range(8):
            mm(k, pss[:, :n], onesB[:], sq[:, kk, :n], kk == 0, kk == 7, [onesB, sq], [pss])
        act(k, rs[:, :n], pss[:, :n], AF.Sqrt, [pss, cst], [rs], scale=1.0 / D, bias=EPS6)
        k.op("dve", lambda g: g.reciprocal(out=rs[:, :n], in_=rs[:, :n]), R=[rs], W=[rs])
        for kk in range(8):
            tt(k, "dve", tmp[:, kk, :n], hb[:, kk, :n], rs[:, :n], ALU.mult, [hb, rs], [tmp])
            act(k, uout(kk), tmp[:, kk, :n], AF.Identity, [tmp, AB], [ubuf],
                scale=AB[:, l, 3 * which + 0, kk, j:j + 1], bias=AB[:, l, 3 * which + 1, kk, j:j + 1])

    with ExitStack() as es:
        k.es = es
        cTs = k.sb([128, 8, 2], F32, "cTs")
        sTs = k.sb([128, 8, 2], F32, "sTs")
        bm = k.sb([128, DEPTH, 48], F32, "bm")
        k.dma("sp", cTs[:], cT_d.ap, R=[cT_d], W=[cTs])
        for l in range(DEPTH):
            k.dma("sp", bm[:, l, :], b_modT.ap[l], R=[b_modT], W=[bm])
        act(k, sTs[:], cTs[:], AF.Silu, [cTs], [sTs])
        wm = Rot([k.sb([128, 8, 512], F32, "wm") for _ in range(2)])
        pmod = k.ps([128, 48, 2], F32, "pmod")
        for l in range(DEPTH):
            for g in range(12):
                wt = wm.get()
                k.dma("sp", wt[:], w_mod.ap[l].rearrange("(k p) c -> p k c", p=128)[:, :, g * 512:(g + 1) * 512], R=[w_mod], W=[wt])
                for c4 in range(4):
                    ch = g * 4 + c4
                    for kk in range(8):
                        mm(k, pmod[:, ch, :], wt[:, kk, c4 * 128:(c4 + 1) * 128], sTs[:, kk, :], kk == 0, kk == 7, [wt, sTs], [pmod])
            for j in range(2):
                tt(k, "dve", modT[:, l, :, j], pmod[:, :, j], bm[:, l, :], ALU.add, [pmod, bm], [modT])
            for s, (ish, isc, ig) in enumerate([(0, 1, 2), (3, 4, 5)]):
                for j in range(2):
                    stt(k, AB[:, l, 3 * s + 0, :, j], modT[:, l, isc * 8:(isc + 1) * 8, j], 1.0, nrmT[:, l, s, :], ALU.add, ALU.mult, [modT, nrmT], [AB])
                    cp(k, "dve", AB[:, l, 3 * s + 1, :, j], modT[:, l, ish * 8:(ish + 1) * 8, j], [modT], [AB])
                    cp(k, "dve", AB[:, l, 3 * s + 2, :, j], modT[:, l, ig * 8:(ig + 1) * 8, j], [modT], [AB])
        k.barrier()
    k.es = es0

    with ExitStack() as es:
        k.es = es
        xt = Rot([k.sb([128, D], F32, "xt") for _ in range(2)])
        ht = Rot([k.sb([128, 8, 128], F32, "ht") for _ in range(2)])
        pp = Rot([k.ps([128, 4, 128], F32, "ppT") for _ in range(4)])
        for t in range(T // 128):
            x_ = xt.get()
            k.dma("sp", x_[:], xin.ap[t * 128:(t + 1) * 128, :], R=[xin], W=[x_])
            h_ = ht.get()
            for half in range(2):
                p_ = pp.get()
                for q in range(4):
                    kk = half * 4 + q
                    tr(k, p_[:, q, :], x_[:, kk * 128:(kk + 1) * 128], identF[:], [x_, identF], [p_])
                cp(k, k.alt(), h_[:, half * 4:half * 4 + 4, :], p_[:], [p_], [h_])
            k.dma("sp", hTv[:, :, t * 128:(t + 1) * 128], h_[:], R=[h_], W=[hT])
        k.barrier()
    k.es = es0

    def off(t):
        return t + 1 if t < NCTX else t + 4

    def phase_A(l):
        with ExitStack() as esA:
            k.es = esA
            UT = k.sb([128, 8, T], BF16, "UT")
            with ExitStack() as es1:
                k.es = es1
                hbR = Rot([k.sb([128, 8, 512], F32, "hbA") for _ in range(2)])
                sq = k.sb([128, 8, 512], BF16, "sqA")
                tmp = k.sb([128, 8, 512], F32, "tmpA")
                rs = k.sb([128, 512], F32, "rsA")
                pss = k.ps([128, 512], F32, "pssA")
                for (t0, n) in BLOCKS:
                    j = 0 if t0 < NCTX else 1
                    h_ = hbR.get()
                    k.dma("sp", h_[:, :, :n], hTv[:, :, t0:t0 + n], R=[hT], W=[h_])
                    norm_block(l, 0, h_, n, j, sq, pss, rs, tmp, UT, lambda kk: UT[:, kk, t0:t0 + n])
                k.barrier()
            k.es = esA
            permB = k.sb([128, 128], BF16, "permB")
            cw = k.sb([128, 12, 4], F32, "cw")
            lcw = k.sb([128, 4, 4], F32, "lcw")
            lcb = k.sb([128, 4], F32, "lcb")
            dnab_s = k.sb([4, 4], F32, "dnab_s")
            nA = k.sb([4, 2], F32, "nA")
            m01 = k.sb([4, 2, 512], F32, "m01")
            k.dma("pool", permB[:], c_perm.ap, R=[c_perm], W=[permB])
            k.dma("sp", cw[:], dn_convT.ap[l], R=[dn_convT], W=[cw])
            k.dma("sp", lcw[:], lru_cw.ap[l], R=[lru_cw], W=[lcw])
            k.dma("sp", lcb[:], lru_cb.ap[l], R=[lru_cb], W=[lcb])
            k.dma("sp", dnab_s[:], dnab.ap[l], R=[dnab], W=[dnab_s])
            k.dma("sp", m01[:], c_m01.ap, R=[c_m01], W=[m01])
            act(k, nA[:], dnab_s[:, 0:2], AF.Exp, [dnab_s], [nA])
            ts(k, "dve", nA[:], nA[:], -1.0, None, ALU.mult, None, [nA], [nA])
            WG = Rot([k.sb([128, 8, 512], BF16, "WG") for _ in range(2)])
            OB = Rot([k.sb([128, T], BF16, "OB") for _ in range(2)])
            pg = Rot([k.ps([128, 512], F32, "pg") for _ in range(4)])
            p2 = Rot([k.ps([128, 512], F32, "p2") for _ in range(2)])

            def rot(shape, dt, name, n=2):
                return Rot([k.sb(shape, dt, name) for _ in range(n)])
            w_inv = w_in.ap[l].rearrange("(k p) c -> p k c", p=128)
            groups_conv = [("dnq", 0, 512), ("dnk", 512, 512), ("dnv", 1024, 512), ("lrux", 2064, 512)]
            groups_rest = [("z", 1536, 512), ("ab", 2048, 16), ("lrug", 2576, 512), ("daq", 3088, 512),
                           ("dak", 3600, 512), ("dav", 4112, 512)] + [("gate%d" % g, 4624 + 512 * g, 512) for g in range(6)]
            def process(groups):
                for (kind, c0, ncol) in groups:
                    wt = WG.get()
                    k.dma("pool", wt[:, :, :ncol], w_inv[:, :, c0:c0 + ncol], R=[w_in], W=[wt])
                    if kind == "dav":
                        for ti in range(T // 128):
                            p_ = pg.get()
                            for kk in range(8):
                                mm(k, p_[:, :], UT[:, kk, ti * 128:(ti + 1) * 128], wt[:, kk, :], kk == 0, kk == 7, [UT, wt], [p_])
                            v_ = vbR.get()
                            cp(k, k.alt(), v_[:], p_[:], [p_], [v_])
                            k.dma("sp", VT.ap[ti * 128:(ti + 1) * 128, :], v_[:], R=[v_], W=[VT])
                        continue
                    if kind == "ab":
                        for (t0, n) in BLOCKS:
                            abt = abR.get()
                            for d in range(2):
                                p_ = pg.get()
                                for kk in range(8):
                                    mm(k, p_[0:4, :n], wt[:, kk, 4 * d:4 * d + 4], UT[:, kk, t0:t0 + n], kk == 0, kk == 7, [UT, wt], [p_])
                                e_ = eR.get()
                                act(k, e_[:, :n], p_[0:4, :n], AF.Exp, [p_, dnab_s], [e_], bias=dnab_s[:, 2 + d:3 + d])
                                act(k, e_[:, :n], e_[:, :n], AF.Ln, [e_, cst], [e_], bias=cst[0:4, 2:3])
                                ts(k, "dve", e_[:, :n], e_[:, :n], nA[:, d:d + 1], None, ALU.mult, None, [e_, nA], [e_])
                                if d == 0:
                                    k.op("dve", lambda g: g.tensor_tensor_scan(out=abt[:, 0, :n], data0=m01[:, 0, :n], data1=e_[:, :n], initial=0.0, op0=ALU.mult, op1=ALU.add), R=[m01, e_], W=[abt])
                                else:
                                    k.op("dve", lambda g: g.tensor_tensor_scan(out=abt[:, 1, :n][:, ::-1], data0=m01[:, 1, :n][:, ::-1], data1=e_[:, :n][:, ::-1], initial=0.0, op0=ALU.mult, op1=ALU.add), R=[m01, e_], W=[abt])
                                p_ = pg.get()
                                for kk in range(8):
                                    mm(k, p_[0:4, :n], wt[:, kk, 8 + 4 * d:12 + 4 * d], UT[:, kk, t0:t0 + n], kk == 0, kk == 7, [UT, wt], [p_])
                                act(k, abt[:, 2 + d, :n], p_[0:4, :n], AF.Sigmoid, [p_], [abt])
                            k.dma("sp", GCB.ap[:, :, t0:t0 + n], abt[:, :, :n], R=[abt], W=[GCB])
                        continue
                    for c4 in range(4):
                        conv = kind in ("dnq", "dnk", "dnv", "lrux")
                        xp = XP.get() if conv else None
                        ob = OB.get() if kind != "lrux" else None
                        for (t0, n) in BLOCKS:
                            p_ = pg.get()
                            for kk in range(8):
                                mm(k, p_[:, :n], wt[:, kk, c4 * 128:(c4 + 1) * 128], UT[:, kk, t0:t0 + n], kk == 0, kk == 7, [UT, wt], [p_])
                            if conv:
                                cp(k, k.alt(), xp[:, off(t0):off(t0) + n], p_[:, :n], [p_], [xp])
                            elif kind == "z":
                                act(k, ob[:, t0:t0 + n], p_[:, :n], AF.Silu, [p_], [ob])
                            elif kind == "lrug":
                                act(k, ob[:, t0:t0 + n], p_[:, :n], AF.Gelu, [p_], [ob])
                            elif kind.startswith("gate"):
                                act(k, ob[:, t0:t0 + n], p_[:, :n], AF.Sigmoid, [p_], [ob])
                            else:
                                qr_ = qrawR.get()
                                cp(k, "act", qr_[:, :n], p_[:, :n], [p_], [qr_])
                                q2 = p2.get()
                                mm(k, q2[:, :n], permB[:], qr_[:, :n], True, True, [permB, qr_], [q2])
                                a1 = t1R.get()
                                tt(k, "pool", a1[:, :n], qr_[:, :n], cosT[:, t0:t0 + n], ALU.mult, [qr_, cosT], [a1])
                                a2 = t2R.get()
                                tt(k, "dve", a2[:, :n], q2[:, :n], sinT[:, t0:t0 + n], ALU.mult, [q2, sinT], [a2])
                                tt(k, "pool", ob[:, t0:t0 + n], a1[:, :n], a2[:, :n], ALU.add, [a1, a2], [ob])
                        if conv:
                            if kind == "lrux":
                                wv = lambda jj, c4=c4: lcw[:, c4, jj:jj + 1]
                            else:
                                ci = {"dnq": 0, "dnk": 4, "dnv": 8}[kind] + c4
                                wv = lambda jj, ci=ci: cw[:, ci, jj:jj + 1]
                            wbuf = lcw if kind == "lrux" else cw
                            Dg = DgR.get()
                            for jj in range(4):
                                ts(k, "dve", Dg[:, jj, :], identB[:], wv(jj), None, ALU.mult, None, [identB, wbuf], [Dg])

                            def post(kind=kind, c4=c4, xp=xp, ob=ob, Dg=Dg):
                                ar = accrow.get() if kind != "dnv" else None
                                for (t0, n) in BLOCKS:
                                    c = off(t0)
                                    pc = pg.get()
                                    for idx, (jj, sh) in enumerate(((0, -1), (1, 0), (2, 1), (3, 2))):
                                        mm(k, pc[:, :n], Dg[:, jj, :], xp[:, c + sh:c + sh + n], idx == 0, idx == 3, [Dg, xp], [pc])
                                    if kind == "lrux":
                                        act(k, ar[:, t0:t0 + n], pc[:, :n], AF.Identity, [pc, lcb], [ar], bias=lcb[:, c4:c4 + 1])
                                    elif kind == "dnv":
                                        act(k, ob[:, t0:t0 + n], pc[:, :n], AF.Silu, [pc], [ob])
                                    else:
                                        act(k, ar[:, t0:t0 + n], pc[:, :n], AF.Silu, [pc], [ar])
                                if kind == "lrux":
                                    k.dma("sp", LX.ap[c4 * 128:(c4 + 1) * 128, :], ar[:], R=[ar], W=[LX])
                                    return
                                if kind != "dnv":
                                    for (t0, n) in BLOCKS:
                                        tt(k, "pool", sqrow[:, t0:t0 + n], ar[:, t0:t0 + n], ar[:, t0:t0 + n], ALU.mult, [ar], [sqrow])
                                    for (t0, n) in BLOCKS:
                                        q2 = p2.get()
                                        mm(k, q2[:, :n], onesB[:], sqrow[:, t0:t0 + n], True, True, [onesB, sqrow], [q2])
                                        act(k, lnrow[:, t0:t0 + n], q2[:, :n], AF.Ln, [q2, cst], [lnrow], bias=EPS6)
                                    for (t0, n) in BLOCKS:
                                        act(k, lnrow[:, t0:t0 + n], lnrow[:, t0:t0 + n], AF.Exp, [lnrow], [lnrow], scale=-0.5)
                                    for (t0, n) in BLOCKS:
                                        if kind == "dnq":
                                            stt(k, ob[:, t0:t0 + n], ar[:, t0:t0 + n], 128.0 ** -0.5, lnrow[:, t0:t0 + n], ALU.mult, ALU.mult, [ar, lnrow], [ob])
                                        else:
                                            tt(k, "dve", ob[:, t0:t0 + n], ar[:, t0:t0 + n], lnrow[:, t0:t0 + n], ALU.mult, [ar, lnrow], [ob])
                                dst = DNQKV.ap[{"dnq": 0, "dnk": 1, "dnv": 2}[kind], c4 * 128:(c4 + 1) * 128, :]
                                k.dma("sp", dst, ob[:], R=[ob], W=[DNQKV])
                            while pending:
                                pending.pop(0)()
                            pending.append(post)
                            continue
                        if ob is not None:
                            if kind in ("dnq", "dnk", "dnv"):
                                dst = DNQKV.ap[{"dnq": 0, "dnk": 1, "dnv": 2}[kind], c4 * 128:(c4 + 1) * 128, :]
                                dbuf = DNQKV
                            elif kind == "z":
                                dst, dbuf = ZS.ap[c4 * 128:(c4 + 1) * 128, :], ZS
                            elif kind == "lrug":
                                dst, dbuf = LG.ap[c4 * 128:(c4 + 1) * 128, :], LG
                            elif kind == "daq":
                                dst, dbuf = QR.ap[c4 * 128:(c4 + 1) * 128, :], QR
                            elif kind == "dak":
                                dst, dbuf = KR.ap[c4 * 128:(c4 + 1) * 128, :], KR
                            else:
                                g = int(kind[4:])
                                r0 = (g * 4 + c4) * 128
                                dst, dbuf = SG.ap[r0:r0 + 128, :], SG
                            k.dma("sp", dst, ob[:], R=[ob], W=[dbuf])
            with ExitStack() as e2:
                k.es = e2
                XPs = [k.sb([128, T + 6], BF16, "XP") for _ in range(2)]
                for x_ in XPs:
                    k.op("pool", lambda g: g.memset(x_[:], 0.0), W=[x_])
                XP = Rot(XPs)
                accrow = rot([128, T], F32, "accrow", 2)
                DgR = rot([128, 4, 128], BF16, "Dg", 2)
                pending = []
                sqrow = k.sb([128, T], BF16, "sqrow")
                lnrow = k.sb([128, T], F32, "lnrow")
                process(groups_conv)
                while pending:
                    pending.pop(0)()
                k.barrier()
            with ExitStack() as e3:
                k.es = e3
                cosT = k.sb([128, T], BF16, "cosT")
                sinT = k.sb([128, T], BF16, "sinT")
                k.dma("pool", cosT[:], c_rope.ap[0], R=[c_rope], W=[cosT])
                k.dma("pool", sinT[:], c_rope.ap[1], R=[c_rope], W=[sinT])
                qrawR = rot([128, 512], BF16, "qraw")
                t1R = rot([128, 512], F32, "t1")
                t2R = rot([128, 512], F32, "t2")
                vbR = rot([128, 512], BF16, "vb")
                abR = rot([4, 4, 512], F32, "abt", 2)
                eR = rot([4, 512], F32, "eab")
                process(groups_rest)
                k.barrier()
            k.es = esA
            k.barrier()
        k.es = es0

    def phase_C(l):
        precast_ffn(l)
        with ExitStack() as es:
            k.es = es
            lv = k.sb([128, 3, 2, 4], F32, "lv")
            cneg = k.sb([128, 2, 4], F32, "cneg")
            k.dma("sp", lv[:], lru_vec.ap[l], R=[lru_vec], W=[lv])
            act(k, cneg[:], lv[:, 2], AF.Exp, [lv], [cneg], scale=-1.0)
            act(k, cneg[:], cneg[:], AF.Ln, [cneg, cst], [cneg], bias=ONE)
            ts(k, "dve", cneg[:], cneg[:], -8.0, None, ALU.mult, None, [cneg], [cneg])
            xc = k.sb([128, T], F32, "xc")
            xcb = k.sb([128, T], BF16, "xcb")
            gg = k.sb([128, T], BF16, "gg")
            A_ = k.sb([128, T], F32, "lruA")
            IG = k.sb([128, T], F32, "lruIG")
            W_ = k.sb([128, T], F32, "lruW")
            H = [k.sb([128, T], F32, "lruH%d" % d) for d in range(2)]
            yb = k.sb([128, T], BF16, "lruY")
            wb = k.sb([128, 2, 2, 128], BF16, "lruwb")
            pg = Rot([k.ps([128, 512], F32, "pgC") for _ in range(4)])
            for cc in range(4):
                k.dma("sp", xc[:], LX.ap[cc * 128:(cc + 1) * 128, :], R=[LX], W=[xc])
                k.dma("sp", gg[:], LG.ap[cc * 128:(cc + 1) * 128, :], R=[LG], W=[gg])
                k.dma("pool", wb[:], lru_wblk.ap[l, :, :, cc].rearrange("a d i j -> i a d j"), R=[lru_wblk], W=[wb])
                cp(k, "act", xcb[:], xc[:], [xc], [xcb])
                for d in range(2):
                    for (t0, n) in BLOCKS:
                        p_ = pg.get()
                        mm(k, p_[:, :n], wb[:, 0, d, :], xcb[:, t0:t0 + n], True, True, [wb, xcb], [p_])
                        act(k, A_[:, t0:t0 + n], p_[:, :n], AF.Sigmoid, [p_, lv], [A_], bias=lv[:, 0, d, cc:cc + 1])
                        p_ = pg.get()
                        mm(k, p_[:, :n], wb[:, 1, d, :], xcb[:, t0:t0 + n], True, True, [wb, xcb], [p_])
                        act(k, IG[:, t0:t0 + n], p_[:, :n], AF.Sigmoid, [p_, lv], [IG], bias=lv[:, 1, d, cc:cc + 1])
                    act(k, A_[:], A_[:], AF.Exp, [A_, cneg], [A_], scale=cneg[:, d, cc:cc + 1])
                    act(k, W_[:], A_[:], AF.Square, [A_], [W_])
                    ts(k, "dve", W_[:], W_[:], -1.0, 1.0, ALU.mult, ALU.add, [W_], [W_])
                    act(k, W_[:], W_[:], AF.Sqrt, [W_], [W_])
                    tt(k, "pool", IG[:], IG[:], xc[:], ALU.mult, [IG, xc], [IG])
                    tt(k, "dve", IG[:], IG[:], W_[:], ALU.mult, [IG, W_], [IG])
                    Hd = H[d]
                    if d == 0:
                        k.op("dve", lambda g: g.tensor_tensor_scan(out=Hd[:], data0=A_[:], data1=IG[:], initial=0.0, op0=ALU.mult, op1=ALU.add), R=[A_, IG], W=[Hd])
                    else:
                        k.op("dve", lambda g: g.tensor_tensor_scan(out=Hd[:, 0:NCTX][:, ::-1], data0=A_[:, 0:NCTX][:, ::-1], data1=IG[:, 0:NCTX][:, ::-1], initial=0.0, op0=ALU.mult, op1=ALU.add), R=[A_, IG], W=[Hd])
                        k.op("dve", lambda g: g.tensor_tensor_scan(out=Hd[:, NCTX:T][:, ::-1], data0=A_[:, NCTX:T][:, ::-1], data1=IG[:, NCTX:T][:, ::-1], initial=Hd[:, 0:1], op0=ALU.mult, op1=ALU.add), R=[A_, IG, Hd], W=[Hd])
                tt(k, "pool", H[0][:], H[0][:], H[1][:], ALU.add, [H[0], H[1]], [H[0]])
                tt(k, "dve", yb[:], H[0][:], gg[:], ALU.mult, [H[0], gg], [yb])
                k.dma("sp", YB.ap[cc * 128:(cc + 1) * 128, :], yb[:], R=[yb], W=[YB])
            k.barrier()
        k.es = es0

    def phase_B(l):
        with ExitStack() as es:
            k.es = es
            mb = k.sb([128, 2, 128], F32, "mb")
            nod = k.sb([128, 128], F32, "nod")
            sel = k.sb([4, 4, 128], F32, "sel")
            blk = k.sb([128, 5, 128], F32, "blk")
            k.dma("sp", mb[:], c_mb.ap.rearrange("d j i -> j d i"), R=[c_mb], W=[mb])
            k.dma("sp", nod[:], c_nod.ap, R=[c_nod], W=[nod])
            k.dma("sp", sel[:], c_sel.ap, R=[c_sel], W=[sel])
            k.dma("sp", blk[:], c_blk.ap.rearrange("m j i -> j m i"), R=[c_blk], W=[blk])
            qv = DNQKV.ap.rearrange("w (h f) t -> w f h t", h=4)
            otv = OT.ap.rearrange("d (h f) t -> d f h t", h=4)
            gcv = GCB.ap
            order = {0: list(range(len(BLOCKS))), 1: [0] + list(range(len(BLOCKS) - 1, 0, -1))}
            B4 = [128, 4, 128]
            bc_h = lambda ap2: ap2.unsqueeze(1).broadcast_to(B4)
            bc_i = lambda ap2: ap2.unsqueeze(2).broadcast_to(B4)

            def dn_dir(d):
                def rot(shape, dt, name, n=1):
                    return Rot([k.sb(shape, dt, name + "%d" % d) for _ in range(n)])
                QTr = rot([128, 4, 512], BF16, "QT")
                KTr = rot([128, 4, 512], BF16, "KT")
                VTr = rot([128, 4, 512], BF16, "VTt")
                GBr = rot([4, 2, 512], F32, "GB")
                OBr = rot([128, 4, 512], F32, "OTb", 2)
                S_f = k.sb(B4, F32, "S32_%d" % d)
                S_b = k.sb(B4, BF16, "Sb_%d" % d)
                k.op("pool", lambda g: g.memset(S_f[:], 0.0), W=[S_f])
                k.op("pool", lambda g: g.memset(S_b[:], 0.0), W=[S_b])
                pf = Rot([k.ps(B4, F32, "pfB%d" % d) for _ in range(2)])
                prc = Rot([k.ps(B4, F32, "prB%d" % d) for _ in range(1)])
                pb = Rot([k.ps(B4, BF16, "pbB%d" % d) for _ in range(1)])
                gbc = rot([128, 8], F32, "gbc", 2)
                sc4 = rot([128, 4, 4], F32, "sc4", 3)
                D1 = rot(B4, F32, "D1")
                E = rot(B4, F32, "E", 2)
                nE = rot(B4, F32, "nE")
                EG = rot(B4, F32, "EG")
                Nr = rot(B4, F32, "N")
                L0r = rot(B4, F32, "L0")
                Ndr = rot(B4, F32, "Nd")
                Ldr = rot(B4, F32, "Ld")
                N2r = rot(B4, F32, "N2")
                L2r = rot(B4, F32, "L2")
                L4r = rot(B4, F32, "L4")
                Xr = rot(B4, F32, "X", 2)
                Lbr = rot(B4, F32, "Lb")
                XTr = rot(B4, F32, "XT")
                Yr = rot(B4, F32, "Y")
                Rr = rot(B4, BF16, "R", 2)
                QK = rot(B4, BF16, "QK", 3)
                kbg = rot(B4, BF16, "kbg", 2)
                kout = rot(B4, BF16, "kout", 3)
                vbt = rot(B4, BF16, "vbt", 2)
                wTr = rot(B4, BF16, "wT", 3)
                ur = rot(B4, F32, "u", 3)
                qin = rot(B4, BF16, "qin", 3)
                vnew = rot(B4, BF16, "vnew", 2)
                stmp = rot(B4, F32, "stmp")
                bm_ = lambda i_: bc_h(blk[:, i_, :])

                def mm4(p, lf, rf, R):
                    for h in range(4):
                        mm(k, p[:, h, :], lf(h), rf(h), True, True, R, [p])

                def tr4(p, src, ident, R):
                    for h in range(4):
                        tr(k, p[:, h, :], src(h), ident, R, [p])

                DEPTH = 2
                HQ = {}
                pdone = [0]
                rdone = [0]
                cidx = [0]

                def prep():
                  for step in range(len(BLOCKS)):
                      t0, n = BLOCKS[order[d][step]]
                      QT, KT, VTt, GB = QTr.get(), KTr.get(), VTr.get(), GBr.get()
                      k.dma("sp", QT[:, :, :n], qv[0, :, :, t0:t0 + n], R=[DNQKV], W=[QT])
                      k.dma("sp", KT[:, :, :n], qv[1, :, :, t0:t0 + n], R=[DNQKV], W=[KT])
                      k.dma("sp", VTt[:, :, :n], qv[2, :, :, t0:t0 + n], R=[DNQKV], W=[VTt])
                      k.dma("sp", GB[:, 0, :n], gcv[:, d, t0:t0 + n], R=[GCB], W=[GB])
                      k.dma("sp", GB[:, 1, :n], gcv[:, 2 + d, t0:t0 + n], R=[GCB], W=[GB])
                      yield
                      nch = n // 128
                      chs = list(range(nch)) if d == 0 else list(range(nch - 1, -1, -1))
                      lastc = 127 if d == 0 else 0
                      for ci_, c in enumerate(chs):
                          while cidx[0] - rdone[0] >= DEPTH:
                              yield
                          o = c * 128
                          sl = slice(o, o + 128)
                          p0 = pf.get()
                          tr(k, p0[:, 0, 0:4], GB[:, 0, sl], identF[0:4, 0:4], [GB, identF], [p0])
                          tr(k, p0[:, 0, 4:8], GB[:, 1, sl], identF[0:4, 0:4], [GB, identF], [p0])
                          g8 = gbc.get()
                          cp(k, "dve", g8[:], p0[:, 0, 0:8], [p0], [g8])
                          yield
                          GR = pf.get()
                          mm4(GR, lambda h: sel[:, h, :], lambda h: GB[:, 0, sl], [sel, GB])
                          d1 = D1.get()
                          tt(k, "dve", d1[:], GR[:], bc_i(g8[:, 0:4]), ALU.subtract, [GR, g8], [d1])
                          s4 = sc4.get()
                          act(k, s4[:, 0, :], g8[:, 0:4], AF.Exp, [g8], [s4])
                          yield
                          tt(k, "dve", s4[:, 1, :], s4[:, 0, :], g8[:, 4:8], ALU.mult, [s4, g8], [s4])
                          tt(k, "dve", s4[:, 2, :], GR[:, :, lastc], g8[:, 0:4], ALU.subtract, [GR, g8], [s4])
                          act(k, s4[:, 2, :], s4[:, 2, :], AF.Exp, [s4], [s4])
                          act(k, s4[:, 3, :], GR[:, :, lastc], AF.Exp, [GR], [s4])
                          eg_ = EG.get()
                          act(k, eg_[:], GR[:], AF.Exp, [GR], [eg_])
                          yield
                          tt(k, "pool", d1[:], d1[:], bc_h(mb[:, d, :]), ALU.add, [d1, mb], [d1])
                          e_ = E.get()
                          act(k, e_[:], d1[:], AF.Exp, [d1], [e_])
                          yield
                          ne_ = nE.get()
                          tt(k, "pool", ne_[:], e_[:], bc_h(nod[:]), ALU.mult, [e_, nod], [ne_])
                          BR = pf.get()
                          mm4(BR, lambda h: sel[:, h, :], lambda h: GB[:, 1, sl], [sel, GB])
                          tt(k, "dve", ne_[:], BR[:], ne_[:], ALU.mult, [BR, ne_], [ne_])
                          yield
                          KK = pf.get()
                          mm4(KK, lambda h: KT[:, h, sl], lambda h: KT[:, h, sl], [KT])
                          N_ = Nr.get()
                          tt(k, "dve", N_[:], KK[:], ne_[:], ALU.mult, [KK, ne_], [N_])
                          yield
                          KQ = pf.get()
                          mm4(KQ, lambda h: KT[:, h, sl], lambda h: QT[:, h, sl], [KT, QT])
                          qk = QK.get()
                          tt(k, "dve", qk[:], KQ[:], e_[:], ALU.mult, [KQ, e_], [qk])
                          yield
                          pL = pf.get()
                          tr4(pL, lambda h: N_[:, h, :], identF[:], [N_, identF])
                          L0 = L0r.get()
                          cp(k, "act", L0[:], pL[:], [pL], [L0])
                          Nd = Ndr.get()
                          tt(k, "pool", Nd[:], N_[:], bm_(0), ALU.mult, [N_, blk], [Nd])
                          yield
                          Ld = Ldr.get()
                          tt(k, "pool", Ld[:], L0[:], bm_(0), ALU.mult, [L0, blk], [Ld])
                          yield
                          pA = pf.get()
                          mm4(pA, lambda h: Ld[:, h, :], lambda h: Nd[:, h, :], [Ld, Nd])
                          N2 = N2r.get()
                          cp(k, "act", N2[:], pA[:], [pA], [N2])
                          pA = pf.get()
                          mm4(pA, lambda h: Nd[:, h, :], lambda h: Ld[:, h, :], [Ld, Nd])
                          L2 = L2r.get()
                          cp(k, "dve", L2[:], pA[:], [pA], [L2])
                          X = Xr.get()
                          tt(k, "pool", X[:], Nd[:], bc_h(identF[:]), ALU.add, [Nd, identF], [X])
                          yield
                          pA = pf.get()
                          mm4(pA, lambda h: N2[:, h, :], lambda h: L2[:, h, :], [N2, L2])
                          L4 = L4r.get()
                          cp(k, "act", L4[:], pA[:], [pA], [L4])
                          yield
                          for Lk in (L2, L4):
                              pA = pf.get()
                              mm4(pA, lambda h: Lk[:, h, :], lambda h: X[:, h, :], [Lk, X])
                              X2 = Xr.get()
                              tt(k, "dve", X2[:], pA[:], X[:], ALU.add, [pA, X], [X2])
                              X = X2
                              yield
                          for lev in range(1, 5):
                              Lb = Lbr.get()
                              tt(k, "pool", Lb[:], L0[:], bm_(lev), ALU.mult, [L0, blk], [Lb])
                              pA = pf.get()
                              tr4(pA, lambda h: X[:, h, :], identF[:], [X, identF])
                              XT = XTr.get()
                              cp(k, "act", XT[:], pA[:], [pA], [XT])
                              yield
                              pA = pf.get()
                              mm4(pA, lambda h: Lb[:, h, :], lambda h: X[:, h, :], [Lb, X])
                              Y = Yr.get()
                              cp(k, "dve", Y[:], pA[:], [pA], [Y])
                              yield
                              pA = pf.get()
                              mm4(pA, lambda h: XT[:, h, :], lambda h: Y[:, h, :], [XT, Y])
                              if lev < 4:
                                  X2 = Xr.get()
                                  tt(k, "dve", X2[:], pA[:], X[:], ALU.add, [pA, X], [X2])
                                  X = X2
                              else:
                                  R_ = Rr.get()
                                  tt(k, "dve", R_[:], pA[:], X[:], ALU.add, [pA, X], [R_])
                              yield
                          pK = pb.get()
                          tr4(pK, lambda h: KT[:, h, sl], identB[:], [KT, identB])
                          kb_ = kbg.get()
                          tt(k, "dve", kb_[:], pK[:], bc_i(s4[:, 1, :]), ALU.mult, [pK, s4], [kb_])
                          ko_ = kout.get()
                          tt(k, "dve", ko_[:], pK[:], bc_i(s4[:, 2, :]), ALU.mult, [pK, s4], [ko_])
                          yield
                          pV = pb.get()
                          tr4(pV, lambda h: VTt[:, h, sl], identB[:], [VTt, identB])
                          vb_ = vbt.get()
                          tt(k, "dve", vb_[:], pV[:], bc_i(g8[:, 4:8]), ALU.mult, [pV, g8], [vb_])
                          yield
                          pW = pf.get()
                          mm4(pW, lambda h: kb_[:, h, :], lambda h: R_[:, h, :], [kb_, R_])
                          wT = wTr.get()
                          cp(k, "act", wT[:], pW[:], [pW], [wT])
                          yield
                          pUu = pf.get()
                          mm4(pUu, lambda h: R_[:, h, :], lambda h: vb_[:, h, :], [vb_, R_])
                          u_ = ur.get()
                          cp(k, "dve", u_[:], pUu[:], [pUu], [u_])
                          qi = qin.get()
                          tt(k, "pool", qi[:], QT[:, :, sl], eg_[:], ALU.mult, [QT, eg_], [qi])
                          yield
                          HQ[cidx[0]] = dict(wT=wT, u_=u_, qi=qi, qk=qk, ko_=ko_, s4=s4, t0=t0, n=n, sl=sl, first=(ci_ == 0), last=(ci_ == len(chs) - 1))
                          cidx[0] += 1
                          pdone[0] = cidx[0]
                          yield

                def recur():
                    OTb = None
                    for idx in range(34):
                        while pdone[0] <= idx:
                            yield
                        hq = HQ.pop(idx)
                        wT, u_, qi, qk, ko_, s4, t0, n, sl = hq["wT"], hq["u_"], hq["qi"], hq["qk"], hq["ko_"], hq["s4"], hq["t0"], hq["n"], hq["sl"]
                        if hq["first"]:
                            OTb = OBr.get()
                        pWS = prc.get()
                        mm4(pWS, lambda h: wT[:, h, :], lambda h: S_b[:, h, :], [wT, S_b])
                        vn = vnew.get()
                        tt(k, "dve", vn[:], u_[:], pWS[:], ALU.subtract, [u_, pWS], [vn])
                        yield
                        pO = prc.get()
                        for h in range(4):
                            mm(k, pO[:, h, :], S_b[:, h, :], qi[:, h, :], True, False, [S_b, qi], [pO])
                            mm(k, pO[:, h, :], vn[:, h, :], qk[:, h, :], False, True, [vn, qk], [pO])
                        cp(k, "act", OTb[:, :, sl], pO[:], [pO], [OTb])
                        yield
                        pS = prc.get()
                        mm4(pS, lambda h: ko_[:, h, :], lambda h: vn[:, h, :], [ko_, vn])
                        st_ = stmp.get()
                        tt(k, "pool", st_[:], S_f[:], bc_i(s4[:, 3, :]), ALU.mult, [S_f, s4], [st_])
                        tt(k, "dve", S_f[:], st_[:], pS[:], ALU.add, [st_, pS], [S_f])
                        cp(k, "act", S_b[:], S_f[:], [S_f], [S_b])
                        yield
                        if hq["last"]:
                            k.dma("sp", otv[d, :, :, t0:t0 + n], OTb[:, :, :n], R=[OTb], W=[OT])
                        rdone[0] = idx + 1
                        yield

                return prep(), recur()

            gens = list(dn_dir(0)) + list(dn_dir(1))
            alive = [True] * len(gens)
            while any(alive):
                for gi in range(len(gens)):
                    if alive[gi]:
                        try:
                            next(gens[gi])
                        except StopIteration:
                            alive[gi] = False
            k.barrier()
        k.es = es0

    def phase_D(l):
        lam_init = 0.8 - 0.6 * math.exp(-0.3 * l)
        with ExitStack() as es:
            k.es = es
            krv = KR.ap.rearrange("(h f) t -> f h t", h=4)
            KS = [k.sb([128, 4, T], BF16, "KS%d" % s_) for s_ in range(2)]
            QRa = k.sb([128, 4, T], BF16, "QRa")
            Va = k.sb([128, T // 128, 512], BF16, "Va")
            k.op("pool", lambda g: g.memset(KS[0][64:128], 0.0), W=[KS[0]])
            k.op("pool", lambda g: g.memset(KS[1][0:64], 0.0), W=[KS[1]])
            k.dma("sp", KS[0][0:64], krv[0:64], R=[KR], W=[KS[0]])
            k.dma("sp", KS[1][64:128], krv[64:128], R=[KR], W=[KS[1]])
            k.dma("sp", QRa[:], QR.ap.rearrange("(h f) t -> f h t", h=4), R=[QR], W=[QRa])
            vtv = VT.ap.rearrange("(kt p) v -> p kt v", p=128)
            for g in range(0, T // 128, 6):
                g1 = min(g + 6, T // 128)
                k.dma("sp", Va[:, g:g1, :], vtv[:, g:g1, :], R=[VT], W=[Va])
            dl = k.sb([128, 2, 2, 64], F32, "dl")
            k.dma("sp", dl[:].rearrange("p a b f -> p (a b f)"), da_lam.ap[l:l + 1, :].partition_broadcast(128) if False else da_lam.ap[l].partition_broadcast(128), R=[da_lam], W=[dl])
            pr = k.sb([128, 2, 64], F32, "pr")
            sv = k.sb([128, 4], F32, "sv")
            dnw = k.sb([128, 1], F32, "dnw")
            k.dma("sp", dnw[:], da_normT.ap[l], R=[da_normT], W=[dnw])
            tt(k, "dve", pr[:], dl[:, :, 0, :], dl[:, :, 1, :], ALU.mult, [dl], [pr])
            k.op("dve", lambda g: g.tensor_reduce(out=sv[:, 0:2], in_=pr[:], axis=mybir.AxisListType.X, op=ALU.add), R=[pr], W=[sv])
            act(k, sv[:, 0:2], sv[:, 0:2], AF.Exp, [sv], [sv])
            tt(k, "dve", sv[:, 2:3], sv[:, 1:2], sv[:, 0:1], ALU.subtract, [sv], [sv])
            ts(k, "dve", sv[:, 2:3], sv[:, 2:3], -lam_init, None, ALU.add, None, [sv], [sv])
            ts(k, "dve", dnw[:], dnw[:], 1.0 - lam_init, None, ALU.mult, None, [dnw], [dnw])
            pst = Rot([k.ps([128, 512], F32, "pst") for _ in range(3)])
            poS = [k.ps([128, 512], F32, "po%d" % s_) for s_ in range(2)]
            plS = [k.ps([128, 512], F32, "pl%d" % s_) for s_ in range(2)]
            pn = k.ps([128, 512], F32, "pn")
            ptR = Rot([k.sb([128, 512], BF16, "pt") for _ in range(12)])
            sA = Rot([k.sb([128, 512], F32, "sA") for _ in range(2)])
            sB = Rot([k.sb([128, 512], F32, "sB") for _ in range(2)])
            sC = Rot([k.sb([128, 512], BF16, "sC") for _ in range(4)])
            oS = [Rot([k.sb([128, 512], F32, "oS%d" % s_) for _ in range(2)]) for s_ in range(2)]
            lS = [Rot([k.sb([128, 512], F32, "lS%d" % s_) for _ in range(2)]) for s_ in range(2)]
            accR = Rot([k.sb([128, 512], F32, "accD") for _ in range(2)])
            sqdR = Rot([k.sb([128, 512], BF16, "sqD") for _ in range(2)])
            rnR = Rot([k.sb([128, 512], F32, "rnD") for _ in range(2)])
            deferred = []
            ycR = Rot([k.sb([128, T], BF16, "ycrow") for _ in range(1)])
            LOOK = 2
            for h in range(4):
                yc = ycR.get()
                for (t0, n) in BLOCKS:
                    kts = [0, 1] if t0 < NCTX else list(range(T // 128))
                    items = [(s_, i_, kt) for i_, kt in enumerate(kts) for s_ in range(2)]
                    pend = []

                    grp = [[], []]
                    ngrp = [0, 0]
                    lq = []
                    ngroups = (len(kts) + 3) // 4

                    def close_group(s_):
                        g_ = grp[s_]
                        grp[s_] = []
                        gi_ = ngrp[s_]
                        ngrp[s_] += 1
                        if len(g_) == 1:
                            src = g_[0]
                        else:
                            a1 = sA.get()
                            tt(k, "dve", a1[:, :n], g_[0][:, :n], g_[1][:, :n], ALU.add, [g_[0], g_[1]], [a1])
                            if len(g_) > 2:
                                a2 = sB.get()
                                if len(g_) == 4:
                                    tt(k, "pool", a2[:, :n], g_[2][:, :n], g_[3][:, :n], ALU.add, [g_[2], g_[3]], [a2])
                                    src2 = a2
                                else:
                                    src2 = g_[2]
                                src = sC.get()
                                tt(k, "dve", src[:, :n], a1[:, :n], src2[:, :n], ALU.add, [a1, src2], [src])
                            else:
                                src = sC.get()
                                cp(k, "pool", src[:, :n], a1[:, :n], [a1], [src])

                        def lmm(src=src, gi_=gi_, s_=s_):
                            mm(k, plS[s_][:, :n], onesB[:], src[:, :n], gi_ == 0, gi_ == ngroups - 1, [onesB, src], [plS[s_]])
                        lq.append([6, lmm])

                    def flush_one():
                        s_, i_, kt, pt_ = pend.pop(0)
                        mm(k, poS[s_][:, :n], Va[:, kt, h * 128:(h + 1) * 128], pt_[:, :n], i_ == 0, i_ == len(kts) - 1, [Va, pt_], [poS[s_]])
                        grp[s_].append(pt_)
                        if len(grp[s_]) == 4 or i_ == len(kts) - 1:
                            close_group(s_)
                        for e_ in lq:
                            e_[0] -= 1
                        while lq and lq[0][0] <= 0:
                            lq.pop(0)[1]()
                    for it_, (s_, i_, kt) in enumerate(items):
                        st = pst.get()
                        mm(k, st[:, :n], KS[s_][:, h, kt * 128:(kt + 1) * 128], QRa[:, h, t0:t0 + n], True, True, [KS[s_], QRa], [st])
                        pt_ = ptR.get()
                        act(k, pt_[:, :n], st[:, :n], AF.Exp, [st], [pt_], scale=0.125)
                        pend.append((s_, i_, kt, pt_))
                        if len(pend) > LOOK:
                            flush_one()
                        if it_ == 12:
                            while deferred:
                                deferred.pop(0)()
                    while pend:
                        flush_one()
                    while lq:
                        lq.pop(0)[1]()
                    while deferred:
                        deferred.pop(0)()
                    os_, ls_ = [], []
                    for s_ in range(2):
                        o_ = oS[s_].get()
                        l_ = lS[s_].get()
                        cp(k, "dve", o_[:, :n], poS[s_][:, :n], [poS[s_]], [o_])
                        cp(k, "dve", l_[:, :n], plS[s_][:, :n], [plS[s_]], [l_])
                        os_.append(o_)
                        ls_.append(l_)
                    for s_ in range(2):
                        k.op("dve", lambda g: g.reciprocal(out=ls_[s_][:, :n], in_=ls_[s_][:, :n]), R=[ls_[s_]], W=[ls_[s_]])
                    acc, sqd, rn = accR.get(), sqdR.get(), rnR.get()
                    tt(k, "dve", acc[:, :n], os_[0][:, :n], ls_[0][:, :n], ALU.mult, [os_[0], ls_[0]], [acc])
                    tt(k, "pool", os_[1][:, :n], os_[1][:, :n], ls_[1][:, :n], ALU.mult, [os_[1], ls_[1]], [os_[1]])
                    stt(k, acc[:, :n], os_[1][:, :n], sv[:, 2:3], acc[:, :n], ALU.mult, ALU.add, [os_[1], sv, acc], [acc])
                    k.op("pool", lambda g: g.tensor_tensor(out=sqd[:, :n], in0=acc[:, :n], in1=acc[:, :n], op=ALU.mult), R=[acc], W=[sqd])

                    def part2(acc=acc, sqd=sqd, rn=rn, yc=yc, t0=t0, n=n):
                        mm(k, pn[:, :n], onesB[:], sqd[:, :n], True, True, [onesB, sqd], [pn])
                        act(k, rn[:, :n], pn[:, :n], AF.Sqrt, [pn, cst], [rn], scale=1.0 / 128, bias=EPS5)
                        k.op("dve", lambda g: g.reciprocal(out=rn[:, :n], in_=rn[:, :n]), R=[rn], W=[rn])
                        stt(k, yc[:, t0:t0 + n], acc[:, :n], dnw[:, 0:1], rn[:, :n], ALU.mult, ALU.mult, [acc, dnw, rn], [yc])
                    deferred.append(part2)
                while deferred:
                    deferred.pop(0)()
                k.dma("sp", YC.ap[h * 128:(h + 1) * 128, :], yc[:], R=[yc], W=[YC])
            k.barrier()
        k.es = es0

    def phase_E(l, last):
        with ExitStack() as es:
            k.es = es
            Wbr = k.sb([128, 3, 4, D], BF16, "Wbr")
            Wo = k.sb([128, 8, D], BF16, "Wo")
            for nb in range(3):
                k.dma("pool", Wbr[:, nb], w_branch.ap[l, nb].rearrange("(k p) c -> p k c", p=128), R=[w_branch], W=[Wbr])
            for h2 in range(2):
                k.dma("pool", Wo[:, h2 * 4:h2 * 4 + 4, :], w_out.ap[l].rearrange("(k p) c -> p k c", p=128)[:, h2 * 4:h2 * 4 + 4, :], R=[w_out], W=[Wo])
            dnw = k.sb([128, 1], F32, "dnwE")
            k.dma("sp", dnw[:], dn_normT.ap[l], R=[dn_normT], W=[dnw])
            WguR = Rot([k.sb([128, 8, 256], BF16, "Wgu") for _ in range(4)])
            WdR = Rot([k.sb([128, 22, 128], BF16, "Wd") for _ in range(2)])
            tmp = k.sb([128, 8, 512], F32, "tmpE")
            sq = k.sb([128, 8, 512], BF16, "sqE")
            zs = k.sb([128, 4, 512], BF16, "zsE")
            ya = k.sb([128, 4, 512], BF16, "yaE")
            yb = k.sb([128, 4, 512], BF16, "ybE")
            yc = k.sb([128, 4, 512], BF16, "ycE")
            sgR = Rot([k.sb([128, 3, 512], BF16, "sgE") for _ in range(2)])
            merged = k.sb([128, 8, 512], BF16, "mergedE")
            HB = [k.sb([128, 8, 512], F32, "hbE%d" % i) for i in range(2)]
            U2 = [k.sb([128, 8, 512], BF16, "u2E%d" % i) for i in range(2)]
            hid = k.sb([128, 22, 512], BF16, "hidE")
            rs = k.sb([128, 512], F32, "rsE")
            rs2 = k.sb([128, 512], F32, "rs2E")
            macc = k.sb([128, 512], F32, "maccE")
            mt = k.sb([128, 512], F32, "mtE")
            sgl = Rot([k.sb([128, 512], F32, "sglE") for _ in range(2)])
            otbufs = [Buf(hid.ap[:, 8 + 4 * i:12 + 4 * i, :].bitcast(F32).rearrange("p a b -> p (a b)"), "otE%d" % i) for i in range(2)]
            otile = Rot(otbufs)
            pg1 = Rot([k.ps([128, 512], F32, "pgE1") for _ in range(3)])
            pg2 = Rot([k.ps([128, 512], F32, "pgE2") for _ in range(3)])
            pss = k.ps([128, 512], F32, "pssE")
            ppT = k.ps([128, 4, 128], F32, "ppTE")
            ytv = lambda Y: Y.ap.rearrange("(k p) t -> p k t", p=128)
            otv = OT.ap.rearrange("d (h f) t -> d f h t", h=4)
            sgv = SG.ap.rearrange("(nb dc p) t -> p nb dc t", nb=3, dc=8)
            wgv = w_g.ap[l].rearrange("(k p) c -> p k c", p=128)
            wuv = w_u.ap[l].rearrange("(k p) c -> p k c", p=128)
            wdv = w_d.ap[l].rearrange("(fc p) c -> p fc c", p=128)
            blocks = [bk for bk in BLOCKS if not (last and bk[0] < NCTX)]

            def stage1(bi):
                t0, n = blocks[bi]
                j = 0 if t0 < NCTX else 1
                hb, u2 = HB[bi % 2], U2[bi % 2]
                k.dma("sp", tmp[:, 0:4, :n], otv[0, :, :, t0:t0 + n], R=[OT], W=[tmp])
                k.dma("sp", tmp[:, 4:8, :n], otv[1, :, :, t0:t0 + n], R=[OT], W=[tmp])
                k.dma("sp", zs[:, :, :n], ytv(ZS)[:, :, t0:t0 + n], R=[ZS], W=[zs])
                k.dma("sp", yb[:, :, :n], ytv(YB)[:, :, t0:t0 + n], R=[YB], W=[yb])
                k.dma("sp", yc[:, :, :n], ytv(YC)[:, :, t0:t0 + n], R=[YC], W=[yc])
                k.dma("sp", hb[:, :, :n], hTv[:, :, t0:t0 + n], R=[hT], W=[hb])
                yield
                tt(k, "pool", tmp[:, 0:4, :n], tmp[:, 0:4, :n], tmp[:, 4:8, :n], ALU.add, [tmp], [tmp])
                act(k, sq[:, 0:4, :n], tmp[:, 0:4, :n], AF.Square, [tmp], [sq])
                yield
                for h in range(4):
                    mm(k, pss[:, :n], onesB[:], sq[:, h, :n], True, True, [onesB, sq], [pss])
                    act(k, rs[:, :n], pss[:, :n], AF.Sqrt, [pss, cst], [rs], scale=1.0 / 128, bias=EPS6)
                    k.op("dve", lambda g: g.reciprocal(out=rs[:, :n], in_=rs[:, :n]), R=[rs], W=[rs])
                    tt(k, "dve", tmp[:, 4 + h, :n], tmp[:, h, :n], rs[:, :n], ALU.mult, [tmp, rs], [tmp])
                    stt(k, ya[:, h, :n], tmp[:, 4 + h, :n], dnw[:, 0:1], zs[:, h, :n], ALU.mult, ALU.mult, [tmp, dnw, zs], [ya])
                    yield
                for _ in range(4):
                    yield
                ys = [ya, yb, yc]
                for dc in range(8):
                    sg = sgR.get()
                    k.dma("sp", sg[:, :, :n], sgv[:, :, dc, t0:t0 + n], R=[SG], W=[sg])
                    for nb in range(3):
                        pu = pg1.get()
                        for k4 in range(4):
                            mm(k, pu[:, :n], Wbr[:, nb, k4, dc * 128:(dc + 1) * 128], ys[nb][:, k4, :n], k4 == 0, k4 == 3, [Wbr, ys[nb]], [pu])
                        if nb == 0:
                            tt(k, "dve", macc[:, :n], pu[:, :n], sg[:, 0, :n], ALU.mult, [pu, sg], [macc])
                        else:
                            tt(k, "dve", mt[:, :n], pu[:, :n], sg[:, nb, :n], ALU.mult, [pu, sg], [mt])
                            if nb == 1:
                                tt(k, "pool", macc[:, :n], macc[:, :n], mt[:, :n], ALU.add, [macc, mt], [macc])
                            else:
                                tt(k, "pool", merged[:, dc, :n], macc[:, :n], mt[:, :n], ALU.add, [macc, mt], [merged])
                        yield
                for dc in range(8):
                    py = pg1.get()
                    for k8 in range(8):
                        mm(k, py[:, :n], Wo[:, k8, dc * 128:(dc + 1) * 128], merged[:, k8, :n], k8 == 0, k8 == 7, [Wo, merged], [py])
                    stt(k, hb[:, dc, :n], py[:, :n], AB[:, l, 2, dc, j:j + 1], hb[:, dc, :n], ALU.mult, ALU.add, [py, AB, hb], [hb])
                    yield
                act(k, sq[:, :, :n], hb[:, :, :n], AF.Square, [hb], [sq])
                yield
                for kk in range(8):
                    mm(k, pss[:, :n], onesB[:], sq[:, kk, :n], kk == 0, kk == 7, [onesB, sq], [pss])
                act(k, rs[:, :n], pss[:, :n], AF.Sqrt, [pss, cst], [rs], scale=1.0 / D, bias=EPS6)
                k.op("dve", lambda g: g.reciprocal(out=rs[:, :n], in_=rs[:, :n]), R=[rs], W=[rs])
                yield
                for kk in range(8):
                    tt(k, "dve", tmp[:, kk, :n], hb[:, kk, :n], rs[:, :n], ALU.mult, [hb, rs], [tmp])
                    act(k, u2[:, kk, :n], tmp[:, kk, :n], AF.Identity, [tmp, AB], [u2],
                        scale=AB[:, l, 3, kk, j:j + 1], bias=AB[:, l, 4, kk, j:j + 1])
                    if kk % 2 == 1:
                        yield

            def stage2(bi):
                t0, n = blocks[bi]
                j = 0 if t0 < NCTX else 1
                hb, u2 = HB[bi % 2], U2[bi % 2]
                for f2 in range(11):
                    wg_ = WguR.get()
                    k.dma("sp", wg_[:], WGT.ap[f2], R=[WGT], W=[wg_])
                    wu_ = WguR.get()
                    k.dma("sp", wu_[:], WUT.ap[f2], R=[WUT], W=[wu_])
                    for c2 in range(2):
                        fc = f2 * 2 + c2
                        pg_ = pg2.get()
                        for kk in range(8):
                            mm(k, pg_[:, :n], wg_[:, kk, c2 * 128:(c2 + 1) * 128], u2[:, kk, :n], kk == 0, kk == 7, [wg_, u2], [pg_])
                        pu_ = pg2.get()
                        for kk in range(8):
                            mm(k, pu_[:, :n], wu_[:, kk, c2 * 128:(c2 + 1) * 128], u2[:, kk, :n], kk == 0, kk == 7, [wu_, u2], [pu_])
                        sg_ = sgl.get()
                        act(k, sg_[:, :n], pg_[:, :n], AF.Silu, [pg_], [sg_])
                        tt(k, "dve", hid[:, fc, :n], sg_[:, :n], pu_[:, :n], ALU.mult, [sg_, pu_], [hid] + (otbufs if (last and 8 <= fc < 16) else []))
                        yield
                for dc in range(8):
                    wd_ = WdR.get()
                    k.dma("sp", wd_[:], WDT.ap[dc], R=[WDT], W=[wd_])
                    py = pg2.get()
                    for fc in range(22):
                        mm(k, py[:, :n], wd_[:, fc, :], hid[:, fc, :n], fc == 0, fc == 21, [wd_, hid], [py])
                    stt(k, hb[:, dc, :n], py[:, :n], AB[:, l, 5, dc, j:j + 1], hb[:, dc, :n], ALU.mult, ALU.add, [py, AB, hb], [hb])
                    yield
                if not last:
                    k.dma("sp", hTv[:, :, t0:t0 + n], hb[:, :, :n], R=[hb], W=[hT])
                    yield
                else:
                    sq2 = hid
                    act(k, sq2[:, 0:8, :n], hb[:, :, :n], AF.Square, [hb], [sq2])
                    for kk in range(8):
                        mm(k, ppT[:].rearrange("p a b -> p (a b)")[:, :n], onesB[:], sq2[:, kk, :n], kk == 0, kk == 7, [onesB, sq2], [ppT])
                    act(k, rs2[:, :n], ppT[:].rearrange("p a b -> p (a b)")[:, :n], AF.Sqrt, [ppT, cst], [rs2], scale=1.0 / D, bias=EPS6)
                    k.op("dve", lambda g: g.reciprocal(out=rs2[:, :n], in_=rs2[:, :n]), R=[rs2], W=[rs2])
                    yield
                    for kk in range(8):
                        stt(k, hb[:, kk, :n], hb[:, kk, :n], nrmF[:, kk:kk + 1], rs2[:, :n], ALU.mult, ALU.mult, [hb, nrmF, rs2], [hb])
                    yield
                    for q in range(n // 128):
                        ot = otile.get()
                        for half in range(2):
                            for qq in range(4):
                                kk = half * 4 + qq
                                tr(k, ppT[:, qq, :], hb[:, kk, q * 128:(q + 1) * 128], identF[:], [hb, identF], [ppT])
                            cp(k, k.alt(), ot[:, half * 512:(half + 1) * 512], ppT[:].rearrange("p a b -> p (a b)"), [ppT], [ot, hid])
                        r0 = t0 - NCTX + q * 128
                        k.dma("sp", out_d.ap[r0:r0 + 128, :], ot[:], R=[ot], W=[out_d])
                        yield

            def run_pair(g1, g2):
                gens = [g for g in (g1, g2) if g is not None]
                alive = [True] * len(gens)
                while any(alive):
                    for gi in range(len(gens)):
                        if alive[gi]:
                            try:
                                next(gens[gi])
                            except StopIteration:
                                alive[gi] = False
            nbk = len(blocks)
            run_pair(stage1(0), None)
            for bi in range(nbk):
                run_pair(stage2(bi), stage1(bi + 1) if bi + 1 < nbk else None)
            k.barrier()
        k.es = es0

    done = False
    for l in range(DEPTH):
        last = (l == DEPTH - 1)
        for name, fn in (("A", lambda: phase_A(l)), ("C", lambda: phase_C(l)), ("B", lambda: phase_B(l)),
                         ("D", lambda: phase_D(l)), ("E", lambda: phase_E(l, last))):
            fn()
            if stop_after == "%s%d" % (name, l):
                done = True
                break
        if done:
            break
    k.barrier()
    es0.close()
    return nc, k


def _consts():
    c = {}
    c["c_ident"] = np.eye(128, dtype=np.float32)
    inv = (10000.0 ** (-np.arange(16, dtype=np.float32) / 16)).astype(np.float32)
    lt = np.arange(NLAT)
    row = (lt // 64).astype(np.float32)
    col = (lt % 64).astype(np.float32)
    cos = np.ones((128, T), np.float32)
    sin = np.zeros((128, T), np.float32)
    perm = np.zeros((128, 128), np.float32)
    for p in range(128):
        e = p % 64
        axis = e // 32
        half = (e % 32) // 16
        f = e % 16
        pos = row if axis == 0 else col
        ang = (pos * inv[f]).astype(np.float32)
        cos[p, NCTX:] = np.cos(ang)
        sn = np.sin(ang)
        sin[p, NCTX:] = -sn if half == 0 else sn
        partner = p + 16 if half == 0 else p - 16
        perm[partner, p] = 1.0
    c["c_rope"] = np.stack([cos, sin]).astype(np.float32)
    c["c_perm"] = perm
    jj, ii = np.meshgrid(np.arange(128), np.arange(128), indexing="ij")
    mb = np.stack([np.where(jj <= ii, 0.0, NEG), np.where(jj >= ii, 0.0, NEG)]).astype(np.float32)
    c["c_mb"] = mb
    c["c_nod"] = (-(1.0 - np.eye(128))).astype(np.float32)
    m01 = np.ones((4, 2, 512), np.float32)
    m01[:, 0, 0::128] = 0.0
    m01[:, 1, 127::128] = 0.0
    c["c_m01"] = m01
    sel = np.zeros((4, 4, 128), np.float32)
    for h in range(4):
        sel[h, h, :] = 1.0
    c["c_sel"] = sel
    bl = np.zeros((5, 128, 128), np.float32)
    bl[0] = (jj // 8 == ii // 8)
    for m_, b_ in enumerate((8, 16, 32, 64)):
        bl[m_ + 1] = (jj // (2 * b_) == ii // (2 * b_)) & (jj // b_ != ii // b_)
    c["c_blk"] = bl
    return c


def _prep_shared(inp):
    f = lambda a: np.ascontiguousarray(a, dtype=np.float32)
    s = {}
    s["w_mod"] = f(inp["w_mod"])
    s["b_modT"] = f(inp["b_mod"].reshape(DEPTH, 48, 128).transpose(0, 2, 1))
    nm = np.stack([inp["norm_mix"], inp["norm_ffn"]], 1)
    s["normsT"] = f(nm.reshape(DEPTH, 2, 8, 128).transpose(3, 0, 1, 2))
    s["normfT"] = f(inp["norm_final"].reshape(8, 128).T)
    s["w_in"] = f(inp["w_in"])
    s["dn_convT"] = f(inp["dn_conv"].reshape(DEPTH, 4, 12, 128).transpose(0, 3, 2, 1))
    s["dnab"] = f(np.concatenate([inp["dn_a_log"].transpose(0, 2, 1), inp["dn_dt_bias"].transpose(0, 2, 1)], -1))
    s["dn_normT"] = f(inp["dn_norm"].reshape(DEPTH, 128, 1))
    s["lru_cw"] = f(inp["lru_conv_w"].reshape(DEPTH, 4, 4, 128).transpose(0, 3, 2, 1))
    s["lru_cb"] = f(inp["lru_conv_b"].reshape(DEPTH, 4, 128).transpose(0, 2, 1))
    lv = np.stack([inp["lru_ba"], inp["lru_bi"], inp["lru_lambda"]], 1)
    s["lru_vec"] = f(lv.reshape(DEPTH, 3, 2, 4, 128).transpose(0, 4, 1, 2, 3))
    wb = np.zeros((DEPTH, 2, 2, 4, 128, 128), np.float32)
    for ai, w in enumerate([inp["lru_wa"], inp["lru_wi"]]):
        for cc in range(4):
            for g2 in range(2):
                wb[:, ai, :, cc, g2 * 64:(g2 + 1) * 64, g2 * 64:(g2 + 1) * 64] = w[:, :, cc * 2 + g2]
    s["lru_wblk"] = wb
    s["da_lam"] = f(inp["da_lambda"].reshape(DEPTH, 256))
    s["da_normT"] = f(inp["da_norm"].reshape(DEPTH, 128, 1))
    s["w_branch"] = f(inp["w_branch"])
    s["w_out"] = f(inp["w_out"])
    s["w_ffn_gate"] = f(inp["w_ffn_gate"])
    s["w_ffn_up"] = f(inp["w_ffn_up"])
    s["w_ffn_down"] = f(inp["w_ffn_down"])
    s.update(_consts())
    return s


def _prep_core(inp, b):
    m = {}
    m["xin"] = np.ascontiguousarray(np.concatenate([inp["ctx"][b], inp["x"][b]], 0), dtype=np.float32)
    cv = np.stack([inp["c_ctx"], inp["c"][b]], 0)
    m["cT"] = np.ascontiguousarray(cv.reshape(2, 8, 128).transpose(2, 1, 0), dtype=np.float32)
    return m


_CACHE = {}


def kernel(**inputs):
    inp = {k_: np.asarray(v) for k_, v in inputs.items()}
    if "nc" not in _CACHE:
        _CACHE["nc"] = build()[0]
    nc = _CACHE["nc"]
    shared = _prep_shared(inp)
    in_maps = []
    for b in range(8):
        m = dict(shared)
        m.update(_prep_core(inp, b))
        in_maps.append(m)
    res = run_bass_kernel_spmd(nc, in_maps, core_ids=list(range(8)))
    return np.stack([np.asarray(r["out"], dtype=np.float32) for r in res.results], 0)
```

```python
import math
import numpy as np
from contextlib import ExitStack
import concourse.bass as bass
import concourse.mybir as mybir
from concourse.bass_utils import run_bass_kernel_spmd
from concourse.alu_op_type import AluOpType as ALU

AF = mybir.ActivationFunctionType
F32 = mybir.dt.float32
BF16 = mybir.dt.bfloat16

D = 1024
NCTX = 256
NLAT = 4096
T = NCTX + NLAT
DEPTH = 2
DFF = 2816
INC = 7696
BLOCKS = [(0, 256)] + [(256 + 512 * j, 512) for j in range(8)]
NEG = -30000.0


class Buf:
    def __init__(self, ap, name, share=None):
        self.ap = ap
        self.name = name
        self.st = share.st if share is not None else [None, []]

    @property
    def w(self):
        return self.st[0]

    @w.setter
    def w(self, v):
        self.st[0] = v

    @property
    def r(self):
        return self.st[1]

    @r.setter
    def r(self, v):
        self.st[1] = v

    def __getitem__(self, k):
        return self.ap[k]


class K:
    NDMA = 8

    def __init__(self, nc, es):
        self.nc = nc
        self.es = es
        self.eng = {"pe": nc.tensor, "act": nc.scalar, "dve": nc.vector,
                    "pool": nc.gpsimd, "sp": nc.sync}
        self.sem = {}
        self.cnt = {}
        for e in self.eng:
            self.sem[e] = es.enter_context(nc.semaphore("s_" + e))
            self.cnt[e] = 0
        self.dring = {}
        for q in ("sp", "pool"):
            ring = []
            for i in range(self.NDMA):
                key = "d_%s%d" % (q, i)
                self.sem[key] = es.enter_context(nc.semaphore(key))
                self.cnt[key] = 0
                ring.append(key)
            self.dring[q] = ring
        self.dpos = {q: 0 for q in self.dring}
        self.seen = {e: {} for e in self.eng}
        self.nbuf = 0
        self.ninst = 0
        self.rr = 0

    def sb(self, shape, dt, name=None):
        self.nbuf += 1
        name = (name or "sb") + "_%d" % self.nbuf
        t = self.es.enter_context(self.nc.sbuf_tensor(name, list(shape), dt))
        return Buf(t, name)

    def ps(self, shape, dt, name=None):
        self.nbuf += 1
        name = (name or "ps") + "_%d" % self.nbuf
        t = self.es.enter_context(self.nc.psum_tensor(name, list(shape), dt))
        return Buf(t, name)

    def _deps(self, R, W):
        d = {}

        def add(ev):
            if ev is None:
                return
            k, v = ev
            if d.get(k, 0) < v:
                d[k] = v
        for b in R:
            add(b.w)
        for b in W:
            add(b.w)
            for ev in b.r:
                add(ev)
        return d

    def _wait(self, e, d):
        eng = self.eng[e]
        seen = self.seen[e]
        for k, v in d.items():
            if e == "pe" and k == "pe":
                continue
            if seen.get(k, 0) >= v:
                continue
            eng.wait_ge(self.sem[k], v)
            seen[k] = v

    def _mark(self, ev, R, W):
        for b in R:
            b.r = [x for x in b.r if x[0] != ev[0]] + [ev]
        for b in W:
            b.w = ev
            b.r = []

    def op(self, e, fn, R=(), W=()):
        d = self._deps(R, W)
        self._wait(e, d)
        inst = fn(self.eng[e])
        self.cnt[e] += 1
        inst.then_inc(self.sem[e], 1)
        self._mark((e, self.cnt[e]), R, W)
        self.ninst += 1

    def dma(self, q, out, in_, R=(), W=(), **kw):
        ring = self.dring[q]
        key = ring[self.dpos[q] % self.NDMA]
        self.dpos[q] += 1
        d = self._deps(R, W)
        if self.cnt[key] > 0 and d.get(key, 0) < self.cnt[key]:
            d[key] = self.cnt[key]
        self._wait(q, d)
        inst = self.eng[q].dma_start(out=out, in_=in_, **kw)
        self.cnt[key] += 16
        inst.then_inc(self.sem[key], 16)
        self._mark((key, self.cnt[key]), R, W)
        self.ninst += 1

    def barrier(self):
        d = {k: v for k, v in self.cnt.items() if v > 0}
        for e in self.eng:
            self._wait(e, dict(d))

    def alt(self):
        self.rr += 1
        return "act" if self.rr % 2 else "dve"


def tt(k, e, out, a, b, op, R, W):
    k.op(e, lambda g: g.tensor_tensor(out=out, in0=a, in1=b, op=op), R=R, W=W)


def ts(k, e, out, a, s1, s2, op0, op1, R, W):
    if op1 is None:
        k.op(e, lambda g: g.tensor_scalar(out=out, in0=a, scalar1=s1, scalar2=None, op0=op0), R=R, W=W)
    else:
        k.op(e, lambda g: g.tensor_scalar(out=out, in0=a, scalar1=s1, scalar2=s2, op0=op0, op1=op1), R=R, W=W)


def stt(k, out, a, s, b, op0, op1, R, W):
    k.op("dve", lambda g: g.scalar_tensor_tensor(out=out, in0=a, scalar=s, in1=b, op0=op0, op1=op1), R=R, W=W)


def act(k, out, a, func, R, W, scale=None, bias=None):
    kw = {}
    if scale is not None:
        kw["scale"] = scale
    if bias is not None:
        kw["bias"] = bias
    k.op("act", lambda g: g.activation(out=out, in_=a, func=func, **kw), R=R, W=W)


def cp(k, e, out, a, R, W):
    if e == "act":
        k.op("act", lambda g: g.activation(out=out, in_=a, func=AF.Copy), R=R, W=W)
    else:
        k.op(e, lambda g: g.tensor_copy(out=out, in_=a), R=R, W=W)


def mm(k, out, lhsT, rhs, start, stop, R, W):
    k.op("pe", lambda g: g.matmul(out, lhsT=lhsT, rhs=rhs, start=start, stop=stop), R=R, W=W)


def tr(k, out, in_, ident, R, W):
    k.op("pe", lambda g: g.transpose(out=out, in_=in_, identity=ident), R=R, W=W)


class Rot:
    def __init__(self, bufs):
        self.bufs = bufs
        self.i = 0

    def get(self):
        b = self.bufs[self.i % len(self.bufs)]
        self.i += 1
        return b


def build(stop_after=None, debug=False):
    nc = bass.Bass("TRN2", target_bir_lowering=False)
    SK = "ExternalOutput" if debug else "Internal"

    def din(name, shape, dt=F32):
        return Buf(nc.dram_tensor(name, list(shape), dt, kind="ExternalInput").ap(), name)

    def dsc(name, shape, dt):
        return Buf(nc.dram_tensor(name, list(shape), dt, kind=SK).ap(), name)

    xin = din("xin", [T, D])
    cT_d = din("cT", [128, 8, 2])
    w_mod = din("w_mod", [DEPTH, D, 6 * D])
    b_modT = din("b_modT", [DEPTH, 128, 48])
    normsT = din("normsT", [128, DEPTH, 2, 8])
    normfT = din("normfT", [128, 8])
    w_in = din("w_in", [DEPTH, D, INC])
    dn_convT = din("dn_convT", [DEPTH, 128, 12, 4])
    dnab = din("dnab", [DEPTH, 4, 4])
    dn_normT = din("dn_normT", [DEPTH, 128, 1])
    lru_cw = din("lru_cw", [DEPTH, 128, 4, 4])
    lru_cb = din("lru_cb", [DEPTH, 128, 4])
    lru_vec = din("lru_vec", [DEPTH, 128, 3, 2, 4])
    lru_wblk = din("lru_wblk", [DEPTH, 2, 2, 4, 128, 128])
    da_lam = din("da_lam", [DEPTH, 256])
    da_normT = din("da_normT", [DEPTH, 128, 1])
    w_branch = din("w_branch", [DEPTH, 3, 512, D])
    w_out = din("w_out", [DEPTH, D, D])
    w_g = din("w_ffn_gate", [DEPTH, D, DFF])
    w_u = din("w_ffn_up", [DEPTH, D, DFF])
    w_d = din("w_ffn_down", [DEPTH, DFF, D])
    c_ident = din("c_ident", [128, 128])
    c_rope = din("c_rope", [2, 128, T])
    c_perm = din("c_perm", [128, 128])
    c_mb = din("c_mb", [2, 128, 128])
    c_nod = din("c_nod", [128, 128])
    c_m01 = din("c_m01", [4, 2, 512])
    c_sel = din("c_sel", [4, 4, 128])
    c_blk = din("c_blk", [5, 128, 128])
    out_d = Buf(nc.dram_tensor("out", [NLAT, D], F32, kind="ExternalOutput").ap(), "out")

    hT = dsc("hT", [D, T], F32)
    DNQKV = dsc("DNQKV", [3, 512, T], BF16)
    ZS = dsc("ZS", [512, T], BF16)
    GCB = dsc("GCB", [4, 4, T], F32)
    LX = dsc("LX", [512, T], F32)
    LG = dsc("LG", [512, T], BF16)
    QR = dsc("QR", [512, T], BF16)
    KR = dsc("KR", [512, T], BF16)
    VT = dsc("VT", [T, 512], BF16)
    SG = dsc("SG", [3072, T], BF16)
    OT = dsc("OT", [2, 512, T], F32)
    YB = dsc("YB", [512, T], BF16)
    YC = dsc("YC", [512, T], BF16)
    hTv = hT.ap.rearrange("(k p) t -> p k t", p=128)
    WGT = dsc("WGT", [11, 128, 8, 256], BF16)
    WUT = dsc("WUT", [11, 128, 8, 256], BF16)
    WDT = dsc("WDT", [8, 128, 22, 128], BF16)

    def precast_ffn(l):
        wgv = w_g.ap[l].rearrange("(k p) c -> p k c", p=128)
        wuv = w_u.ap[l].rearrange("(k p) c -> p k c", p=128)
        wdv = w_d.ap[l].rearrange("(fc p) c -> p fc c", p=128)
        for f2 in range(11):
            k.dma("pool", WGT.ap[f2], wgv[:, :, f2 * 256:(f2 + 1) * 256], R=[w_g], W=[WGT])
            k.dma("pool", WUT.ap[f2], wuv[:, :, f2 * 256:(f2 + 1) * 256], R=[w_u], W=[WUT])
        for dc in range(8):
            k.dma("pool", WDT.ap[dc], wdv[:, :, dc * 128:(dc + 1) * 128], R=[w_d], W=[WDT])

    es0 = ExitStack()
    k = K(nc, es0)
    identF = k.sb([128, 128], F32, "identF")
    identB = k.sb([128, 128], BF16, "identB")
    onesB = k.sb([128, 128], BF16, "onesB")
    modT = k.sb([128, DEPTH, 48, 2], F32, "modT")
    AB = k.sb([128, DEPTH, 6, 8, 2], F32, "AB")
    nrmT = k.sb([128, DEPTH, 2, 8], F32, "nrmT")
    nrmF = k.sb([128, 8], F32, "nrmF")
    cst = k.sb([128, 4], F32, "cst")
    k.dma("sp", identF[:], c_ident.ap, R=[c_ident], W=[identF])
    k.dma("sp", nrmT[:], normsT.ap, R=[normsT], W=[nrmT])
    k.dma("sp", nrmF[:], normfT.ap, R=[normfT], W=[nrmF])
    cp(k, "dve", identB[:], identF[:], [identF], [identB])
    k.op("dve", lambda g: g.memset(onesB[:], 1.0), W=[onesB])
    k.op("dve", lambda g: g.memset(cst[:, 0:1], 1e-6), W=[cst])
    k.op("dve", lambda g: g.memset(cst[:, 1:2], 1e-5), W=[cst])
    k.op("dve", lambda g: g.memset(cst[:, 2:3], 1.0), W=[cst])
    k.op("dve", lambda g: g.memset(cst[:, 3:4], 0.0), W=[cst])
    EPS6, EPS5, ONE = cst[:, 0:1], cst[:, 1:2], cst[:, 2:3]

    def norm_block(l, which, hb, n, j, sq, pss, rs, tmp, ubuf, uout):
        act(k, sq[:, :, :n], hb[:, :, :n], AF.Square, [hb], [sq])
        for kk in range(8):
            mm(k, pss[:, :n], onesB[:], sq[:, kk, :n], kk == 0, kk == 7, [onesB, sq], [pss])
        act(k, rs[:, :n], pss[:, :n], AF.Sqrt, [pss, cst], [rs], scale=1.0 / D, bias=EPS6)
        k.op("dve", lambda g: g.reciprocal(out=rs[:, :n], in_=rs[:, :n]), R=[rs], W=[rs])
        for kk in range(8):
            tt(k, "dve", tmp[:, kk, :n], hb[:, kk, :n], rs[:, :n], ALU.mult, [hb, rs], [tmp])
            act(k, uout(kk), tmp[:, kk, :n], AF.Identity, [tmp, AB], [ubuf],
                scale=AB[:, l, 3 * which + 0, kk, j:j + 1], bias=AB[:, l, 3 * which + 1, kk, j:j + 1])

    with ExitStack() as es:
        k.es = es
        cTs = k.sb([128, 8, 2], F32, "cTs")
        sTs = k.sb([128, 8, 2], F32, "sTs")
        bm = k.sb([128, DEPTH, 48], F32, "bm")
        k.dma("sp", cTs[:], cT_d.ap, R=[cT_d], W=[cTs])
        for l in range(DEPTH):
            k.dma("sp", bm[:, l, :], b_modT.ap[l], R=[b_modT], W=[bm])
        act(k, sTs[:], cTs[:], AF.Silu, [cTs], [sTs])
        wm = Rot([k.sb([128, 8, 512], F32, "wm") for _ in range(2)])
        pmod = k.ps([128, 48, 2], F32, "pmod")
        for l in range(DEPTH):
            for g in range(12):
                wt = wm.get()
                k.dma("sp", wt[:], w_mod.ap[l].rearrange("(k p) c -> p k c", p=128)[:, :, g * 512:(g + 1) * 512], R=[w_mod], W=[wt])
                for c4 in range(4):
                    ch = g * 4 + c4
                    for kk in range(8):
                        mm(k, pmod[:, ch, :], wt[:, kk, c4 * 128:(c4 + 1) * 128], sTs[:, kk, :], kk == 0, kk == 7, [wt, sTs], [pmod])
            for j in range(2):
                tt(k, "dve", modT[:, l, :, j], pmod[:, :, j], bm[:, l, :], ALU.add, [pmod, bm], [modT])
            for s, (ish, isc, ig) in enumerate([(0, 1, 2), (3, 4, 5)]):
                for j in range(2):
                    stt(k, AB[:, l, 3 * s + 0, :, j], modT[:, l, isc * 8:(isc + 1) * 8, j], 1.0, nrmT[:, l, s, :], ALU.add, ALU.mult, [modT, nrmT], [AB])
                    cp(k, "dve", AB[:, l, 3 * s + 1, :, j], modT[:, l, ish * 8:(ish + 1) * 8, j], [modT], [AB])
                    cp(k, "dve", AB[:, l, 3 * s + 2, :, j], modT[:, l, ig * 8:(ig + 1) * 8, j], [modT], [AB])
        k.barrier()
    k.es = es0

    with ExitStack() as es:
        k.es = es
        xt = Rot([k.sb([128, D], F32, "xt") for _ in range(2)])
        ht = Rot([k.sb([128, 8, 128], F32, "ht") for _ in range(2)])
        pp = Rot([k.ps([128, 4, 128], F32, "ppT") for _ in range(4)])
        for t in range(T // 128):
            x_ = xt.get()
            k.dma("sp", x_[:], xin.ap[t * 128:(t + 1) * 128, :], R=[xin], W=[x_])
            h_ = ht.get()
            for half in range(2):
                p_ = pp.get()
                for q in range(4):
                    kk = half * 4 + q
                    tr(k, p_[:, q, :], x_[:, kk * 128:(kk + 1) * 128], identF[:], [x_, identF], [p_])
                cp(k, k.alt(), h_[:, half * 4:half * 4 + 4, :], p_[:], [p_], [h_])
            k.dma("sp", hTv[:, :, t * 128:(t + 1) * 128], h_[:], R=[h_], W=[hT])
        k.barrier()
    k.es = es0

    def off(t):
        return t + 1 if t < NCTX else t + 4

    def phase_A(l):
        with ExitStack() as esA:
            k.es = esA
            UT = k.sb([128, 8, T], BF16, "UT")
            with ExitStack() as es1:
                k.es = es1
                hbR = Rot([k.sb([128, 8, 512], F32, "hbA") for _ in range(2)])
                sq = k.sb([128, 8, 512], BF16, "sqA")
                tmp = k.sb([128, 8, 512], F32, "tmpA")
                rs = k.sb([128, 512], F32, "rsA")
                pss = k.ps([128, 512], F32, "pssA")
                for (t0, n) in BLOCKS:
                    j = 0 if t0 < NCTX else 1
                    h_ = hbR.get()
                    k.dma("sp", h_[:, :, :n], hTv[:, :, t0:t0 + n], R=[hT], W=[h_])
                    norm_block(l, 0, h_, n, j, sq, pss, rs, tmp, UT, lambda kk: UT[:, kk, t0:t0 + n])
                k.barrier()
            k.es = esA
            permB = k.sb([128, 128], BF16, "permB")
            cw = k.sb([128, 12, 4], F32, "cw")
            lcw = k.sb([128, 4, 4], F32, "lcw")
            lcb = k.sb([128, 4], F32, "lcb")
            dnab_s = k.sb([4, 4], F32, "dnab_s")
            nA = k.sb([4, 2], F32, "nA")
            m01 = k.sb([4, 2, 512], F32, "m01")
            k.dma("pool", permB[:], c_perm.ap, R=[c_perm], W=[permB])
            k.dma("sp", cw[:], dn_convT.ap[l], R=[dn_convT], W=[cw])
            k.dma("sp", lcw[:], lru_cw.ap[l], R=[lru_cw], W=[lcw])
            k.dma("sp", lcb[:], lru_cb.ap[l], R=[lru_cb], W=[lcb])
            k.dma("sp", dnab_s[:], dnab.ap[l], R=[dnab], W=[dnab_s])
            k.dma("sp", m01[:], c_m01.ap, R=[c_m01], W=[m01])
            act(k, nA[:], dnab_s[:, 0:2], AF.Exp, [dnab_s], [nA])
            ts(k, "dve", nA[:], nA[:], -1.0, None, ALU.mult, None, [nA], [nA])
            WG = Rot([k.sb([128, 8, 512], BF16, "WG") for _ in range(2)])
            OB = Rot([k.sb([128, T], BF16, "OB") for _ in range(2)])
            pg = Rot([k.ps([128, 512], F32, "pg") for _ in range(4)])
            p2 = Rot([k.ps([128, 512], F32, "p2") for _ in range(2)])

            def rot(shape, dt, name, n=2):
                return Rot([k.sb(shape, dt, name) for _ in range(n)])
            w_inv = w_in.ap[l].rearrange("(k p) c -> p k c", p=128)
            groups_conv = [("dnq", 0, 512), ("dnk", 512, 512), ("dnv", 1024, 512), ("lrux", 2064, 512)]
            groups_rest = [("z", 1536, 512), ("ab", 2048, 16), ("lrug", 2576, 512), ("daq", 3088, 512),
                           ("dak", 3600, 512), ("dav", 4112, 512)] + [("gate%d" % g, 4624 + 512 * g, 512) for g in range(6)]
            def process(groups):
                for (kind, c0, ncol) in groups:
                    wt = WG.get()
                    k.dma("pool", wt[:, :, :ncol], w_inv[:, :, c0:c0 + ncol], R=[w_in], W=[wt])
                    if kind == "dav":
                        for ti in range(T // 128):
                            p_ = pg.get()
                            for kk in range(8):
                                mm(k, p_[:, :], UT[:, kk, ti * 128:(ti + 1) * 128], wt[:, kk, :], kk == 0, kk == 7, [UT, wt], [p_])
                            v_ = vbR.get()
                            cp(k, k.alt(), v_[:], p_[:], [p_], [v_])
                            k.dma("sp", VT.ap[ti * 128:(ti + 1) * 128, :], v_[:], R=[v_], W=[VT])
                        continue
                    if kind == "ab":
                        for (t0, n) in BLOCKS:
                            abt = abR.get()
                            for d in range(2):
                                p_ = pg.get()
                                for kk in range(8):
                                    mm(k, p_[0:4, :n], wt[:, kk, 4 * d:4 * d + 4], UT[:, kk, t0:t0 + n], kk == 0, kk == 7, [UT, wt], [p_])
                                e_ = eR.get()
                                act(k, e_[:, :n], p_[0:4, :n], AF.Exp, [p_, dnab_s], [e_], bias=dnab_s[:, 2 + d:3 + d])
                                act(k, e_[:, :n], e_[:, :n], AF.Ln, [e_, cst], [e_], bias=cst[0:4, 2:3])
                                ts(k, "dve", e_[:, :n], e_[:, :n], nA[:, d:d + 1], None, ALU.mult, None, [e_, nA], [e_])
                                if d == 0:
                                    k.op("dve", lambda g: g.tensor_tensor_scan(out=abt[:, 0, :n], data0=m01[:, 0, :n], data1=e_[:, :n], initial=0.0, op0=ALU.mult, op1=ALU.add), R=[m01, e_], W=[abt])
                                else:
                                    k.op("dve", lambda g: g.tensor_tensor_scan(out=abt[:, 1, :n][:, ::-1], data0=m01[:, 1, :n][:, ::-1], data1=e_[:, :n][:, ::-1], initial=0.0, op0=ALU.mult, op1=ALU.add), R=[m01, e_], W=[abt])
                                p_ = pg.get()
                                for kk in range(8):
                                    mm(k, p_[0:4, :n], wt[:, kk, 8 + 4 * d:12 + 4 * d], UT[:, kk, t0:t0 + n], kk == 0, kk == 7, [UT, wt], [p_])
                                act(k, abt[:, 2 + d, :n], p_[0:4, :n], AF.Sigmoid, [p_], [abt])
                            k.dma("sp", GCB.ap[:, :, t0:t0 + n], abt[:, :, :n], R=[abt], W=[GCB])
                        continue
                    for c4 in range(4):
                        conv = kind in ("dnq", "dnk", "dnv", "lrux")
                        xp = XP.get() if conv else None
                        ob = OB.get() if kind != "lrux" else None
                        for (t0, n) in BLOCKS:
                            p_ = pg.get()
                            for kk in range(8):
                                mm(k, p_[:, :n], wt[:, kk, c4 * 128:(c4 + 1) * 128], UT[:, kk, t0:t0 + n], kk == 0, kk == 7, [UT, wt], [p_])
                            if conv:
                                cp(k, k.alt(), xp[:, off(t0):off(t0) + n], p_[:, :n], [p_], [xp])
                            elif kind == "z":
                                act(k, ob[:, t0:t0 + n], p_[:, :n], AF.Silu, [p_], [ob])
                            elif kind == "lrug":
                                act(k, ob[:, t0:t0 + n], p_[:, :n], AF.Gelu, [p_], [ob])
                            elif kind.startswith("gate"):
                                act(k, ob[:, t0:t0 + n], p_[:, :n], AF.Sigmoid, [p_], [ob])
                            else:
                                qr_ = qrawR.get()
                                cp(k, "act", qr_[:, :n], p_[:, :n], [p_], [qr_])
                                q2 = p2.get()
                                mm(k, q2[:, :n], permB[:], qr_[:, :n], True, True, [permB, qr_], [q2])
                                a1 = t1R.get()
                                tt(k, "pool", a1[:, :n], qr_[:, :n], cosT[:, t0:t0 + n], ALU.mult, [qr_, cosT], [a1])
                                a2 = t2R.get()
                                tt(k, "dve", a2[:, :n], q2[:, :n], sinT[:, t0:t0 + n], ALU.mult, [q2, sinT], [a2])
                                tt(k, "pool", ob[:, t0:t0 + n], a1[:, :n], a2[:, :n], ALU.add, [a1, a2], [ob])
                        if conv:
                            if kind == "lrux":
                                wv = lambda jj, c4=c4: lcw[:, c4, jj:jj + 1]
                            else:
                                ci = {"dnq": 0, "dnk": 4, "dnv": 8}[kind] + c4
                                wv = lambda jj, ci=ci: cw[:, ci, jj:jj + 1]
                            wbuf = lcw if kind == "lrux" else cw
                            Dg = DgR.get()
                            for jj in range(4):
                                ts(k, "dve", Dg[:, jj, :], identB[:], wv(jj), None, ALU.mult, None, [identB, wbuf], [Dg])

                            def post(kind=kind, c4=c4, xp=xp, ob=ob, Dg=Dg):
                                ar = accrow.get() if kind != "dnv" else None
                                for (t0, n) in BLOCKS:
                                    c = off(t0)
                                    pc = pg.get()
                                    for idx, (jj, sh) in enumerate(((0, -1), (1, 0), (2, 1), (3, 2))):
                                        mm(k, pc[:, :n], Dg[:, jj, :], xp[:, c + sh:c + sh + n], idx == 0, idx == 3, [Dg, xp], [pc])
                                    if kind == "lrux":
                                        act(k, ar[:, t0:t0 + n], pc[:, :n], AF.Identity, [pc, lcb], [ar], bias=lcb[:, c4:c4 + 1])
                                    elif kind == "dnv":
                                        act(k, ob[:, t0:t0 + n], pc[:, :n], AF.Silu, [pc], [ob])
                                    else:
                                        act(k, ar[:, t0:t0 + n], pc[:, :n], AF.Silu, [pc], [ar])
                                if kind == "lrux":
                                    k.dma("sp", LX.ap[c4 * 128:(c4 + 1) * 128, :], ar[:], R=[ar], W=[LX])
                                    return
                                if kind != "dnv":
                                    for (t0, n) in BLOCKS:
                                        tt(k, "pool", sqrow[:, t0:t0 + n], ar[:, t0:t0 + n], ar[:, t0:t0 + n], ALU.mult, [ar], [sqrow])
                                    for (t0, n) in BLOCKS:
                                        q2 = p2.get()
                                        mm(k, q2[:, :n], onesB[:], sqrow[:, t0:t0 + n], True, True, [onesB, sqrow], [q2])
                                        act(k, lnrow[:, t0:t0 + n], q2[:, :n], AF.Ln, [q2, cst], [lnrow], bias=EPS6)
                                    for (t0, n) in BLOCKS:
                                        act(k, lnrow[:, t0:t0 + n], lnrow[:, t0:t0 + n], AF.Exp, [lnrow], [lnrow], scale=-0.5)
                                    for (t0, n) in BLOCKS:
                                        if kind == "dnq":
                                            stt(k, ob[:, t0:t0 + n], ar[:, t0:t0 + n], 128.0 ** -0.5, lnrow[:, t0:t0 + n], ALU.mult, ALU.mult, [ar, lnrow], [ob])
                                        else:
                                            tt(k, "dve", ob[:, t0:t0 + n], ar[:, t0:t0 + n], lnrow[:, t0:t0 + n], ALU.mult, [ar, lnrow], [ob])
                                dst = DNQKV.ap[{"dnq": 0, "dnk": 1, "dnv": 2}[kind], c4 * 128:(c4 + 1) * 128, :]
                                k.dma("sp", dst, ob[:], R=[ob], W=[DNQKV])
                            while pending:
                                pending.pop(0)()
                            pending.append(post)
                            continue
                        if ob is not None:
                            if kind in ("dnq", "dnk", "dnv"):
                                dst = DNQKV.ap[{"dnq": 0, "dnk": 1, "dnv": 2}[kind], c4 * 128:(c4 + 1) * 128, :]
                                dbuf = DNQKV
                            elif kind == "z":
                                dst, dbuf = ZS.ap[c4 * 128:(c4 + 1) * 128, :], ZS
                            elif kind == "lrug":
                                dst, dbuf = LG.ap[c4 * 128:(c4 + 1) * 128, :], LG
                            elif kind == "daq":
                                dst, dbuf = QR.ap[c4 * 128:(c4 + 1) * 128, :], QR
                            elif kind == "dak":
                                dst, dbuf = KR.ap[c4 * 128:(c4 + 1) * 128, :], KR
                            else:
                                g = int(kind[4:])
                                r0 = (g * 4 + c4) * 128
                                dst, dbuf = SG.ap[r0:r0 + 128, :], SG
                            k.dma("sp", dst, ob[:], R=[ob], W=[dbuf])
            with ExitStack() as e2:
                k.es = e2
                XPs = [k.sb([128, T + 6], BF16, "XP") for _ in range(2)]
                for x_ in XPs:
                    k.op("pool", lambda g: g.memset(x_[:], 0.0), W=[x_])
                XP = Rot(XPs)
                accrow = rot([128, T], F32, "accrow", 2)
                DgR = rot([128, 4, 128], BF16, "Dg", 2)
                pending = []
                sqrow = k.sb([128, T], BF16, "sqrow")
                lnrow = k.sb([128, T], F32, "lnrow")
                process(groups_conv)
                while pending:
                    pending.pop(0)()
                k.barrier()
            with ExitStack() as e3:
                k.es = e3
                cosT = k.sb([128, T], BF16, "cosT")
                sinT = k.sb([128, T], BF16, "sinT")
                k.dma("pool", cosT[:], c_rope.ap[0], R=[c_rope], W=[cosT])
                k.dma("pool", sinT[:], c_rope.ap[1], R=[c_rope], W=[sinT])
                qrawR = rot([128, 512], BF16, "qraw")
                t1R = rot([128, 512], F32, "t1")
                t2R = rot([128, 512], F32, "t2")
                vbR = rot([128, 512], BF16, "vb")
                abR = rot([4, 4, 512], F32, "abt", 2)
                eR = rot([4, 512], F32, "eab")
                process(groups_rest)
                k.barrier()
            k.es = esA
            k.barrier()
        k.es = es0

    def phase_C(l):
        with ExitStack() as es:
            k.es = es
            lv = k.sb([128, 3, 2, 4], F32, "lv")
            cneg = k.sb([128, 2, 4], F32, "cneg")
            k.dma("sp", lv[:], lru_vec.ap[l], R=[lru_vec], W=[lv])
            act(k, cneg[:], lv[:, 2], AF.Exp, [lv], [cneg], scale=-1.0)
            act(k, cneg[:], cneg[:], AF.Ln, [cneg, cst], [cneg], bias=ONE)
            ts(k, "dve", cneg[:], cneg[:], -8.0, None, ALU.mult, None, [cneg], [cneg])
            xc = k.sb([128, T], F32, "xc")
            xcb = k.sb([128, T], BF16, "xcb")
            gg = k.sb([128, T], BF16, "gg")
            A_s = [k.sb([128, T], F32, "lruA%d" % d_) for d_ in range(2)]
            IG_s = [k.sb([128, T], F32, "lruIG%d" % d_) for d_ in range(2)]
            W_s = [k.sb([128, T], F32, "lruW%d" % d_) for d_ in range(2)]
            H = [k.sb([128, T], F32, "lruH%d" % d) for d in range(2)]
            yb = k.sb([128, T], BF16, "lruY")
            wb = k.sb([128, 2, 2, 128], BF16, "lruwb")
            pg = Rot([k.ps([128, 512], F32, "pgC") for _ in range(4)])
            for cc in range(4):
                k.dma("sp", xc[:], LX.ap[cc * 128:(cc + 1) * 128, :], R=[LX], W=[xc])
                k.dma("sp", gg[:], LG.ap[cc * 128:(cc + 1) * 128, :], R=[LG], W=[gg])
                k.dma("pool", wb[:], lru_wblk.ap[l, :, :, cc].rearrange("a d i j -> i a d j"), R=[lru_wblk], W=[wb])
                cp(k, "act", xcb[:], xc[:], [xc], [xcb])
                for d in range(2):
                    A_, IG = A_s[d], IG_s[d]
                    for (t0, n) in BLOCKS:
                        p_ = pg.get()
                        mm(k, p_[:, :n], wb[:, 0, d, :], xcb[:, t0:t0 + n], True, True, [wb, xcb], [p_])
                        act(k, A_[:, t0:t0 + n], p_[:, :n], AF.Sigmoid, [p_, lv], [A_], bias=lv[:, 0, d, cc:cc + 1])
                        p_ = pg.get()
                        mm(k, p_[:, :n], wb[:, 1, d, :], xcb[:, t0:t0 + n], True, True, [wb, xcb], [p_])
                        act(k, IG[:, t0:t0 + n], p_[:, :n], AF.Sigmoid, [p_, lv], [IG], bias=lv[:, 1, d, cc:cc + 1])
                for d in range(2):
                    act(k, A_s[d][:], A_s[d][:], AF.Exp, [A_s[d], cneg], [A_s[d]], scale=cneg[:, d, cc:cc + 1])
                    tt(k, "pool", IG_s[d][:], IG_s[d][:], xc[:], ALU.mult, [IG_s[d], xc], [IG_s[d]])
                for d in range(2):
                    act(k, W_s[d][:], A_s[d][:], AF.Square, [A_s[d]], [W_s[d]])
                for d in range(2):
                    ts(k, "dve", W_s[d][:], W_s[d][:], -1.0, 1.0, ALU.mult, ALU.add, [W_s[d]], [W_s[d]])
                for d in range(2):
                    act(k, W_s[d][:], W_s[d][:], AF.Sqrt, [W_s[d]], [W_s[d]])
                for d in range(2):
                    tt(k, "dve", IG_s[d][:], IG_s[d][:], W_s[d][:], ALU.mult, [IG_s[d], W_s[d]], [IG_s[d]])
                for d in range(2):
                    A_, IG, Hd = A_s[d], IG_s[d], H[d]
                    if d == 0:
                        k.op("dve", lambda g: g.tensor_tensor_scan(out=Hd[:], data0=A_[:], data1=IG[:], initial=0.0, op0=ALU.mult, op1=ALU.add), R=[A_, IG], W=[Hd])
                    else:
                        k.op("dve", lambda g: g.tensor_tensor_scan(out=Hd[:, 0:NCTX][:, ::-1], data0=A_[:, 0:NCTX][:, ::-1], data1=IG[:, 0:NCTX][:, ::-1], initial=0.0, op0=ALU.mult, op1=ALU.add), R=[A_, IG], W=[Hd])
                        k.op("dve", lambda g: g.tensor_tensor_scan(out=Hd[:, NCTX:T][:, ::-1], data0=A_[:, NCTX:T][:, ::-1], data1=IG[:, NCTX:T][:, ::-1], initial=Hd[:, 0:1], op0=ALU.mult, op1=ALU.add), R=[A_, IG, Hd], W=[Hd])
                tt(k, "pool", H[0][:], H[0][:], H[1][:], ALU.add, [H[0], H[1]], [H[0]])
                tt(k, "dve", yb[:], H[0][:], gg[:], ALU.mult, [H[0], gg], [yb])
                k.dma("sp", YB.ap[cc * 128:(cc + 1) * 128, :], yb[:], R=[yb], W=[YB])
            k.barrier()
        k.es = es0

    def phase_B(l):
        with ExitStack() as es:
            k.es = es
            mb = k.sb([128, 2, 128], F32, "mb")
            nod = k.sb([128, 128], F32, "nod")
            sel = k.sb([4, 4, 128], F32, "sel")
            blk = k.sb([128, 5, 128], F32, "blk")
            k.dma("sp", mb[:], c_mb.ap.rearrange("d j i -> j d i"), R=[c_mb], W=[mb])
            k.dma("sp", nod[:], c_nod.ap, R=[c_nod], W=[nod])
            k.dma("sp", sel[:], c_sel.ap, R=[c_sel], W=[sel])
            k.dma("sp", blk[:], c_blk.ap.rearrange("m j i -> j m i"), R=[c_blk], W=[blk])
            precast_ffn(l)
            qv = DNQKV.ap.rearrange("w (h f) t -> w f h t", h=4)
            otv = OT.ap.rearrange("d (h f) t -> d f h t", h=4)
            gcv = GCB.ap
            order = {0: list(range(len(BLOCKS))), 1: [0] + list(range(len(BLOCKS) - 1, 0, -1))}
            B4 = [128, 4, 128]
            bc_h = lambda ap2: ap2.unsqueeze(1).broadcast_to(B4)
            bc_i = lambda ap2: ap2.unsqueeze(2).broadcast_to(B4)

            def dn_dir(d):
                def rot(shape, dt, name, n=1):
                    return Rot([k.sb(shape, dt, name + "%d" % d) for _ in range(n)])
                QTr = rot([128, 4, 512], BF16, "QT")
                KTr = rot([128, 4, 512], BF16, "KT")
                VTr = rot([128, 4, 512], BF16, "VTt")
                GBr = rot([4, 2, 512], F32, "GB")
                OBr = rot([128, 4, 512], F32, "OTb", 2)
                S_f = k.sb(B4, F32, "S32_%d" % d)
                S_b = k.sb(B4, BF16, "Sb_%d" % d)
                k.op("pool", lambda g: g.memset(S_f[:], 0.0), W=[S_f])
                k.op("pool", lambda g: g.memset(S_b[:], 0.0), W=[S_b])
                pf = Rot([k.ps(B4, F32, "pfB%d" % d) for _ in range(2)])
                prc = Rot([k.ps(B4, F32, "prB%d" % d) for _ in range(1)])
                pb = Rot([k.ps(B4, BF16, "pbB%d" % d) for _ in range(1)])
                gbc = rot([128, 8], F32, "gbc", 2)
                sc4 = rot([128, 4, 4], F32, "sc4", 3)
                D1 = rot(B4, F32, "D1")
                E = rot(B4, F32, "E", 2)
                nE = rot(B4, F32, "nE")
                EG = rot(B4, F32, "EG")
                Nr = rot(B4, F32, "N")
                L0r = rot(B4, F32, "L0")
                Ndr = rot(B4, F32, "Nd")
                Ldr = rot(B4, F32, "Ld")
                N2r = rot(B4, F32, "N2")
                L2r = rot(B4, F32, "L2")
                L4r = rot(B4, F32, "L4")
                Xr = rot(B4, F32, "X", 2)
                Lbr = rot(B4, F32, "Lb")
                XTr = rot(B4, F32, "XT")
                Yr = rot(B4, F32, "Y")
                Rr = rot(B4, BF16, "R", 2)
                QK = rot(B4, BF16, "QK", 3)
                kbg = rot(B4, BF16, "kbg", 2)
                kout = rot(B4, BF16, "kout", 3)
                vbt = rot(B4, BF16, "vbt", 2)
                wTr = rot(B4, BF16, "wT", 3)
                ur = rot(B4, F32, "u", 3)
                qin = rot(B4, BF16, "qin", 3)
                vnew = rot(B4, BF16, "vnew", 2)
                stmp = rot(B4, F32, "stmp")
                bm_ = lambda i_: bc_h(blk[:, i_, :])

                def mm4(p, lf, rf, R):
                    for h in range(4):
                        mm(k, p[:, h, :], lf(h), rf(h), True, True, R, [p])

                def tr4(p, src, ident, R):
                    for h in range(4):
                        tr(k, p[:, h, :], src(h), ident, R, [p])

                DEPTH = 2
                HQ = {}
                pdone = [0]
                rdone = [0]
                cidx = [0]

                def prep():
                  for step in range(len(BLOCKS)):
                      t0, n = BLOCKS[order[d][step]]
                      QT, KT, VTt, GB = QTr.get(), KTr.get(), VTr.get(), GBr.get()
                      k.dma("sp", QT[:, :, :n], qv[0, :, :, t0:t0 + n], R=[DNQKV], W=[QT])
                      k.dma("sp", KT[:, :, :n], qv[1, :, :, t0:t0 + n], R=[DNQKV], W=[KT])
                      k.dma("sp", VTt[:, :, :n], qv[2, :, :, t0:t0 + n], R=[DNQKV], W=[VTt])
                      k.dma("sp", GB[:, 0, :n], gcv[:, d, t0:t0 + n], R=[GCB], W=[GB])
                      k.dma("sp", GB[:, 1, :n], gcv[:, 2 + d, t0:t0 + n], R=[GCB], W=[GB])
                      yield
                      nch = n // 128
                      chs = list(range(nch)) if d == 0 else list(range(nch - 1, -1, -1))
                      lastc = 127 if d == 0 else 0
                      for ci_, c in enumerate(chs):
                          while cidx[0] - rdone[0] >= DEPTH:
                              yield
                          o = c * 128
                          sl = slice(o, o + 128)
                          p0 = pf.get()
                          tr(k, p0[:, 0, 0:4], GB[:, 0, sl], identF[0:4, 0:4], [GB, identF], [p0])
                          tr(k, p0[:, 0, 4:8], GB[:, 1, sl], identF[0:4, 0:4], [GB, identF], [p0])
                          g8 = gbc.get()
                          cp(k, "dve", g8[:], p0[:, 0, 0:8], [p0], [g8])
                          yield
                          GR = pf.get()
                          mm4(GR, lambda h: sel[:, h, :], lambda h: GB[:, 0, sl], [sel, GB])
                          d1 = D1.get()
                          tt(k, "dve", d1[:], GR[:], bc_i(g8[:, 0:4]), ALU.subtract, [GR, g8], [d1])
                          s4 = sc4.get()
                          act(k, s4[:, 0, :], g8[:, 0:4], AF.Exp, [g8], [s4])
                          yield
                          tt(k, "dve", s4[:, 1, :], s4[:, 0, :], g8[:, 4:8], ALU.mult, [s4, g8], [s4])
                          tt(k, "dve", s4[:, 2, :], GR[:, :, lastc], g8[:, 0:4], ALU.subtract, [GR, g8], [s4])
                          act(k, s4[:, 2, :], s4[:, 2, :], AF.Exp, [s4], [s4])
                          act(k, s4[:, 3, :], GR[:, :, lastc], AF.Exp, [GR], [s4])
                          eg_ = EG.get()
                          act(k, eg_[:], GR[:], AF.Exp, [GR], [eg_])
                          yield
                          tt(k, "pool", d1[:], d1[:], bc_h(mb[:, d, :]), ALU.add, [d1, mb], [d1])
                          e_ = E.get()
                          act(k, e_[:], d1[:], AF.Exp, [d1], [e_])
                          yield
                          ne_ = nE.get()
                          tt(k, "pool", ne_[:], e_[:], bc_h(nod[:]), ALU.mult, [e_, nod], [ne_])
                          BR = pf.get()
                          mm4(BR, lambda h: sel[:, h, :], lambda h: GB[:, 1, sl], [sel, GB])
                          tt(k, "dve", ne_[:], BR[:], ne_[:], ALU.mult, [BR, ne_], [ne_])
                          yield
                          KK = pf.get()
                          mm4(KK, lambda h: KT[:, h, sl], lambda h: KT[:, h, sl], [KT])
                          N_ = Nr.get()
                          tt(k, "dve", N_[:], KK[:], ne_[:], ALU.mult, [KK, ne_], [N_])
                          yield
                          KQ = pf.get()
                          mm4(KQ, lambda h: KT[:, h, sl], lambda h: QT[:, h, sl], [KT, QT])
                          qk = QK.get()
                          tt(k, "dve", qk[:], KQ[:], e_[:], ALU.mult, [KQ, e_], [qk])
                          yield
                          pL = pf.get()
                          tr4(pL, lambda h: N_[:, h, :], identF[:], [N_, identF])
                          L0 = L0r.get()
                          cp(k, "act", L0[:], pL[:], [pL], [L0])
                          Nd = Ndr.get()
                          tt(k, "pool", Nd[:], N_[:], bm_(0), ALU.mult, [N_, blk], [Nd])
                          yield
                          Ld = Ldr.get()
                          tt(k, "pool", Ld[:], L0[:], bm_(0), ALU.mult, [L0, blk], [Ld])
                          yield
                          pA = pf.get()
                          mm4(pA, lambda h: Ld[:, h, :], lambda h: Nd[:, h, :], [Ld, Nd])
                          N2 = N2r.get()
                          cp(k, "act", N2[:], pA[:], [pA], [N2])
                          pA = pf.get()
                          mm4(pA, lambda h: Nd[:, h, :], lambda h: Ld[:, h, :], [Ld, Nd])
                          L2 = L2r.get()
                          cp(k, "dve", L2[:], pA[:], [pA], [L2])
                          X = Xr.get()
                          tt(k, "pool", X[:], Nd[:], bc_h(identF[:]), ALU.add, [Nd, identF], [X])
                          yield
                          pA = pf.get()
                          mm4(pA, lambda h: N2[:, h, :], lambda h: L2[:, h, :], [N2, L2])
                          L4 = L4r.get()
                          cp(k, "act", L4[:], pA[:], [pA], [L4])
                          yield
                          for Lk in (L2, L4):
                              pA = pf.get()
                              mm4(pA, lambda h: Lk[:, h, :], lambda h: X[:, h, :], [Lk, X])
                              X2 = Xr.get()
                              tt(k, "dve", X2[:], pA[:], X[:], ALU.add, [pA, X], [X2])
                              X = X2
                              yield
                          for lev in range(1, 5):
                              Lb = Lbr.get()
                              tt(k, "pool", Lb[:], L0[:], bm_(lev), ALU.mult, [L0, blk], [Lb])
                              pA = pf.get()
                              tr4(pA, lambda h: X[:, h, :], identF[:], [X, identF])
                              XT = XTr.get()
                              cp(k, "act", XT[:], pA[:], [pA], [XT])
                              yield
                              pA = pf.get()
                              mm4(pA, lambda h: Lb[:, h, :], lambda h: X[:, h, :], [Lb, X])
                              Y = Yr.get()
                              cp(k, "dve", Y[:], pA[:], [pA], [Y])
                              yield
                              pA = pf.get()
                              mm4(pA, lambda h: XT[:, h, :], lambda h: Y[:, h, :], [XT, Y])
                              if lev < 4:
                                  X2 = Xr.get()
                                  tt(k, "dve", X2[:], pA[:], X[:], ALU.add, [pA, X], [X2])
                                  X = X2
                              else:
                                  R_ = Rr.get()
                                  tt(k, "dve", R_[:], pA[:], X[:], ALU.add, [pA, X], [R_])
                              yield
                          pK = pb.get()
                          tr4(pK, lambda h: KT[:, h, sl], identB[:], [KT, identB])
                          kb_ = kbg.get()
                          tt(k, "dve", kb_[:], pK[:], bc_i(s4[:, 1, :]), ALU.mult, [pK, s4], [kb_])
                          ko_ = kout.get()
                          tt(k, "dve", ko_[:], pK[:], bc_i(s4[:, 2, :]), ALU.mult, [pK, s4], [ko_])
                          yield
                          pV = pb.get()
                          tr4(pV, lambda h: VTt[:, h, sl], identB[:], [VTt, identB])
                          vb_ = vbt.get()
                          tt(k, "dve", vb_[:], pV[:], bc_i(g8[:, 4:8]), ALU.mult, [pV, g8], [vb_])
                          yield
                          pW = pf.get()
                          mm4(pW, lambda h: kb_[:, h, :], lambda h: R_[:, h, :], [kb_, R_])
                          wT = wTr.get()
                          cp(k, "act", wT[:], pW[:], [pW], [wT])
                          yield
                          pUu = pf.get()
                          mm4(pUu, lambda h: R_[:, h, :], lambda h: vb_[:, h, :], [vb_, R_])
                          u_ = ur.get()
                          cp(k, "dve", u_[:], pUu[:], [pUu], [u_])
                          qi = qin.get()
                          tt(k, "pool", qi[:], QT[:, :, sl], eg_[:], ALU.mult, [QT, eg_], [qi])
                          yield
                          HQ[cidx[0]] = dict(wT=wT, u_=u_, qi=qi, qk=qk, ko_=ko_, s4=s4, t0=t0, n=n, sl=sl, first=(ci_ == 0), last=(ci_ == len(chs) - 1))
                          cidx[0] += 1
                          pdone[0] = cidx[0]
                          yield

                def recur():
                    OTb = None
                    for idx in range(34):
                        while pdone[0] <= idx:
                            yield
                        hq = HQ.pop(idx)
                        wT, u_, qi, qk, ko_, s4, t0, n, sl = hq["wT"], hq["u_"], hq["qi"], hq["qk"], hq["ko_"], hq["s4"], hq["t0"], hq["n"], hq["sl"]
                        if hq["first"]:
                            OTb = OBr.get()
                        pWS = prc.get()
                        mm4(pWS, lambda h: wT[:, h, :], lambda h: S_b[:, h, :], [wT, S_b])
                        vn = vnew.get()
                        tt(k, "dve", vn[:], u_[:], pWS[:], ALU.subtract, [u_, pWS], [vn])
                        yield
                        pO = prc.get()
                        for h in range(4):
                            mm(k, pO[:, h, :], S_b[:, h, :], qi[:, h, :], True, False, [S_b, qi], [pO])
                            mm(k, pO[:, h, :], vn[:, h, :], qk[:, h, :], False, True, [vn, qk], [pO])
                        cp(k, "act", OTb[:, :, sl], pO[:], [pO], [OTb])
                        yield
                        pS = prc.get()
                        mm4(pS, lambda h: ko_[:, h, :], lambda h: vn[:, h, :], [ko_, vn])
                        st_ = stmp.get()
                        tt(k, "pool", st_[:], S_f[:], bc_i(s4[:, 3, :]), ALU.mult, [S_f, s4], [st_])
                        tt(k, "dve", S_f[:], st_[:], pS[:], ALU.add, [st_, pS], [S_f])
                        cp(k, "act", S_b[:], S_f[:], [S_f], [S_b])
                        yield
                        if hq["last"]:
                            k.dma("sp", otv[d, :, :, t0:t0 + n], OTb[:, :, :n], R=[OTb], W=[OT])
                        rdone[0] = idx + 1
                        yield

                return prep(), recur()

            gens = list(dn_dir(0)) + list(dn_dir(1))
            alive = [True] * len(gens)
            while any(alive):
                for gi in range(len(gens)):
                    if alive[gi]:
                        try:
                            next(gens[gi])
                        except StopIteration:
                            alive[gi] = False
            k.barrier()
        k.es = es0

    def phase_D(l):
        lam_init = 0.8 - 0.6 * math.exp(-0.3 * l)
        with ExitStack() as es:
            k.es = es
            krv = KR.ap.rearrange("(h f) t -> f h t", h=4)
            KS = [k.sb([128, 4, T], BF16, "KS%d" % s_) for s_ in range(2)]
            QRa = k.sb([128, 4, T], BF16, "QRa")
            Va = k.sb([128, T // 128, 512], BF16, "Va")
            k.op("pool", lambda g: g.memset(KS[0][64:128], 0.0), W=[KS[0]])
            k.op("pool", lambda g: g.memset(KS[1][0:64], 0.0), W=[KS[1]])
            k.dma("sp", KS[0][0:64], krv[0:64], R=[KR], W=[KS[0]])
            k.dma("sp", KS[1][64:128], krv[64:128], R=[KR], W=[KS[1]])
            k.dma("sp", QRa[:], QR.ap.rearrange("(h f) t -> f h t", h=4), R=[QR], W=[QRa])
            vtv = VT.ap.rearrange("(kt p) v -> p kt v", p=128)
            for g in range(0, T // 128, 6):
                g1 = min(g + 6, T // 128)
                k.dma("sp", Va[:, g:g1, :], vtv[:, g:g1, :], R=[VT], W=[Va])
            dl = k.sb([128, 2, 2, 64], F32, "dl")
            k.dma("sp", dl[:].rearrange("p a b f -> p (a b f)"), da_lam.ap[l:l + 1, :].partition_broadcast(128) if False else da_lam.ap[l].partition_broadcast(128), R=[da_lam], W=[dl])
            pr = k.sb([128, 2, 64], F32, "pr")
            sv = k.sb([128, 4], F32, "sv")
            dnw = k.sb([128, 1], F32, "dnw")
            k.dma("sp", dnw[:], da_normT.ap[l], R=[da_normT], W=[dnw])
            tt(k, "dve", pr[:], dl[:, :, 0, :], dl[:, :, 1, :], ALU.mult, [dl], [pr])
            k.op("dve", lambda g: g.tensor_reduce(out=sv[:, 0:2], in_=pr[:], axis=mybir.AxisListType.X, op=ALU.add), R=[pr], W=[sv])
            act(k, sv[:, 0:2], sv[:, 0:2], AF.Exp, [sv], [sv])
            tt(k, "dve", sv[:, 2:3], sv[:, 1:2], sv[:, 0:1], ALU.subtract, [sv], [sv])
            ts(k, "dve", sv[:, 2:3], sv[:, 2:3], -lam_init, None, ALU.add, None, [sv], [sv])
            ts(k, "dve", dnw[:], dnw[:], 1.0 - lam_init, None, ALU.mult, None, [dnw], [dnw])
            pst = Rot([k.ps([128, 512], F32, "pst") for _ in range(3)])
            poS = [k.ps([128, 512], F32, "po%d" % s_) for s_ in range(2)]
            plS = [k.ps([128, 512], F32, "pl%d" % s_) for s_ in range(2)]
            pn = k.ps([128, 512], F32, "pn")
            ptR = Rot([k.sb([128, 512], BF16, "pt") for _ in range(12)])
            sA = Rot([k.sb([128, 512], F32, "sA") for _ in range(2)])
            sB = Rot([k.sb([128, 512], F32, "sB") for _ in range(2)])
            sC = Rot([k.sb([128, 512], BF16, "sC") for _ in range(4)])
            oS = [Rot([k.sb([128, 512], F32, "oS%d" % s_) for _ in range(2)]) for s_ in range(2)]
            lS = [Rot([k.sb([128, 512], F32, "lS%d" % s_) for _ in range(2)]) for s_ in range(2)]
            accR = Rot([k.sb([128, 512], F32, "accD") for _ in range(2)])
            sqdR = Rot([k.sb([128, 512], BF16, "sqD") for _ in range(2)])
            rnR = Rot([k.sb([128, 512], F32, "rnD") for _ in range(2)])
            deferred = []
            ycR = Rot([k.sb([128, T], BF16, "ycrow") for _ in range(1)])
            LOOK = 2
            for h in range(4):
                yc = ycR.get()
                for (t0, n) in BLOCKS:
                    kts = [0, 1] if t0 < NCTX else list(range(T // 128))
                    items = [(s_, i_, kt) for i_, kt in enumerate(kts) for s_ in range(2)]
                    pend = []

                    grp = [[], []]
                    ngrp = [0, 0]
                    lq = []
                    ngroups = (len(kts) + 3) // 4

                    def close_group(s_):
                        g_ = grp[s_]
                        grp[s_] = []
                        gi_ = ngrp[s_]
                        ngrp[s_] += 1
                        if len(g_) == 1:
                            src = g_[0]
                        else:
                            a1 = sA.get()
                            tt(k, "dve", a1[:, :n], g_[0][:, :n], g_[1][:, :n], ALU.add, [g_[0], g_[1]], [a1])
                            if len(g_) > 2:
                                a2 = sB.get()
                                if len(g_) == 4:
                                    tt(k, "pool", a2[:, :n], g_[2][:, :n], g_[3][:, :n], ALU.add, [g_[2], g_[3]], [a2])
                                    src2 = a2
                                else:
                                    src2 = g_[2]
                                src = sC.get()
                                tt(k, "dve", src[:, :n], a1[:, :n], src2[:, :n], ALU.add, [a1, src2], [src])
                            else:
                                src = sC.get()
                                cp(k, "pool", src[:, :n], a1[:, :n], [a1], [src])

                        def lmm(src=src, gi_=gi_, s_=s_):
                            mm(k, plS[s_][:, :n], onesB[:], src[:, :n], gi_ == 0, gi_ == ngroups - 1, [onesB, src], [plS[s_]])
                        lq.append([6, lmm])

                    def flush_one():
                        s_, i_, kt, pt_ = pend.pop(0)
                        mm(k, poS[s_][:, :n], Va[:, kt, h * 128:(h + 1) * 128], pt_[:, :n], i_ == 0, i_ == len(kts) - 1, [Va, pt_], [poS[s_]])
                        grp[s_].append(pt_)
                        if len(grp[s_]) == 4 or i_ == len(kts) - 1:
                            close_group(s_)
                        for e_ in lq:
                            e_[0] -= 1
                        while lq and lq[0][0] <= 0:
                            lq.pop(0)[1]()
                    for it_, (s_, i_, kt) in enumerate(items):
                        st = pst.get()
                        mm(k, st[:, :n], KS[s_][:, h, kt * 128:(kt + 1) * 128], QRa[:, h, t0:t0 + n], True, True, [KS[s_], QRa], [st])
                        pt_ = ptR.get()
                        act(k, pt_[:, :n], st[:, :n], AF.Exp, [st], [pt_], scale=0.125)
                        pend.append((s_, i_, kt, pt_))
                        if len(pend) > LOOK:
                            flush_one()
                        if it_ == 12:
                            while deferred:
                                deferred.pop(0)()
                    while pend:
                        flush_one()
                    while lq:
                        lq.pop(0)[1]()
                    while deferred:
                        deferred.pop(0)()
                    os_, ls_ = [], []
                    for s_ in range(2):
                        o_ = oS[s_].get()
                        l_ = lS[s_].get()
                        cp(k, "dve", o_[:, :n], poS[s_][:, :n], [poS[s_]], [o_])
                        cp(k, "dve", l_[:, :n], plS[s_][:, :n], [plS[s_]], [l_])
                        os_.append(o_)
                        ls_.append(l_)
                    for s_ in range(2):
                        k.op("dve", lambda g: g.reciprocal(out=ls_[s_][:, :n], in_=ls_[s_][:, :n]), R=[ls_[s_]], W=[ls_[s_]])
                    acc, sqd, rn = accR.get(), sqdR.get(), rnR.get()
                    tt(k, "dve", acc[:, :n], os_[0][:, :n], ls_[0][:, :n], ALU.mult, [os_[0], ls_[0]], [acc])
                    tt(k, "pool", os_[1][:, :n], os_[1][:, :n], ls_[1][:, :n], ALU.mult, [os_[1], ls_[1]], [os_[1]])
                    stt(k, acc[:, :n], os_[1][:, :n], sv[:, 2:3], acc[:, :n], ALU.mult, ALU.add, [os_[1], sv, acc], [acc])
                    k.op("pool", lambda g: g.tensor_tensor(out=sqd[:, :n], in0=acc[:, :n], in1=acc[:, :n], op=ALU.mult), R=[acc], W=[sqd])

                    def part2(acc=acc, sqd=sqd, rn=rn, yc=yc, t0=t0, n=n):
                        mm(k, pn[:, :n], onesB[:], sqd[:, :n], True, True, [onesB, sqd], [pn])
                        act(k, rn[:, :n], pn[:, :n], AF.Sqrt, [pn, cst], [rn], scale=1.0 / 128, bias=EPS5)
                        k.op("dve", lambda g: g.reciprocal(out=rn[:, :n], in_=rn[:, :n]), R=[rn], W=[rn])
                        stt(k, yc[:, t0:t0 + n], acc[:, :n], dnw[:, 0:1], rn[:, :n], ALU.mult, ALU.mult, [acc, dnw, rn], [yc])
                    deferred.append(part2)
                while deferred:
                    deferred.pop(0)()
                k.dma("sp", YC.ap[h * 128:(h + 1) * 128, :], yc[:], R=[yc], W=[YC])
            k.barrier()
        k.es = es0

    def phase_E(l, last):
        with ExitStack() as es:
            k.es = es
            Wbr = k.sb([128, 3, 4, D], BF16, "Wbr")
            Wo = k.sb([128, 8, D], BF16, "Wo")
            for nb in range(3):
                k.dma("pool", Wbr[:, nb], w_branch.ap[l, nb].rearrange("(k p) c -> p k c", p=128), R=[w_branch], W=[Wbr])
            for h2 in range(2):
                k.dma("pool", Wo[:, h2 * 4:h2 * 4 + 4, :], w_out.ap[l].rearrange("(k p) c -> p k c", p=128)[:, h2 * 4:h2 * 4 + 4, :], R=[w_out], W=[Wo])
            dnw = k.sb([128, 1], F32, "dnwE")
            k.dma("sp", dnw[:], dn_normT.ap[l], R=[dn_normT], W=[dnw])
            WguR = Rot([k.sb([128, 8, 256], BF16, "Wgu") for _ in range(4)])
            WdR = Rot([k.sb([128, 22, 128], BF16, "Wd") for _ in range(2)])
            tmp = k.sb([128, 8, 512], F32, "tmpE")
            sq = k.sb([128, 8, 512], BF16, "sqE")
            zs = k.sb([128, 4, 512], BF16, "zsE")
            ya = k.sb([128, 4, 512], BF16, "yaE")
            yb = k.sb([128, 4, 512], BF16, "ybE")
            yc = k.sb([128, 4, 512], BF16, "ycE")
            sgR = Rot([k.sb([128, 3, 512], BF16, "sgE") for _ in range(2)])
            merged = k.sb([128, 8, 512], BF16, "mergedE")
            HB = [k.sb([128, 8, 512], F32, "hbE%d" % i) for i in range(2)]
            U2 = [k.sb([128, 8, 512], BF16, "u2E%d" % i) for i in range(2)]
            hid = k.sb([128, 22, 512], BF16, "hidE")
            rs = k.sb([128, 512], F32, "rsE")
            rs2 = k.sb([128, 512], F32, "rs2E")
            macc = k.sb([128, 512], F32, "maccE")
            mt = k.sb([128, 512], F32, "mtE")
            sgl = Rot([k.sb([128, 512], F32, "sglE") for _ in range(2)])
            otbufs = [Buf(hid.ap[:, 8 + 4 * i:12 + 4 * i, :].bitcast(F32).rearrange("p a b -> p (a b)"), "otE%d" % i) for i in range(2)]
            otile = Rot(otbufs)
            pg1 = Rot([k.ps([128, 512], F32, "pgE1") for _ in range(3)])
            pg2 = Rot([k.ps([128, 512], F32, "pgE2") for _ in range(3)])
            pss = k.ps([128, 512], F32, "pssE")
            ppT = k.ps([128, 4, 128], F32, "ppTE")
            ytv = lambda Y: Y.ap.rearrange("(k p) t -> p k t", p=128)
            otv = OT.ap.rearrange("d (h f) t -> d f h t", h=4)
            sgv = SG.ap.rearrange("(nb dc p) t -> p nb dc t", nb=3, dc=8)
            wgv = w_g.ap[l].rearrange("(k p) c -> p k c", p=128)
            wuv = w_u.ap[l].rearrange("(k p) c -> p k c", p=128)
            wdv = w_d.ap[l].rearrange("(fc p) c -> p fc c", p=128)
            blocks = [bk for bk in BLOCKS if not (last and bk[0] < NCTX)]

            def stage1(bi):
                t0, n = blocks[bi]
                j = 0 if t0 < NCTX else 1
                hb, u2 = HB[bi % 2], U2[bi % 2]
                k.dma("sp", tmp[:, 0:4, :n], otv[0, :, :, t0:t0 + n], R=[OT], W=[tmp])
                k.dma("sp", tmp[:, 4:8, :n], otv[1, :, :, t0:t0 + n], R=[OT], W=[tmp])
                k.dma("sp", zs[:, :, :n], ytv(ZS)[:, :, t0:t0 + n], R=[ZS], W=[zs])
                k.dma("sp", yb[:, :, :n], ytv(YB)[:, :, t0:t0 + n], R=[YB], W=[yb])
                k.dma("sp", yc[:, :, :n], ytv(YC)[:, :, t0:t0 + n], R=[YC], W=[yc])
                k.dma("sp", hb[:, :, :n], hTv[:, :, t0:t0 + n], R=[hT], W=[hb])
                yield
                tt(k, "pool", tmp[:, 0:4, :n], tmp[:, 0:4, :n], tmp[:, 4:8, :n], ALU.add, [tmp], [tmp])
                act(k, sq[:, 0:4, :n], tmp[:, 0:4, :n], AF.Square, [tmp], [sq])
                yield
                for h in range(4):
                    mm(k, pss[:, :n], onesB[:], sq[:, h, :n], True, True, [onesB, sq], [pss])
                    act(k, rs[:, :n], pss[:, :n], AF.Sqrt, [pss, cst], [rs], scale=1.0 / 128, bias=EPS6)
                    k.op("dve", lambda g: g.reciprocal(out=rs[:, :n], in_=rs[:, :n]), R=[rs], W=[rs])
                    tt(k, "dve", tmp[:, 4 + h, :n], tmp[:, h, :n], rs[:, :n], ALU.mult, [tmp, rs], [tmp])
                    stt(k, ya[:, h, :n], tmp[:, 4 + h, :n], dnw[:, 0:1], zs[:, h, :n], ALU.mult, ALU.mult, [tmp, dnw, zs], [ya])
                    yield
                for _ in range(4):
                    yield
                ys = [ya, yb, yc]
                for dc in range(8):
                    sg = sgR.get()
                    k.dma("sp", sg[:, :, :n], sgv[:, :, dc, t0:t0 + n], R=[SG], W=[sg])
                    for nb in range(3):
                        pu = pg1.get()
                        for k4 in range(4):
                            mm(k, pu[:, :n], Wbr[:, nb, k4, dc * 128:(dc + 1) * 128], ys[nb][:, k4, :n], k4 == 0, k4 == 3, [Wbr, ys[nb]], [pu])
                        if nb == 0:
                            tt(k, "dve", macc[:, :n], pu[:, :n], sg[:, 0, :n], ALU.mult, [pu, sg], [macc])
                        else:
                            tt(k, "dve", mt[:, :n], pu[:, :n], sg[:, nb, :n], ALU.mult, [pu, sg], [mt])
                            if nb == 1:
                                tt(k, "pool", macc[:, :n], macc[:, :n], mt[:, :n], ALU.add, [macc, mt], [macc])
                            else:
                                tt(k, "pool", merged[:, dc, :n], macc[:, :n], mt[:, :n], ALU.add, [macc, mt], [merged])
                        yield
                for dc in range(8):
                    py = pg1.get()
                    for k8 in range(8):
                        mm(k, py[:, :n], Wo[:, k8, dc * 128:(dc + 1) * 128], merged[:, k8, :n], k8 == 0, k8 == 7, [Wo, merged], [py])
                    stt(k, hb[:, dc, :n], py[:, :n], AB[:, l, 2, dc, j:j + 1], hb[:, dc, :n], ALU.mult, ALU.add, [py, AB, hb], [hb])
                    yield
                act(k, sq[:, :, :n], hb[:, :, :n], AF.Square, [hb], [sq])
                yield
                for kk in range(8):
                    mm(k, pss[:, :n], onesB[:], sq[:, kk, :n], kk == 0, kk == 7, [onesB, sq], [pss])
                act(k, rs[:, :n], pss[:, :n], AF.Sqrt, [pss, cst], [rs], scale=1.0 / D, bias=EPS6)
                k.op("dve", lambda g: g.reciprocal(out=rs[:, :n], in_=rs[:, :n]), R=[rs], W=[rs])
                yield
                for kk in range(8):
                    tt(k, "dve", tmp[:, kk, :n], hb[:, kk, :n], rs[:, :n], ALU.mult, [hb, rs], [tmp])
                    act(k, u2[:, kk, :n], tmp[:, kk, :n], AF.Identity, [tmp, AB], [u2],
                        scale=AB[:, l, 3, kk, j:j + 1], bias=AB[:, l, 4, kk, j:j + 1])
                    if kk % 2 == 1:
                        yield

            def stage2(bi):
                t0, n = blocks[bi]
                j = 0 if t0 < NCTX else 1
                hb, u2 = HB[bi % 2], U2[bi % 2]
                for f2 in range(11):
                    wg_ = WguR.get()
                    k.dma("sp", wg_[:], WGT.ap[f2], R=[WGT], W=[wg_])
                    wu_ = WguR.get()
                    k.dma("sp", wu_[:], WUT.ap[f2], R=[WUT], W=[wu_])
                    for c2 in range(2):
                        fc = f2 * 2 + c2
                        pg_ = pg2.get()
                        for kk in range(8):
                            mm(k, pg_[:, :n], wg_[:, kk, c2 * 128:(c2 + 1) * 128], u2[:, kk, :n], kk == 0, kk == 7, [wg_, u2], [pg_])
                        pu_ = pg2.get()
                        for kk in range(8):
                            mm(k, pu_[:, :n], wu_[:, kk, c2 * 128:(c2 + 1) * 128], u2[:, kk, :n], kk == 0, kk == 7, [wu_, u2], [pu_])
                        sg_ = sgl.get()
                        act(k, sg_[:, :n], pg_[:, :n], AF.Silu, [pg_], [sg_])
                        tt(k, "dve", hid[:, fc, :n], sg_[:, :n], pu_[:, :n], ALU.mult, [sg_, pu_], [hid] + (otbufs if (last and 8 <= fc < 16) else []))
                        yield
                for dc in range(8):
                    wd_ = WdR.get()
                    k.dma("sp", wd_[:], WDT.ap[dc], R=[WDT], W=[wd_])
                    py = pg2.get()
                    for fc in range(22):
                        mm(k, py[:, :n], wd_[:, fc, :], hid[:, fc, :n], fc == 0, fc == 21, [wd_, hid], [py])
                    stt(k, hb[:, dc, :n], py[:, :n], AB[:, l, 5, dc, j:j + 1], hb[:, dc, :n], ALU.mult, ALU.add, [py, AB, hb], [hb])
                    yield
                if not last:
                    k.dma("sp", hTv[:, :, t0:t0 + n], hb[:, :, :n], R=[hb], W=[hT])
                    yield
                else:
                    sq2 = hid
                    act(k, sq2[:, 0:8, :n], hb[:, :, :n], AF.Square, [hb], [sq2])
                    for kk in range(8):
                        mm(k, ppT[:].rearrange("p a b -> p (a b)")[:, :n], onesB[:], sq2[:, kk, :n], kk == 0, kk == 7, [onesB, sq2], [ppT])
                    act(k, rs2[:, :n], ppT[:].rearrange("p a b -> p (a b)")[:, :n], AF.Sqrt, [ppT, cst], [rs2], scale=1.0 / D, bias=EPS6)
                    k.op("dve", lambda g: g.reciprocal(out=rs2[:, :n], in_=rs2[:, :n]), R=[rs2], W=[rs2])
                    yield
                    for kk in range(8):
                        stt(k, hb[:, kk, :n], hb[:, kk, :n], nrmF[:, kk:kk + 1], rs2[:, :n], ALU.mult, ALU.mult, [hb, nrmF, rs2], [hb])
                    yield
                    for q in range(n // 128):
                        ot = otile.get()
                        for half in range(2):
                            for qq in range(4):
                                kk = half * 4 + qq
                                tr(k, ppT[:, qq, :], hb[:, kk, q * 128:(q + 1) * 128], identF[:], [hb, identF], [ppT])
                            cp(k, k.alt(), ot[:, half * 512:(half + 1) * 512], ppT[:].rearrange("p a b -> p (a b)"), [ppT], [ot, hid])
                        r0 = t0 - NCTX + q * 128
                        k.dma("sp", out_d.ap[r0:r0 + 128, :], ot[:], R=[ot], W=[out_d])
                        yield

            def run_pair(g1, g2):
                gens = [g for g in (g1, g2) if g is not None]
                alive = [True] * len(gens)
                while any(alive):
                    for gi in range(len(gens)):
                        if alive[gi]:
                            try:
                                next(gens[gi])
                            except StopIteration:
                                alive[gi] = False
            nbk = len(blocks)
            run_pair(stage1(0), None)
            for bi in range(nbk):
                run_pair(stage2(bi), stage1(bi + 1) if bi + 1 < nbk else None)
            k.barrier()
        k.es = es0

    done = False
    for l in range(DEPTH):
        last = (l == DEPTH - 1)
        for name, fn in (("A", lambda: phase_A(l)), ("C", lambda: phase_C(l)), ("B", lambda: phase_B(l)),
                         ("D", lambda: phase_D(l)), ("E", lambda: phase_E(l, last))):
            fn()
            if stop_after == "%s%d" % (name, l):
                done = True
                break
        if done:
            break
    k.barrier()
    es0.close()
    return nc, k


def _consts():
    c = {}
    c["c_ident"] = np.eye(128, dtype=np.float32)
    inv = (10000.0 ** (-np.arange(16, dtype=np.float32) / 16)).astype(np.float32)
    lt = np.arange(NLAT)
    row = (lt // 64).astype(np.float32)
    col = (lt % 64).astype(np.float32)
    cos = np.ones((128, T), np.float32)
    sin = np.zeros((128, T), np.float32)
    perm = np.zeros((128, 128), np.float32)
    for p in range(128):
        e = p % 64
        axis = e // 32
        half = (e % 32) // 16
        f = e % 16
        pos = row if axis == 0 else col
        ang = (pos * inv[f]).astype(np.float32)
        cos[p, NCTX:] = np.cos(ang)
        sn = np.sin(ang)
        sin[p, NCTX:] = -sn if half == 0 else sn
        partner = p + 16 if half == 0 else p - 16
        perm[partner, p] = 1.0
    c["c_rope"] = np.stack([cos, sin]).astype(np.float32)
    c["c_perm"] = perm
    jj, ii = np.meshgrid(np.arange(128), np.arange(128), indexing="ij")
    mb = np.stack([np.where(jj <= ii, 0.0, NEG), np.where(jj >= ii, 0.0, NEG)]).astype(np.float32)
    c["c_mb"] = mb
    c["c_nod"] = (-(1.0 - np.eye(128))).astype(np.float32)
    m01 = np.ones((4, 2, 512), np.float32)
    m01[:, 0, 0::128] = 0.0
    m01[:, 1, 127::128] = 0.0
    c["c_m01"] = m01
    sel = np.zeros((4, 4, 128), np.float32)
    for h in range(4):
        sel[h, h, :] = 1.0
    c["c_sel"] = sel
    bl = np.zeros((5, 128, 128), np.float32)
    bl[0] = (jj // 8 == ii // 8)
    for m_, b_ in enumerate((8, 16, 32, 64)):
        bl[m_ + 1] = (jj // (2 * b_) == ii // (2 * b_)) & (jj // b_ != ii // b_)
    c["c_blk"] = bl
    return c


def _prep_shared(inp):
    f = lambda a: np.ascontiguousarray(a, dtype=np.float32)
    s = {}
    s["w_mod"] = f(inp["w_mod"])
    s["b_modT"] = f(inp["b_mod"].reshape(DEPTH, 48, 128).transpose(0, 2, 1))
    nm = np.stack([inp["norm_mix"], inp["norm_ffn"]], 1)
    s["normsT"] = f(nm.reshape(DEPTH, 2, 8, 128).transpose(3, 0, 1, 2))
    s["normfT"] = f(inp["norm_final"].reshape(8, 128).T)
    s["w_in"] = f(inp["w_in"])
    s["dn_convT"] = f(inp["dn_conv"].reshape(DEPTH, 4, 12, 128).transpose(0, 3, 2, 1))
    s["dnab"] = f(np.concatenate([inp["dn_a_log"].transpose(0, 2, 1), inp["dn_dt_bias"].transpose(0, 2, 1)], -1))
    s["dn_normT"] = f(inp["dn_norm"].reshape(DEPTH, 128, 1))
    s["lru_cw"] = f(inp["lru_conv_w"].reshape(DEPTH, 4, 4, 128).transpose(0, 3, 2, 1))
    s["lru_cb"] = f(inp["lru_conv_b"].reshape(DEPTH, 4, 128).transpose(0, 2, 1))
    lv = np.stack([inp["lru_ba"], inp["lru_bi"], inp["lru_lambda"]], 1)
    s["lru_vec"] = f(lv.reshape(DEPTH, 3, 2, 4, 128).transpose(0, 4, 1, 2, 3))
    wb = np.zeros((DEPTH, 2, 2, 4, 128, 128), np.float32)
    for ai, w in enumerate([inp["lru_wa"], inp["lru_wi"]]):
        for cc in range(4):
            for g2 in range(2):
                wb[:, ai, :, cc, g2 * 64:(g2 + 1) * 64, g2 * 64:(g2 + 1) * 64] = w[:, :, cc * 2 + g2]
    s["lru_wblk"] = wb
    s["da_lam"] = f(inp["da_lambda"].reshape(DEPTH, 256))
    s["da_normT"] = f(inp["da_norm"].reshape(DEPTH, 128, 1))
    s["w_branch"] = f(inp["w_branch"])
    s["w_out"] = f(inp["w_out"])
    s["w_ffn_gate"] = f(inp["w_ffn_gate"])
    s["w_ffn_up"] = f(inp["w_ffn_up"])
    s["w_ffn_down"] = f(inp["w_ffn_down"])
    s.update(_consts())
    return s


def _prep_core(inp, b):
    m = {}
    m["xin"] = np.ascontiguousarray(np.concatenate([inp["ctx"][b], inp["x"][b]], 0), dtype=np.float32)
    cv = np.stack([inp["c_ctx"], inp["c"][b]], 0)
    m["cT"] = np.ascontiguousarray(cv.reshape(2, 8, 128).transpose(2, 1, 0), dtype=np.float32)
    return m


_CACHE = {}


def kernel(**inputs):
    inp = {k_: np.asarray(v) for k_, v in inputs.items()}
    if "nc" not in _CACHE:
        _CACHE["nc"] = build()[0]
    nc = _CACHE["nc"]
    shared = _prep_shared(inp)
    in_maps = []
    for b in range(8):
        m = dict(shared)
        m.update(_prep_core(inp, b))
        in_maps.append(m)
    res = run_bass_kernel_spmd(nc, in_maps, core_ids=list(range(8)))
    return np.stack([np.asarray(r["out"], dtype=np.float32) for r in res.results], 0)
```

```python
import math
import numpy as np
from contextlib import ExitStack
import concourse.bass as bass
import concourse.mybir as mybir
from concourse.bass_utils import run_bass_kernel_spmd
from concourse.alu_op_type import AluOpType as ALU

AF = mybir.ActivationFunctionType
F32 = mybir.dt.float32
BF16 = mybir.dt.bfloat16

D = 1024
NCTX = 256
NLAT = 4096
T = NCTX + NLAT
DEPTH = 2
DFF = 2816
INC = 7696
BLOCKS = [(0, 256)] + [(256 + 512 * j, 512) for j in range(8)]
NEG = -30000.0


class Buf:
    def __init__(self, ap, name, share=None):
        self.ap = ap
        self.name = name
        self.st = share.st if share is not None else [None, []]

    @property
    def w(self):
        return self.st[0]

    @w.setter
    def w(self, v):
        self.st[0] = v

    @property
    def r(self):
        return self.st[1]

    @r.setter
    def r(self, v):
        self.st[1] = v

    def __getitem__(self, k):
        return self.ap[k]


class K:
    NDMA = 8

    def __init__(self, nc, es):
        self.nc = nc
        self.es = es
        self.eng = {"pe": nc.tensor, "act": nc.scalar, "dve": nc.vector,
                    "pool": nc.gpsimd, "sp": nc.sync}
        self.sem = {}
        self.cnt = {}
        for e in self.eng:
            self.sem[e] = es.enter_context(nc.semaphore("s_" + e))
            self.cnt[e] = 0
        self.dring = {}
        for q in ("sp", "pool"):
            ring = []
            for i in range(self.NDMA):
                key = "d_%s%d" % (q, i)
                self.sem[key] = es.enter_context(nc.semaphore(key))
                self.cnt[key] = 0
                ring.append(key)
            self.dring[q] = ring
        self.dpos = {q: 0 for q in self.dring}
        self.seen = {e: {} for e in self.eng}
        self.nbuf = 0
        self.ninst = 0
        self.rr = 0

    def sb(self, shape, dt, name=None):
        self.nbuf += 1
        name = (name or "sb") + "_%d" % self.nbuf
        t = self.es.enter_context(self.nc.sbuf_tensor(name, list(shape), dt))
        return Buf(t, name)

    def ps(self, shape, dt, name=None):
        self.nbuf += 1
        name = (name or "ps") + "_%d" % self.nbuf
        t = self.es.enter_context(self.nc.psum_tensor(name, list(shape), dt))
        return Buf(t, name)

    def _deps(self, R, W):
        d = {}

        def add(ev):
            if ev is None:
                return
            k, v = ev
            if d.get(k, 0) < v:
                d[k] = v
        for b in R:
            add(b.w)
        for b in W:
            add(b.w)
            for ev in b.r:
                add(ev)
        return d

    def _wait(self, e, d):
        eng = self.eng[e]
        seen = self.seen[e]
        for k, v in d.items():
            if e == "pe" and k == "pe":
                continue
            if seen.get(k, 0) >= v:
                continue
            eng.wait_ge(self.sem[k], v)
            seen[k] = v

    def _mark(self, ev, R, W):
        for b in R:
            b.r = [x for x in b.r if x[0] != ev[0]] + [ev]
        for b in W:
            b.w = ev
            b.r = []

    def op(self, e, fn, R=(), W=()):
        d = self._deps(R, W)
        self._wait(e, d)
        inst = fn(self.eng[e])
        self.cnt[e] += 1
        inst.then_inc(self.sem[e], 1)
        self._mark((e, self.cnt[e]), R, W)
        self.ninst += 1

    def dma(self, q, out, in_, R=(), W=(), **kw):
        ring = self.dring[q]
        key = ring[self.dpos[q] % self.NDMA]
        self.dpos[q] += 1
        d = self._deps(R, W)
        if self.cnt[key] > 0 and d.get(key, 0) < self.cnt[key]:
            d[key] = self.cnt[key]
        self._wait(q, d)
        inst = self.eng[q].dma_start(out=out, in_=in_, **kw)
        self.cnt[key] += 16
        inst.then_inc(self.sem[key], 16)
        self._mark((key, self.cnt[key]), R, W)
        self.ninst += 1

    def barrier(self):
        d = {k: v for k, v in self.cnt.items() if v > 0}
        for e in self.eng:
            self._wait(e, dict(d))

    def alt(self):
        self.rr += 1
        return "act" if self.rr % 2 else "dve"


def tt(k, e, out, a, b, op, R, W):
    k.op(e, lambda g: g.tensor_tensor(out=out, in0=a, in1=b, op=op), R=R, W=W)


def ts(k, e, out, a, s1, s2, op0, op1, R, W):
    if op1 is None:
        k.op(e, lambda g: g.tensor_scalar(out=out, in0=a, scalar1=s1, scalar2=None, op0=op0), R=R, W=W)
    else:
        k.op(e, lambda g: g.tensor_scalar(out=out, in0=a, scalar1=s1, scalar2=s2, op0=op0, op1=op1), R=R, W=W)


def stt(k, out, a, s, b, op0, op1, R, W):
    k.op("dve", lambda g: g.scalar_tensor_tensor(out=out, in0=a, scalar=s, in1=b, op0=op0, op1=op1), R=R, W=W)


def act(k, out, a, func, R, W, scale=None, bias=None):
    kw = {}
    if scale is not None:
        kw["scale"] = scale
    if bias is not None:
        kw["bias"] = bias
    k.op("act", lambda g: g.activation(out=out, in_=a, func=func, **kw), R=R, W=W)


def cp(k, e, out, a, R, W):
    if e == "act":
        k.op("act", lambda g: g.activation(out=out, in_=a, func=AF.Copy), R=R, W=W)
    else:
        k.op(e, lambda g: g.tensor_copy(out=out, in_=a), R=R, W=W)


def mm(k, out, lhsT, rhs, start, stop, R, W):
    k.op("pe", lambda g: g.matmul(out, lhsT=lhsT, rhs=rhs, start=start, stop=stop), R=R, W=W)


def tr(k, out, in_, ident, R, W):
    k.op("pe", lambda g: g.transpose(out=out, in_=in_, identity=ident), R=R, W=W)


class Rot:
    def __init__(self, bufs):
        self.bufs = bufs
        self.i = 0

    def get(self):
        b = self.bufs[self.i % len(self.bufs)]
        self.i += 1
        return b


def build(stop_after=None, debug=False):
    nc = bass.Bass("TRN2", target_bir_lowering=False)
    SK = "ExternalOutput" if debug else "Internal"

    def din(name, shape, dt=F32):
        return Buf(nc.dram_tensor(name, list(shape), dt, kind="ExternalInput").ap(), name)

    def dsc(name, shape, dt):
        return Buf(nc.dram_tensor(name, list(shape), dt, kind=SK).ap(), name)

    xin = din("xin", [T, D])
    cT_d = din("cT", [128, 8, 2])
    w_mod = din("w_mod", [DEPTH, D, 6 * D])
    b_modT = din("b_modT", [DEPTH, 128, 48])
    normsT = din("normsT", [128, DEPTH, 2, 8])
    normfT = din("normfT", [128, 8])
    w_in = din("w_in", [DEPTH, D, INC])
    dn_convT = din("dn_convT", [DEPTH, 128, 12, 4])
    dnab = din("dnab", [DEPTH, 4, 4])
    dn_normT = din("dn_normT", [DEPTH, 128, 1])
    lru_cw = din("lru_cw", [DEPTH, 128, 4, 4])
    lru_cb = din("lru_cb", [DEPTH, 128, 4])
    lru_vec = din("lru_vec", [DEPTH, 128, 3, 2, 4])
    lru_wblk = din("lru_wblk", [DEPTH, 2, 2, 4, 128, 128])
    da_lam = din("da_lam", [DEPTH, 256])
    da_normT = din("da_normT", [DEPTH, 128, 1])
    w_branch = din("w_branch", [DEPTH, 3, 512, D])
    w_out = din("w_out", [DEPTH, D, D])
    w_g = din("w_ffn_gate", [DEPTH, D, DFF])
    w_u = din("w_ffn_up", [DEPTH, D, DFF])
    w_d = din("w_ffn_down", [DEPTH, DFF, D])
    c_ident = din("c_ident", [128, 128])
    c_rope = din("c_rope", [2, 128, T])
    c_perm = din("c_perm", [128, 128])
    c_mb = din("c_mb", [2, 128, 128])
    c_nod = din("c_nod", [128, 128])
    c_m01 = din("c_m01", [4, 2, 512])
    c_sel = din("c_sel", [4, 4, 128])
    c_blk = din("c_blk", [5, 128, 128])
    out_d = Buf(nc.dram_tensor("out", [NLAT, D], F32, kind="ExternalOutput").ap(), "out")

    hT = dsc("hT", [D, T], F32)
    DNQKV = dsc("DNQKV", [3, 512, T], BF16)
    ZS = dsc("ZS", [512, T], BF16)
    GCB = dsc("GCB", [4, 4, T], F32)
    LX = dsc("LX", [512, T], F32)
    LG = dsc("LG", [512, T], BF16)
    QR = dsc("QR", [512, T], BF16)
    KR = dsc("KR", [512, T], BF16)
    VT = dsc("VT", [T, 512], BF16)
    SG = dsc("SG", [3072, T], BF16)
    OT = dsc("OT", [2, 512, T], F32)
    YB = dsc("YB", [512, T], BF16)
    YC = dsc("YC", [512, T], BF16)
    hTv = hT.ap.rearrange("(k p) t -> p k t", p=128)
    WGT = dsc("WGT", [11, 128, 8, 256], BF16)
    WUT = dsc("WUT", [11, 128, 8, 256], BF16)
    WDT = dsc("WDT", [8, 128, 22, 128], BF16)

    def precast_ffn(l):
        wgv = w_g.ap[l].rearrange("(k p) c -> p k c", p=128)
        wuv = w_u.ap[l].rearrange("(k p) c -> p k c", p=128)
        wdv = w_d.ap[l].rearrange("(fc p) c -> p fc c", p=128)
        for f2 in range(11):
            k.dma("pool", WGT.ap[f2], wgv[:, :, f2 * 256:(f2 + 1) * 256], R=[w_g], W=[WGT])
            k.dma("pool", WUT.ap[f2], wuv[:, :, f2 * 256:(f2 + 1) * 256], R=[w_u], W=[WUT])
        for dc in range(8):
            k.dma("pool", WDT.ap[dc], wdv[:, :, dc * 128:(dc + 1) * 128], R=[w_d], W=[WDT])

    es0 = ExitStack()
    k = K(nc, es0)
    identF = k.sb([128, 128], F32, "identF")
    identB = k.sb([128, 128], BF16, "identB")
    onesB = k.sb([128, 128], BF16, "onesB")
    modT = k.sb([128, DEPTH, 48, 2], F32, "modT")
    AB = k.sb([128, DEPTH, 6, 8, 2], F32, "AB")
    nrmT = k.sb([128, DEPTH, 2, 8], F32, "nrmT")
    nrmF = k.sb([128, 8], F32, "nrmF")
    cst = k.sb([128, 4], F32, "cst")
    k.dma("sp", identF[:], c_ident.ap, R=[c_ident], W=[identF])
    k.dma("sp", nrmT[:], normsT.ap, R=[normsT], W=[nrmT])
    k.dma("sp", nrmF[:], normfT.ap, R=[normfT], W=[nrmF])
    cp(k, "dve", identB[:], identF[:], [identF], [identB])
    k.op("dve", lambda g: g.memset(onesB[:], 1.0), W=[onesB])
    k.op("dve", lambda g: g.memset(cst[:, 0:1], 1e-6), W=[cst])
    k.op("dve", lambda g: g.memset(cst[:, 1:2], 1e-5), W=[cst])
    k.op("dve", lambda g: g.memset(cst[:, 2:3], 1.0), W=[cst])
    k.op("dve", lambda g: g.memset(cst[:, 3:4], 0.0), W=[cst])
    EPS6, EPS5, ONE = cst[:, 0:1], cst[:, 1:2], cst[:, 2:3]

    def norm_block(l, which, hb, n, j, sq, pss, rs, tmp, ubuf, uout):
        act(k, sq[:, :, :n], hb[:, :, :n], AF.Square, [hb], [sq])
        for kk in range(8):
            mm(k, pss[:, :n], onesB[:], sq[:, kk, :n], kk == 0, kk == 7, [onesB, sq], [pss])
        act(k, rs[:, :n], pss[:, :n], AF.Sqrt, [pss, cst], [rs], scale=1.0 / D, bias=EPS6)
        k.op("dve", lambda g: g.reciprocal(out=rs[:, :n], in_=rs[:, :n]), R=[rs], W=[rs])
        for kk in range(8):
            tt(k, "dve", tmp[:, kk, :n], hb[:, kk, :n], rs[:, :n], ALU.mult, [hb, rs], [tmp])
            act(k, uout(kk), tmp[:, kk, :n], AF.Identity, [tmp, AB], [ubuf],
                scale=AB[:, l, 3 * which + 0, kk, j:j + 1], bias=AB[:, l, 3 * which + 1, kk, j:j + 1])

    with ExitStack() as es:
        k.es = es
        cTs = k.sb([128, 8, 2], F32, "cTs")
        sTs = k.sb([128, 8, 2], F32, "sTs")
        bm = k.sb([128, DEPTH, 48], F32, "bm")
        k.dma("sp", cTs[:], cT_d.ap, R=[cT_d], W=[cTs])
        for l in range(DEPTH):
            k.dma("sp", bm[:, l, :], b_modT.ap[l], R=[b_modT], W=[bm])
        act(k, sTs[:], cTs[:], AF.Silu, [cTs], [sTs])
        wm = Rot([k.sb([128, 8, 512], F32, "wm") for _ in range(2)])
        pmod = k.ps([128, 48, 2], F32, "pmod")
        for l in range(DEPTH):
            for g in range(12):
                wt = wm.get()
                k.dma("sp", wt[:], w_mod.ap[l].rearrange("(k p) c -> p k c", p=128)[:, :, g * 512:(g + 1) * 512], R=[w_mod], W=[wt])
                for c4 in range(4):
                    ch = g * 4 + c4
                    for kk in range(8):
                        mm(k, pmod[:, ch, :], wt[:, kk, c4 * 128:(c4 + 1) * 128], sTs[:, kk, :], kk == 0, kk == 7, [wt, sTs], [pmod])
            for j in range(2):
                tt(k, "dve", modT[:, l, :, j], pmod[:, :, j], bm[:, l, :], ALU.add, [pmod, bm], [modT])
            for s, (ish, isc, ig) in enumerate([(0, 1, 2), (3, 4, 5)]):
                for j in range(2):
                    stt(k, AB[:, l, 3 * s + 0, :, j], modT[:, l, isc * 8:(isc + 1) * 8, j], 1.0, nrmT[:, l, s, :], ALU.add, ALU.mult, [modT, nrmT], [AB])
                    cp(k, "dve", AB[:, l, 3 * s + 1, :, j], modT[:, l, ish * 8:(ish + 1) * 8, j], [modT], [AB])
                    cp(k, "dve", AB[:, l, 3 * s + 2, :, j], modT[:, l, ig * 8:(ig + 1) * 8, j], [modT], [AB])
        k.barrier()
    k.es = es0

    with ExitStack() as es:
        k.es = es
        xt = Rot([k.sb([128, D], F32, "xt") for _ in range(2)])
        ht = Rot([k.sb([128, 8, 128], F32, "ht") for _ in range(2)])
        pp = Rot([k.ps([128, 4, 128], F32, "ppT") for _ in range(4)])
        for t in range(T // 128):
            x_ = xt.get()
            k.dma("sp", x_[:], xin.ap[t * 128:(t + 1) * 128, :], R=[xin], W=[x_])
            h_ = ht.get()
            for half in range(2):
                p_ = pp.get()
                for q in range(4):
                    kk = half * 4 + q
                    tr(k, p_[:, q, :], x_[:, kk * 128:(kk + 1) * 128], identF[:], [x_, identF], [p_])
                cp(k, k.alt(), h_[:, half * 4:half * 4 + 4, :], p_[:], [p_], [h_])
            k.dma("sp", hTv[:, :, t * 128:(t + 1) * 128], h_[:], R=[h_], W=[hT])
        k.barrier()
    k.es = es0

    def off(t):
        return t + 1 if t < NCTX else t + 4

    def phase_A(l):
        with ExitStack() as esA:
            k.es = esA
            UT = k.sb([128, 8, T], BF16, "UT")
            with ExitStack() as es1:
                k.es = es1
                hbR = Rot([k.sb([128, 8, 512], F32, "hbA") for _ in range(2)])
                sq = k.sb([128, 8, 512], BF16, "sqA")
                tmp = k.sb([128, 8, 512], F32, "tmpA")
                rs = k.sb([128, 512], F32, "rsA")
                pss = k.ps([128, 512], F32, "pssA")
                for (t0, n) in BLOCKS:
                    j = 0 if t0 < NCTX else 1
                    h_ = hbR.get()
                    k.dma("sp", h_[:, :, :n], hTv[:, :, t0:t0 + n], R=[hT], W=[h_])
                    norm_block(l, 0, h_, n, j, sq, pss, rs, tmp, UT, lambda kk: UT[:, kk, t0:t0 + n])
                k.barrier()
            k.es = esA
            permB = k.sb([128, 128], BF16, "permB")
            cw = k.sb([128, 12, 4], F32, "cw")
            lcw = k.sb([128, 4, 4], F32, "lcw")
            lcb = k.sb([128, 4], F32, "lcb")
            dnab_s = k.sb([4, 4], F32, "dnab_s")
            nA = k.sb([4, 2], F32, "nA")
            m01 = k.sb([4, 2, 512], F32, "m01")
            k.dma("pool", permB[:], c_perm.ap, R=[c_perm], W=[permB])
            k.dma("sp", cw[:], dn_convT.ap[l], R=[dn_convT], W=[cw])
            k.dma("sp", lcw[:], lru_cw.ap[l], R=[lru_cw], W=[lcw])
            k.dma("sp", lcb[:], lru_cb.ap[l], R=[lru_cb], W=[lcb])
            k.dma("sp", dnab_s[:], dnab.ap[l], R=[dnab], W=[dnab_s])
            k.dma("sp", m01[:], c_m01.ap, R=[c_m01], W=[m01])
            act(k, nA[:], dnab_s[:, 0:2], AF.Exp, [dnab_s], [nA])
            ts(k, "dve", nA[:], nA[:], -1.0, None, ALU.mult, None, [nA], [nA])
            WG = Rot([k.sb([128, 8, 512], BF16, "WG") for _ in range(2)])
            OB = Rot([k.sb([128, T], BF16, "OB") for _ in range(2)])
            pg = Rot([k.ps([128, 512], F32, "pg") for _ in range(4)])
            p2 = Rot([k.ps([128, 512], F32, "p2") for _ in range(2)])

            def rot(shape, dt, name, n=2):
                return Rot([k.sb(shape, dt, name) for _ in range(n)])
            w_inv = w_in.ap[l].rearrange("(k p) c -> p k c", p=128)
            groups_conv = [("dnq", 0, 512), ("dnk", 512, 512), ("dnv", 1024, 512), ("lrux", 2064, 512)]
            groups_rest = [("z", 1536, 512), ("ab", 2048, 16), ("lrug", 2576, 512), ("daq", 3088, 512),
                           ("dak", 3600, 512), ("dav", 4112, 512)] + [("gate%d" % g, 4624 + 512 * g, 512) for g in range(6)]
            def process(groups):
                for (kind, c0, ncol) in groups:
                    wt = WG.get()
                    k.dma("pool", wt[:, :, :ncol], w_inv[:, :, c0:c0 + ncol], R=[w_in], W=[wt])
                    if kind == "dav":
                        for ti in range(T // 128):
                            p_ = pg.get()
                            for kk in range(8):
                                mm(k, p_[:, :], UT[:, kk, ti * 128:(ti + 1) * 128], wt[:, kk, :], kk == 0, kk == 7, [UT, wt], [p_])
                            v_ = vbR.get()
                            cp(k, k.alt(), v_[:], p_[:], [p_], [v_])
                            k.dma("sp", VT.ap[ti * 128:(ti + 1) * 128, :], v_[:], R=[v_], W=[VT])
                        continue
                    if kind == "ab":
                        for (t0, n) in BLOCKS:
                            abt = abR.get()
                            for d in range(2):
                                p_ = pg.get()
                                for kk in range(8):
                                    mm(k, p_[0:4, :n], wt[:, kk, 4 * d:4 * d + 4], UT[:, kk, t0:t0 + n], kk == 0, kk == 7, [UT, wt], [p_])
                                e_ = eR.get()
                                act(k, e_[:, :n], p_[0:4, :n], AF.Exp, [p_, dnab_s], [e_], bias=dnab_s[:, 2 + d:3 + d])
                                act(k, e_[:, :n], e_[:, :n], AF.Ln, [e_, cst], [e_], bias=cst[0:4, 2:3])
                                ts(k, "dve", e_[:, :n], e_[:, :n], nA[:, d:d + 1], None, ALU.mult, None, [e_, nA], [e_])
                                if d == 0:
                                    k.op("dve", lambda g: g.tensor_tensor_scan(out=abt[:, 0, :n], data0=m01[:, 0, :n], data1=e_[:, :n], initial=0.0, op0=ALU.mult, op1=ALU.add), R=[m01, e_], W=[abt])
                                else:
                                    k.op("dve", lambda g: g.tensor_tensor_scan(out=abt[:, 1, :n][:, ::-1], data0=m01[:, 1, :n][:, ::-1], data1=e_[:, :n][:, ::-1], initial=0.0, op0=ALU.mult, op1=ALU.add), R=[m01, e_], W=[abt])
                                p_ = pg.get()
                                for kk in range(8):
                                    mm(k, p_[0:4, :n], wt[:, kk, 8 + 4 * d:12 + 4 * d], UT[:, kk, t0:t0 + n], kk == 0, kk == 7, [UT, wt], [p_])
                                act(k, abt[:, 2 + d, :n], p_[0:4, :n], AF.Sigmoid, [p_], [abt])
                            k.dma("sp", GCB.ap[:, :, t0:t0 + n], abt[:, :, :n], R=[abt], W=[GCB])
                        continue
                    for c4 in range(4):
                        conv = kind in ("dnq", "dnk", "dnv", "lrux")
                        xp = XP.get() if conv else None
                        ob = OB.get() if kind != "lrux" else None
                        for (t0, n) in BLOCKS:
                            p_ = pg.get()
                            for kk in range(8):
                                mm(k, p_[:, :n], wt[:, kk, c4 * 128:(c4 + 1) * 128], UT[:, kk, t0:t0 + n], kk == 0, kk == 7, [UT, wt], [p_])
                            if conv:
                                cp(k, k.alt(), xp[:, off(t0):off(t0) + n], p_[:, :n], [p_], [xp])
                            elif kind == "z":
                                act(k, ob[:, t0:t0 + n], p_[:, :n], AF.Silu, [p_], [ob])
                            elif kind == "lrug":
                                act(k, ob[:, t0:t0 + n], p_[:, :n], AF.Gelu, [p_], [ob])
                            elif kind.startswith("gate"):
                                act(k, ob[:, t0:t0 + n], p_[:, :n], AF.Sigmoid, [p_], [ob])
                            else:
                                qr_ = qrawR.get()
                                cp(k, "act", qr_[:, :n], p_[:, :n], [p_], [qr_])
                                q2 = p2.get()
                                mm(k, q2[:, :n], permB[:], qr_[:, :n], True, True, [permB, qr_], [q2])
                                a1 = t1R.get()
                                tt(k, "pool", a1[:, :n], qr_[:, :n], cosT[:, t0:t0 + n], ALU.mult, [qr_, cosT], [a1])
                                a2 = t2R.get()
                                tt(k, "dve", a2[:, :n], q2[:, :n], sinT[:, t0:t0 + n], ALU.mult, [q2, sinT], [a2])
                                tt(k, "pool", ob[:, t0:t0 + n], a1[:, :n], a2[:, :n], ALU.add, [a1, a2], [ob])
                        if conv:
                            if kind == "lrux":
                                wv = lambda jj, c4=c4: lcw[:, c4, jj:jj + 1]
                            else:
                                ci = {"dnq": 0, "dnk": 4, "dnv": 8}[kind] + c4
                                wv = lambda jj, ci=ci: cw[:, ci, jj:jj + 1]
                            wbuf = lcw if kind == "lrux" else cw
                            Dg = DgR.get()
                            for jj in range(4):
                                ts(k, "dve", Dg[:, jj, :], identB[:], wv(jj), None, ALU.mult, None, [identB, wbuf], [Dg])

                            def post(kind=kind, c4=c4, xp=xp, ob=ob, Dg=Dg):
                                ar = accrow.get() if kind != "dnv" else None
                                for (t0, n) in BLOCKS:
                                    c = off(t0)
                                    pc = pg.get()
                                    for idx, (jj, sh) in enumerate(((0, -1), (1, 0), (2, 1), (3, 2))):
                                        mm(k, pc[:, :n], Dg[:, jj, :], xp[:, c + sh:c + sh + n], idx == 0, idx == 3, [Dg, xp], [pc])
                                    if kind == "lrux":
                                        act(k, ar[:, t0:t0 + n], pc[:, :n], AF.Identity, [pc, lcb], [ar], bias=lcb[:, c4:c4 + 1])
                                    elif kind == "dnv":
                                        act(k, ob[:, t0:t0 + n], pc[:, :n], AF.Silu, [pc], [ob])
                                    else:
                                        act(k, ar[:, t0:t0 + n], pc[:, :n], AF.Silu, [pc], [ar])
                                if kind == "lrux":
                                    k.dma("sp", LX.ap[c4 * 128:(c4 + 1) * 128, :], ar[:], R=[ar], W=[LX])
                                    return
                                if kind != "dnv":
                                    for (t0, n) in BLOCKS:
                                        tt(k, "pool", sqrow[:, t0:t0 + n], ar[:, t0:t0 + n], ar[:, t0:t0 + n], ALU.mult, [ar], [sqrow])
                                    for (t0, n) in BLOCKS:
                                        q2 = p2.get()
                                        mm(k, q2[:, :n], onesB[:], sqrow[:, t0:t0 + n], True, True, [onesB, sqrow], [q2])
                                        act(k, lnrow[:, t0:t0 + n], q2[:, :n], AF.Ln, [q2, cst], [lnrow], bias=EPS6)
                                    for (t0, n) in BLOCKS:
                                        act(k, lnrow[:, t0:t0 + n], lnrow[:, t0:t0 + n], AF.Exp, [lnrow], [lnrow], scale=-0.5)
                                    for (t0, n) in BLOCKS:
                                        if kind == "dnq":
                                            stt(k, ob[:, t0:t0 + n], ar[:, t0:t0 + n], 128.0 ** -0.5, lnrow[:, t0:t0 + n], ALU.mult, ALU.mult, [ar, lnrow], [ob])
                                        else:
                                            tt(k, "dve", ob[:, t0:t0 + n], ar[:, t0:t0 + n], lnrow[:, t0:t0 + n], ALU.mult, [ar, lnrow], [ob])
                                dst = DNQKV.ap[{"dnq": 0, "dnk": 1, "dnv": 2}[kind], c4 * 128:(c4 + 1) * 128, :]
                                k.dma("sp", dst, ob[:], R=[ob], W=[DNQKV])
                            while pending:
                                pending.pop(0)()
                            pending.append(post)
                            continue
                        if ob is not None:
                            if kind in ("dnq", "dnk", "dnv"):
                                dst = DNQKV.ap[{"dnq": 0, "dnk": 1, "dnv": 2}[kind], c4 * 128:(c4 + 1) * 128, :]
                                dbuf = DNQKV
                            elif kind == "z":
                                dst, dbuf = ZS.ap[c4 * 128:(c4 + 1) * 128, :], ZS
                            elif kind == "lrug":
                                dst, dbuf = LG.ap[c4 * 128:(c4 + 1) * 128, :], LG
                            elif kind == "daq":
                                dst, dbuf = QR.ap[c4 * 128:(c4 + 1) * 128, :], QR
                            elif kind == "dak":
                                dst, dbuf = KR.ap[c4 * 128:(c4 + 1) * 128, :], KR
                            else:
                                g = int(kind[4:])
                                r0 = (g * 4 + c4) * 128
                                dst, dbuf = SG.ap[r0:r0 + 128, :], SG
                            k.dma("sp", dst, ob[:], R=[ob], W=[dbuf])
            with ExitStack() as e2:
                k.es = e2
                XPs = [k.sb([128, T + 6], BF16, "XP") for _ in range(2)]
                for x_ in XPs:
                    k.op("pool", lambda g: g.memset(x_[:], 0.0), W=[x_])
                XP = Rot(XPs)
                accrow = rot([128, T], F32, "accrow", 2)
                DgR = rot([128, 4, 128], BF16, "Dg", 2)
                pending = []
                sqrow = k.sb([128, T], BF16, "sqrow")
                lnrow = k.sb([128, T], F32, "lnrow")
                process(groups_conv)
                while pending:
                    pending.pop(0)()
                k.barrier()
            with ExitStack() as e3:
                k.es = e3
                cosT = k.sb([128, T], BF16, "cosT")
                sinT = k.sb([128, T], BF16, "sinT")
                k.dma("pool", cosT[:], c_rope.ap[0], R=[c_rope], W=[cosT])
                k.dma("pool", sinT[:], c_rope.ap[1], R=[c_rope], W=[sinT])
                qrawR = rot([128, 512], BF16, "qraw")
                t1R = rot([128, 512], F32, "t1")
                t2R = rot([128, 512], F32, "t2")
                vbR = rot([128, 512], BF16, "vb")
                abR = rot([4, 4, 512], F32, "abt", 2)
                eR = rot([4, 512], F32, "eab")
                process(groups_rest)
                k.barrier()
            k.es = esA
            k.barrier()
        k.es = es0

    def phase_C(l):
        with ExitStack() as es:
            k.es = es
            lv = k.sb([128, 3, 2, 4], F32, "lv")
            cneg = k.sb([128, 2, 4], F32, "cneg")
            k.dma("sp", lv[:], lru_vec.ap[l], R=[lru_vec], W=[lv])
            act(k, cneg[:], lv[:, 2], AF.Exp, [lv], [cneg], scale=-1.0)
            act(k, cneg[:], cneg[:], AF.Ln, [cneg, cst], [cneg], bias=ONE)
            ts(k, "dve", cneg[:], cneg[:], -8.0, None, ALU.mult, None, [cneg], [cneg])
            xc = k.sb([128, T], F32, "xc")
            xcb = k.sb([128, T], BF16, "xcb")
            gg = k.sb([128, T], BF16, "gg")
            A_s = [k.sb([128, T], F32, "lruA%d" % d_) for d_ in range(2)]
            IG_s = [k.sb([128, T], F32, "lruIG%d" % d_) for d_ in range(2)]
            W_s = [k.sb([128, T], F32, "lruW%d" % d_) for d_ in range(2)]
            H = [k.sb([128, T], F32, "lruH%d" % d) for d in range(2)]
            yb = k.sb([128, T], BF16, "lruY")
            wb = k.sb([128, 2, 2, 128], BF16, "lruwb")
            pg = Rot([k.ps([128, 512], F32, "pgC") for _ in range(4)])
            for cc in range(4):
                k.dma("sp", xc[:], LX.ap[cc * 128:(cc + 1) * 128, :], R=[LX], W=[xc])
                k.dma("sp", gg[:], LG.ap[cc * 128:(cc + 1) * 128, :], R=[LG], W=[gg])
                k.dma("pool", wb[:], lru_wblk.ap[l, :, :, cc].rearrange("a d i j -> i a d j"), R=[lru_wblk], W=[wb])
                cp(k, "act", xcb[:], xc[:], [xc], [xcb])
                for d in range(2):
                    A_, IG = A_s[d], IG_s[d]
                    for (t0, n) in BLOCKS:
                        p_ = pg.get()
                        mm(k, p_[:, :n], wb[:, 0, d, :], xcb[:, t0:t0 + n], True, True, [wb, xcb], [p_])
                        act(k, A_[:, t0:t0 + n], p_[:, :n], AF.Sigmoid, [p_, lv], [A_], bias=lv[:, 0, d, cc:cc + 1])
                        p_ = pg.get()
                        mm(k, p_[:, :n], wb[:, 1, d, :], xcb[:, t0:t0 + n], True, True, [wb, xcb], [p_])
                        act(k, IG[:, t0:t0 + n], p_[:, :n], AF.Sigmoid, [p_, lv], [IG], bias=lv[:, 1, d, cc:cc + 1])
                for d in range(2):
                    act(k, A_s[d][:], A_s[d][:], AF.Exp, [A_s[d], cneg], [A_s[d]], scale=cneg[:, d, cc:cc + 1])
                    tt(k, "pool", IG_s[d][:], IG_s[d][:], xc[:], ALU.mult, [IG_s[d], xc], [IG_s[d]])
                for d in range(2):
                    act(k, W_s[d][:], A_s[d][:], AF.Square, [A_s[d]], [W_s[d]])
                for d in range(2):
                    ts(k, "dve", W_s[d][:], W_s[d][:], -1.0, 1.0, ALU.mult, ALU.add, [W_s[d]], [W_s[d]])
                for d in range(2):
                    act(k, W_s[d][:], W_s[d][:], AF.Sqrt, [W_s[d]], [W_s[d]])
                for d in range(2):
                    tt(k, "dve", IG_s[d][:], IG_s[d][:], W_s[d][:], ALU.mult, [IG_s[d], W_s[d]], [IG_s[d]])
                for d in range(2):
                    A_, IG, Hd = A_s[d], IG_s[d], H[d]
                    if d == 0:
                        k.op("dve", lambda g: g.tensor_tensor_scan(out=Hd[:], data0=A_[:], data1=IG[:], initial=0.0, op0=ALU.mult, op1=ALU.add), R=[A_, IG], W=[Hd])
                    else:
                        k.op("dve", lambda g: g.tensor_tensor_scan(out=Hd[:, 0:NCTX][:, ::-1], data0=A_[:, 0:NCTX][:, ::-1], data1=IG[:, 0:NCTX][:, ::-1], initial=0.0, op0=ALU.mult, op1=ALU.add), R=[A_, IG], W=[Hd])
                        k.op("dve", lambda g: g.tensor_tensor_scan(out=Hd[:, NCTX:T][:, ::-1], data0=A_[:, NCTX:T][:, ::-1], data1=IG[:, NCTX:T][:, ::-1], initial=Hd[:, 0:1], op0=ALU.mult, op1=ALU.add), R=[A_, IG, Hd], W=[Hd])
                tt(k, "pool", H[0][:], H[0][:], H[1][:], ALU.add, [H[0], H[1]], [H[0]])
                tt(k, "dve", yb[:], H[0][:], gg[:], ALU.mult, [H[0], gg], [yb])
                k.dma("sp", YB.ap[cc * 128:(cc + 1) * 128, :], yb[:], R=[yb], W=[YB])
            k.barrier()
        k.es = es0

    def phase_B(l):
        with ExitStack() as es:
            k.es = es
            mb = k.sb([128, 2, 128], F32, "mb")
            nod = k.sb([128, 128], F32, "nod")
            sel = k.sb([4, 4, 128], F32, "sel")
            blk = k.sb([128, 5, 128], F32, "blk")
            k.dma("sp", mb[:], c_mb.ap.rearrange("d j i -> j d i"), R=[c_mb], W=[mb])
            k.dma("sp", nod[:], c_nod.ap, R=[c_nod], W=[nod])
            k.dma("sp", sel[:], c_sel.ap, R=[c_sel], W=[sel])
            k.dma("sp", blk[:], c_blk.ap.rearrange("m j i -> j m i"), R=[c_blk], W=[blk])
            precast_ffn(l)
            qv = DNQKV.ap.rearrange("w (h f) t -> w f h t", h=4)
            otv = OT.ap.rearrange("d (h f) t -> d f h t", h=4)
            gcv = GCB.ap
            order = {0: list(range(len(BLOCKS))), 1: [0] + list(range(len(BLOCKS) - 1, 0, -1))}
            B4 = [128, 4, 128]
            bc_h = lambda ap2: ap2.unsqueeze(1).broadcast_to(B4)
            bc_i = lambda ap2: ap2.unsqueeze(2).broadcast_to(B4)

            def dn_dir(d):
                def rot(shape, dt, name, n=1):
                    return Rot([k.sb(shape, dt, name + "%d" % d) for _ in range(n)])
                QTr = rot([128, 4, 512], BF16, "QT")
                KTr = rot([128, 4, 512], BF16, "KT")
                VTr = rot([128, 4, 512], BF16, "VTt")
                GBr = rot([4, 2, 512], F32, "GB")
                OBr = rot([128, 4, 512], F32, "OTb", 2)
                S_f = k.sb(B4, F32, "S32_%d" % d)
                S_b = k.sb(B4, BF16, "Sb_%d" % d)
                k.op("pool", lambda g: g.memset(S_f[:], 0.0), W=[S_f])
                k.op("pool", lambda g: g.memset(S_b[:], 0.0), W=[S_b])
                pf = Rot([k.ps(B4, F32, "pfB%d" % d) for _ in range(2)])
                prc = Rot([k.ps(B4, F32, "prB%d" % d) for _ in range(1)])
                pb = Rot([k.ps(B4, BF16, "pbB%d" % d) for _ in range(1)])
                gbc = rot([128, 8], F32, "gbc", 2)
                sc4 = rot([128, 4, 4], F32, "sc4", 3)
                D1 = rot(B4, F32, "D1")
                E = rot(B4, F32, "E", 2)
                nE = rot(B4, F32, "nE")
                EG = rot(B4, F32, "EG")
                Nr = rot(B4, F32, "N")
                L0r = rot(B4, F32, "L0")
                Ndr = rot(B4, F32, "Nd")
                Ldr = rot(B4, F32, "Ld")
                N2r = rot(B4, F32, "N2")
                L2r = rot(B4, F32, "L2")
                L4r = rot(B4, F32, "L4")
                Xr = rot(B4, F32, "X", 2)
                Lbr = rot(B4, F32, "Lb")
                XTr = rot(B4, F32, "XT")
                Yr = rot(B4, F32, "Y")
                Rr = rot(B4, BF16, "R", 2)
                QK = rot(B4, BF16, "QK", 3)
                kbg = rot(B4, BF16, "kbg", 2)
                kout = rot(B4, BF16, "kout", 3)
                vbt = rot(B4, BF16, "vbt", 2)
                wTr = rot(B4, BF16, "wT", 3)
                ur = rot(B4, F32, "u", 3)
                qin = rot(B4, BF16, "qin", 3)
                vnew = rot(B4, BF16, "vnew", 2)
                stmp = rot(B4, F32, "stmp")
                bm_ = lambda i_: bc_h(blk[:, i_, :])

                def mm4(p, lf, rf, R):
                    for h in range(4):
                        mm(k, p[:, h, :], lf(h), rf(h), True, True, R, [p])

                def tr4(p, src, ident, R):
                    for h in range(4):
                        tr(k, p[:, h, :], src(h), ident, R, [p])

                DEPTH = 2
                HQ = {}
                pdone = [0]
                rdone = [0]
                cidx = [0]

                def prep():
                  for step in range(len(BLOCKS)):
                      t0, n = BLOCKS[order[d][step]]
                      QT, KT, VTt, GB = QTr.get(), KTr.get(), VTr.get(), GBr.get()
                      k.dma("sp", QT[:, :, :n], qv[0, :, :, t0:t0 + n], R=[DNQKV], W=[QT])
                      k.dma("sp", KT[:, :, :n], qv[1, :, :, t0:t0 + n], R=[DNQKV], W=[KT])
                      k.dma("sp", VTt[:, :, :n], qv[2, :, :, t0:t0 + n], R=[DNQKV], W=[VTt])
                      k.dma("sp", GB[:, 0, :n], gcv[:, d, t0:t0 + n], R=[GCB], W=[GB])
                      k.dma("sp", GB[:, 1, :n], gcv[:, 2 + d, t0:t0 + n], R=[GCB], W=[GB])
                      yield
                      nch = n // 128
                      chs = list(range(nch)) if d == 0 else list(range(nch - 1, -1, -1))
                      lastc = 127 if d == 0 else 0
                      for ci_, c in enumerate(chs):
                          while cidx[0] - rdone[0] >= DEPTH:
                              yield
                          o = c * 128
                          sl = slice(o, o + 128)
                          p0 = pf.get()
                          tr(k, p0[:, 0, 0:4], GB[:, 0, sl], identF[0:4, 0:4], [GB, identF], [p0])
                          tr(k, p0[:, 0, 4:8], GB[:, 1, sl], identF[0:4, 0:4], [GB, identF], [p0])
                          g8 = gbc.get()
                          cp(k, "dve", g8[:], p0[:, 0, 0:8], [p0], [g8])
                          yield
                          GR = pf.get()
                          mm4(GR, lambda h: sel[:, h, :], lambda h: GB[:, 0, sl], [sel, GB])
                          d1 = D1.get()
                          tt(k, "dve", d1[:], GR[:], bc_i(g8[:, 0:4]), ALU.subtract, [GR, g8], [d1])
                          s4 = sc4.get()
                          act(k, s4[:, 0, :], g8[:, 0:4], AF.Exp, [g8], [s4])
                          yield
                          tt(k, "dve", s4[:, 1, :], s4[:, 0, :], g8[:, 4:8], ALU.mult, [s4, g8], [s4])
                          tt(k, "dve", s4[:, 2, :], GR[:, :, lastc], g8[:, 0:4], ALU.subtract, [GR, g8], [s4])
                          act(k, s4[:, 2, :], s4[:, 2, :], AF.Exp, [s4], [s4])
                          act(k, s4[:, 3, :], GR[:, :, lastc], AF.Exp, [GR], [s4])
                          eg_ = EG.get()
                          act(k, eg_[:], GR[:], AF.Exp, [GR], [eg_])
                          yield
                          tt(k, "pool", d1[:], d1[:], bc_h(mb[:, d, :]), ALU.add, [d1, mb], [d1])
                          e_ = E.get()
                          act(k, e_[:], d1[:], AF.Exp, [d1], [e_])
                          yield
                          ne_ = nE.get()
                          tt(k, "pool", ne_[:], e_[:], bc_h(nod[:]), ALU.mult, [e_, nod], [ne_])
                          BR = pf.get()
                          mm4(BR, lambda h: sel[:, h, :], lambda h: GB[:, 1, sl], [sel, GB])
                          tt(k, "dve", ne_[:], BR[:], ne_[:], ALU.mult, [BR, ne_], [ne_])
                          yield
                          KK = pf.get()
                          mm4(KK, lambda h: KT[:, h, sl], lambda h: KT[:, h, sl], [KT])
                          N_ = Nr.get()
                          tt(k, "dve", N_[:], KK[:], ne_[:], ALU.mult, [KK, ne_], [N_])
                          yield
                          KQ = pf.get()
                          mm4(KQ, lambda h: KT[:, h, sl], lambda h: QT[:, h, sl], [KT, QT])
                          qk = QK.get()
                          tt(k, "dve", qk[:], KQ[:], e_[:], ALU.mult, [KQ, e_], [qk])
                          yield
                          pL = pf.get()
                          tr4(pL, lambda h: N_[:, h, :], identF[:], [N_, identF])
                          L0 = L0r.get()
                          cp(k, "act", L0[:], pL[:], [pL], [L0])
                          Nd = Ndr.get()
                          tt(k, "pool", Nd[:], N_[:], bm_(0), ALU.mult, [N_, blk], [Nd])
                          yield
                          Ld = Ldr.get()
                          tt(k, "pool", Ld[:], L0[:], bm_(0), ALU.mult, [L0, blk], [Ld])
                          yield
                          pA = pf.get()
                          mm4(pA, lambda h: Ld[:, h, :], lambda h: Nd[:, h, :], [Ld, Nd])
                          N2 = N2r.get()
                          cp(k, "act", N2[:], pA[:], [pA], [N2])
                          pA = pf.get()
                          mm4(pA, lambda h: Nd[:, h, :], lambda h: Ld[:, h, :], [Ld, Nd])
                          L2 = L2r.get()
                          cp(k, "dve", L2[:], pA[:], [pA], [L2])
                          X = Xr.get()
                          tt(k, "pool", X[:], Nd[:], bc_h(identF[:]), ALU.add, [Nd, identF], [X])
                          yield
                          pA = pf.get()
                          mm4(pA, lambda h: N2[:, h, :], lambda h: L2[:, h, :], [N2, L2])
                          L4 = L4r.get()
                          cp(k, "act", L4[:], pA[:], [pA], [L4])
                          yield
                          for Lk in (L2, L4):
                              pA = pf.get()
                              mm4(pA, lambda h: Lk[:, h, :], lambda h: X[:, h, :], [Lk, X])
                              X2 = Xr.get()
                              tt(k, "dve", X2[:], pA[:], X[:], ALU.add, [pA, X], [X2])
                              X = X2
                              yield
                          for lev in range(1, 5):
                              Lb = Lbr.get()
                              tt(k, "pool", Lb[:], L0[:], bm_(lev), ALU.mult, [L0, blk], [Lb])
                              pA = pf.get()
                              tr4(pA, lambda h: X[:, h, :], identF[:], [X, identF])
                              XT = XTr.get()
                              cp(k, "act", XT[:], pA[:], [pA], [XT])
                              yield
                              pA = pf.get()
                              mm4(pA, lambda h: Lb[:, h, :], lambda h: X[:, h, :], [Lb, X])
                              Y = Yr.get()
                              cp(k, "dve", Y[:], pA[:], [pA], [Y])
                              yield
                              pA = pf.get()
                              mm4(pA, lambda h: XT[:, h, :], lambda h: Y[:, h, :], [XT, Y])
                              if lev < 4:
                                  X2 = Xr.get()
                                  tt(k, "dve", X2[:], pA[:], X[:], ALU.add, [pA, X], [X2])
                                  X = X2
                              else:
                                  R_ = Rr.get()
                                  tt(k, "dve", R_[:], pA[:], X[:], ALU.add, [pA, X], [R_])
                              yield
                          pK = pb.get()
                          tr4(pK, lambda h: KT[:, h, sl], identB[:], [KT, identB])
                          kb_ = kbg.get()
                          tt(k, "dve", kb_[:], pK[:], bc_i(s4[:, 1, :]), ALU.mult, [pK, s4], [kb_])
                          ko_ = kout.get()
                          tt(k, "dve", ko_[:], pK[:], bc_i(s4[:, 2, :]), ALU.mult, [pK, s4], [ko_])
                          yield
                          pV = pb.get()
                          tr4(pV, lambda h: VTt[:, h, sl], identB[:], [VTt, identB])
                          vb_ = vbt.get()
                          tt(k, "dve", vb_[:], pV[:], bc_i(g8[:, 4:8]), ALU.mult, [pV, g8], [vb_])
                          yield
                          pW = pf.get()
                          mm4(pW, lambda h: kb_[:, h, :], lambda h: R_[:, h, :], [kb_, R_])
                          wT = wTr.get()
                          cp(k, "act", wT[:], pW[:], [pW], [wT])
                          yield
                          pUu = pf.get()
                          mm4(pUu, lambda h: R_[:, h, :], lambda h: vb_[:, h, :], [vb_, R_])
                          u_ = ur.get()
                          cp(k, "dve", u_[:], pUu[:], [pUu], [u_])
                          qi = qin.get()
                          tt(k, "pool", qi[:], QT[:, :, sl], eg_[:], ALU.mult, [QT, eg_], [qi])
                          yield
                          HQ[cidx[0]] = dict(wT=wT, u_=u_, qi=qi, qk=qk, ko_=ko_, s4=s4, t0=t0, n=n, sl=sl, first=(ci_ == 0), last=(ci_ == len(chs) - 1))
                          cidx[0] += 1
                          pdone[0] = cidx[0]
                          yield

                def recur():
                    OTb = None
                    for idx in range(34):
                        while pdone[0] <= idx:
                            yield
                        hq = HQ.pop(idx)
                        wT, u_, qi, qk, ko_, s4, t0, n, sl = hq["wT"], hq["u_"], hq["qi"], hq["qk"], hq["ko_"], hq["s4"], hq["t0"], hq["n"], hq["sl"]
                        if hq["first"]:
                            OTb = OBr.get()
                        pWS = prc.get()
                        mm4(pWS, lambda h: wT[:, h, :], lambda h: S_b[:, h, :], [wT, S_b])
                        vn = vnew.get()
                        tt(k, "dve", vn[:], u_[:], pWS[:], ALU.subtract, [u_, pWS], [vn])
                        yield
                        pO = prc.get()
                        for h in range(4):
                            mm(k, pO[:, h, :], S_b[:, h, :], qi[:, h, :], True, False, [S_b, qi], [pO])
                            mm(k, pO[:, h, :], vn[:, h, :], qk[:, h, :], False, True, [vn, qk], [pO])
                        cp(k, "act", OTb[:, :, sl], pO[:], [pO], [OTb])
                        yield
                        pS = prc.get()
                        mm4(pS, lambda h: ko_[:, h, :], lambda h: vn[:, h, :], [ko_, vn])
                        st_ = stmp.get()
                        tt(k, "pool", st_[:], S_f[:], bc_i(s4[:, 3, :]), ALU.mult, [S_f, s4], [st_])
                        tt(k, "dve", S_f[:], st_[:], pS[:], ALU.add, [st_, pS], [S_f])
                        cp(k, "act", S_b[:], S_f[:], [S_f], [S_b])
                        yield
                        if hq["last"]:
                            k.dma("sp", otv[d, :, :, t0:t0 + n], OTb[:, :, :n], R=[OTb], W=[OT])
                        rdone[0] = idx + 1
                        yield

                return prep(), recur()

            gens = list(dn_dir(0)) + list(dn_dir(1))
            alive = [True] * len(gens)
            while any(alive):
                for gi in range(len(gens)):
                    if alive[gi]:
                        try:
                            next(gens[gi])
                        except StopIteration:
                            alive[gi] = False
            k.barrier()
        k.es = es0

    def phase_D(l):
        lam_init = 0.8 - 0.6 * math.exp(-0.3 * l)
        with ExitStack() as es:
            k.es = es
            krv = KR.ap.rearrange("(h f) t -> f h t", h=4)
            KS = [k.sb([128, 4, T], BF16, "KS%d" % s_) for s_ in range(2)]
            QRa = k.sb([128, 4, T], BF16, "QRa")
            Va = k.sb([128, T // 128, 512], BF16, "Va")
            k.op("pool", lambda g: g.memset(KS[0][64:128], 0.0), W=[KS[0]])
            k.op("pool", lambda g: g.memset(KS[1][0:64], 0.0), W=[KS[1]])
            k.dma("sp", KS[0][0:64], krv[0:64], R=[KR], W=[KS[0]])
            k.dma("sp", KS[1][64:128], krv[64:128], R=[KR], W=[KS[1]])
            k.dma("sp", QRa[:], QR.ap.rearrange("(h f) t -> f h t", h=4), R=[QR], W=[QRa])
            vtv = VT.ap.rearrange("(kt p) v -> p kt v", p=128)
            for g in range(0, T // 128, 6):
                g1 = min(g + 6, T // 128)
                k.dma("sp", Va[:, g:g1, :], vtv[:, g:g1, :], R=[VT], W=[Va])
            dl = k.sb([128, 2, 2, 64], F32, "dl")
            k.dma("sp", dl[:].rearrange("p a b f -> p (a b f)"), da_lam.ap[l:l + 1, :].partition_broadcast(128) if False else da_lam.ap[l].partition_broadcast(128), R=[da_lam], W=[dl])
            pr = k.sb([128, 2, 64], F32, "pr")
            sv = k.sb([128, 4], F32, "sv")
            dnw = k.sb([128, 1], F32, "dnw")
            k.dma("sp", dnw[:], da_normT.ap[l], R=[da_normT], W=[dnw])
            tt(k, "dve", pr[:], dl[:, :, 0, :], dl[:, :, 1, :], ALU.mult, [dl], [pr])
            k.op("dve", lambda g: g.tensor_reduce(out=sv[:, 0:2], in_=pr[:], axis=mybir.AxisListType.X, op=ALU.add), R=[pr], W=[sv])
            act(k, sv[:, 0:2], sv[:, 0:2], AF.Exp, [sv], [sv])
            tt(k, "dve", sv[:, 2:3], sv[:, 1:2], sv[:, 0:1], ALU.subtract, [sv], [sv])
            ts(k, "dve", sv[:, 2:3], sv[:, 2:3], -lam_init, None, ALU.add, None, [sv], [sv])
            ts(k, "dve", dnw[:], dnw[:], 1.0 - lam_init, None, ALU.mult, None, [dnw], [dnw])
            pst = Rot([k.ps([128, 512], F32, "pst") for _ in range(3)])
            poS = [k.ps([128, 512], F32, "po%d" % s_) for s_ in range(2)]
            plS = [k.ps([128, 512], F32, "pl%d" % s_) for s_ in range(2)]
            pn = k.ps([128, 512], F32, "pn")
            ptR = Rot([k.sb([128, 512], BF16, "pt") for _ in range(6)])
            oS = [Rot([k.sb([128, 512], F32, "oS%d" % s_) for _ in range(2)]) for s_ in range(2)]
            lS = [Rot([k.sb([128, 512], F32, "lS%d" % s_) for _ in range(2)]) for s_ in range(2)]
            accR = Rot([k.sb([128, 512], F32, "accD") for _ in range(2)])
            sqdR = Rot([k.sb([128, 512], BF16, "sqD") for _ in range(2)])
            rnR = Rot([k.sb([128, 512], F32, "rnD") for _ in range(2)])
            deferred = []
            ycR = Rot([k.sb([128, T], BF16, "ycrow") for _ in range(2)])
            LOOK = 2
            for h in range(4):
                yc = ycR.get()
                for (t0, n) in BLOCKS:
                    kts = [0, 1] if t0 < NCTX else list(range(T // 128))
                    items = [(s_, i_, kt) for i_, kt in enumerate(kts) for s_ in range(2)]
                    pend = []

                    def flush_one():
                        s_, i_, kt, pt_ = pend.pop(0)
                        mm(k, poS[s_][:, :n], Va[:, kt, h * 128:(h + 1) * 128], pt_[:, :n], i_ == 0, i_ == len(kts) - 1, [Va, pt_], [poS[s_]])
                        mm(k, plS[s_][:, :n], onesB[:], pt_[:, :n], i_ == 0, i_ == len(kts) - 1, [onesB, pt_], [plS[s_]])
                    for it_, (s_, i_, kt) in enumerate(items):
                        st = pst.get()
                        mm(k, st[:, :n], KS[s_][:, h, kt * 128:(kt + 1) * 128], QRa[:, h, t0:t0 + n], True, True, [KS[s_], QRa], [st])
                        pt_ = ptR.get()
                        act(k, pt_[:, :n], st[:, :n], AF.Exp, [st], [pt_], scale=0.125)
                        pend.append((s_, i_, kt, pt_))
                        if len(pend) > LOOK:
                            flush_one()
                        if it_ == 12:
                            while deferred:
                                deferred.pop(0)()
                    while pend:
                        flush_one()
                    while deferred:
                        deferred.pop(0)()
                    os_, ls_ = [], []
                    for s_ in range(2):
                        o_ = oS[s_].get()
                        l_ = lS[s_].get()
                        cp(k, "dve", o_[:, :n], poS[s_][:, :n], [poS[s_]], [o_])
                        cp(k, "dve", l_[:, :n], plS[s_][:, :n], [plS[s_]], [l_])
                        os_.append(o_)
                        ls_.append(l_)
                    for s_ in range(2):
                        k.op("dve", lambda g: g.reciprocal(out=ls_[s_][:, :n], in_=ls_[s_][:, :n]), R=[ls_[s_]], W=[ls_[s_]])
                    acc, sqd, rn = accR.get(), sqdR.get(), rnR.get()
                    tt(k, "dve", acc[:, :n], os_[0][:, :n], ls_[0][:, :n], ALU.mult, [os_[0], ls_[0]], [acc])
                    tt(k, "pool", os_[1][:, :n], os_[1][:, :n], ls_[1][:, :n], ALU.mult, [os_[1], ls_[1]], [os_[1]])
                    stt(k, acc[:, :n], os_[1][:, :n], sv[:, 2:3], acc[:, :n], ALU.mult, ALU.add, [os_[1], sv, acc], [acc])
                    k.op("pool", lambda g: g.tensor_tensor(out=sqd[:, :n], in0=acc[:, :n], in1=acc[:, :n], op=ALU.mult), R=[acc], W=[sqd])

                    def part2(acc=acc, sqd=sqd, rn=rn, yc=yc, t0=t0, n=n):
                        mm(k, pn[:, :n], onesB[:], sqd[:, :n], True, True, [onesB, sqd], [pn])
                        act(k, rn[:, :n], pn[:, :n], AF.Sqrt, [pn, cst], [rn], scale=1.0 / 128, bias=EPS5)
                        k.op("dve", lambda g: g.reciprocal(out=rn[:, :n], in_=rn[:, :n]), R=[rn], W=[rn])
                        stt(k, yc[:, t0:t0 + n], acc[:, :n], dnw[:, 0:1], rn[:, :n], ALU.mult, ALU.mult, [acc, dnw, rn], [yc])
                    deferred.append(part2)
                while deferred:
                    deferred.pop(0)()
                k.dma("sp", YC.ap[h * 128:(h + 1) * 128, :], yc[:], R=[yc], W=[YC])
            k.barrier()
        k.es = es0

    def phase_E(l, last):
        with ExitStack() as es:
            k.es = es
            Wbr = k.sb([128, 3, 4, D], BF16, "Wbr")
            Wo = k.sb([128, 8, D], BF16, "Wo")
            for nb in range(3):
                k.dma("pool", Wbr[:, nb], w_branch.ap[l, nb].rearrange("(k p) c -> p k c", p=128), R=[w_branch], W=[Wbr])
            for h2 in range(2):
                k.dma("pool", Wo[:, h2 * 4:h2 * 4 + 4, :], w_out.ap[l].rearrange("(k p) c -> p k c", p=128)[:, h2 * 4:h2 * 4 + 4, :], R=[w_out], W=[Wo])
            dnw = k.sb([128, 1], F32, "dnwE")
            k.dma("sp", dnw[:], dn_normT.ap[l], R=[dn_normT], W=[dnw])
            WguR = Rot([k.sb([128, 8, 256], BF16, "Wgu") for _ in range(4)])
            WdR = Rot([k.sb([128, 22, 128], BF16, "Wd") for _ in range(2)])
            tmp = k.sb([128, 8, 512], F32, "tmpE")
            sq = k.sb([128, 8, 512], BF16, "sqE")
            zs = k.sb([128, 4, 512], BF16, "zsE")
            ya = k.sb([128, 4, 512], BF16, "yaE")
            yb = k.sb([128, 4, 512], BF16, "ybE")
            yc = k.sb([128, 4, 512], BF16, "ycE")
            sgR = Rot([k.sb([128, 3, 512], BF16, "sgE") for _ in range(2)])
            merged = k.sb([128, 8, 512], BF16, "mergedE")
            HB = [k.sb([128, 8, 512], F32, "hbE%d" % i) for i in range(2)]
            U2 = [k.sb([128, 8, 512], BF16, "u2E%d" % i) for i in range(2)]
            hid = k.sb([128, 22, 512], BF16, "hidE")
            rs = k.sb([128, 512], F32, "rsE")
            rs2 = k.sb([128, 512], F32, "rs2E")
            macc = k.sb([128, 512], F32, "maccE")
            mt = k.sb([128, 512], F32, "mtE")
            sgl = Rot([k.sb([128, 512], F32, "sglE") for _ in range(2)])
            otbufs = [Buf(hid.ap[:, 8 + 4 * i:12 + 4 * i, :].bitcast(F32).rearrange("p a b -> p (a b)"), "otE%d" % i) for i in range(2)]
            otile = Rot(otbufs)
            pg1 = Rot([k.ps([128, 512], F32, "pgE1") for _ in range(3)])
            pg2 = Rot([k.ps([128, 512], F32, "pgE2") for _ in range(3)])
            pss = k.ps([128, 512], F32, "pssE")
            ppT = k.ps([128, 4, 128], F32, "ppTE")
            ytv = lambda Y: Y.ap.rearrange("(k p) t -> p k t", p=128)
            otv = OT.ap.rearrange("d (h f) t -> d f h t", h=4)
            sgv = SG.ap.rearrange("(nb dc p) t -> p nb dc t", nb=3, dc=8)
            wgv = w_g.ap[l].rearrange("(k p) c -> p k c", p=128)
            wuv = w_u.ap[l].rearrange("(k p) c -> p k c", p=128)
            wdv = w_d.ap[l].rearrange("(fc p) c -> p fc c", p=128)
            blocks = [bk for bk in BLOCKS if not (last and bk[0] < NCTX)]

            def stage1(bi):
                t0, n = blocks[bi]
                j = 0 if t0 < NCTX else 1
                hb, u2 = HB[bi % 2], U2[bi % 2]
                k.dma("sp", tmp[:, 0:4, :n], otv[0, :, :, t0:t0 + n], R=[OT], W=[tmp])
                k.dma("sp", tmp[:, 4:8, :n], otv[1, :, :, t0:t0 + n], R=[OT], W=[tmp])
                k.dma("sp", zs[:, :, :n], ytv(ZS)[:, :, t0:t0 + n], R=[ZS], W=[zs])
                k.dma("sp", yb[:, :, :n], ytv(YB)[:, :, t0:t0 + n], R=[YB], W=[yb])
                k.dma("sp", yc[:, :, :n], ytv(YC)[:, :, t0:t0 + n], R=[YC], W=[yc])
                k.dma("sp", hb[:, :, :n], hTv[:, :, t0:t0 + n], R=[hT], W=[hb])
                yield
                tt(k, "pool", tmp[:, 0:4, :n], tmp[:, 0:4, :n], tmp[:, 4:8, :n], ALU.add, [tmp], [tmp])
                act(k, sq[:, 0:4, :n], tmp[:, 0:4, :n], AF.Square, [tmp], [sq])
                yield
                for h in range(4):
                    mm(k, pss[:, :n], onesB[:], sq[:, h, :n], True, True, [onesB, sq], [pss])
                    act(k, rs[:, :n], pss[:, :n], AF.Sqrt, [pss, cst], [rs], scale=1.0 / 128, bias=EPS6)
                    k.op("dve", lambda g: g.reciprocal(out=rs[:, :n], in_=rs[:, :n]), R=[rs], W=[rs])
                    tt(k, "dve", tmp[:, 4 + h, :n], tmp[:, h, :n], rs[:, :n], ALU.mult, [tmp, rs], [tmp])
                    stt(k, ya[:, h, :n], tmp[:, 4 + h, :n], dnw[:, 0:1], zs[:, h, :n], ALU.mult, ALU.mult, [tmp, dnw, zs], [ya])
                    yield
                for _ in range(4):
                    yield
                ys = [ya, yb, yc]
                for dc in range(8):
                    sg = sgR.get()
                    k.dma("sp", sg[:, :, :n], sgv[:, :, dc, t0:t0 + n], R=[SG], W=[sg])
                    for nb in range(3):
                        pu = pg1.get()
                        for k4 in range(4):
                            mm(k, pu[:, :n], Wbr[:, nb, k4, dc * 128:(dc + 1) * 128], ys[nb][:, k4, :n], k4 == 0, k4 == 3, [Wbr, ys[nb]], [pu])
                        if nb == 0:
                            tt(k, "dve", macc[:, :n], pu[:, :n], sg[:, 0, :n], ALU.mult, [pu, sg], [macc])
                        else:
                            tt(k, "dve", mt[:, :n], pu[:, :n], sg[:, nb, :n], ALU.mult, [pu, sg], [mt])
                            if nb == 1:
                                tt(k, "pool", macc[:, :n], macc[:, :n], mt[:, :n], ALU.add, [macc, mt], [macc])
                            else:
                                tt(k, "pool", merged[:, dc, :n], macc[:, :n], mt[:, :n], ALU.add, [macc, mt], [merged])
                        yield
                for dc in range(8):
                    py = pg1.get()
                    for k8 in range(8):
                        mm(k, py[:, :n], Wo[:, k8, dc * 128:(dc + 1) * 128], merged[:, k8, :n], k8 == 0, k8 == 7, [Wo, merged], [py])
                    stt(k, hb[:, dc, :n], py[:, :n], AB[:, l, 2, dc, j:j + 1], hb[:, dc, :n], ALU.mult, ALU.add, [py, AB, hb], [hb])
                    yield
                act(k, sq[:, :, :n], hb[:, :, :n], AF.Square, [hb], [sq])
                yield
                for kk in range(8):
                    mm(k, pss[:, :n], onesB[:], sq[:, kk, :n], kk == 0, kk == 7, [onesB, sq], [pss])
                act(k, rs[:, :n], pss[:, :n], AF.Sqrt, [pss, cst], [rs], scale=1.0 / D, bias=EPS6)
                k.op("dve", lambda g: g.reciprocal(out=rs[:, :n], in_=rs[:, :n]), R=[rs], W=[rs])
                yield
                for kk in range(8):
                    tt(k, "dve", tmp[:, kk, :n], hb[:, kk, :n], rs[:, :n], ALU.mult, [hb, rs], [tmp])
                    act(k, u2[:, kk, :n], tmp[:, kk, :n], AF.Identity, [tmp, AB], [u2],
                        scale=AB[:, l, 3, kk, j:j + 1], bias=AB[:, l, 4, kk, j:j + 1])
                    if kk % 2 == 1:
                        yield

            def stage2(bi):
                t0, n = blocks[bi]
                j = 0 if t0 < NCTX else 1
                hb, u2 = HB[bi % 2], U2[bi % 2]
                for f2 in range(11):
                    wg_ = WguR.get()
                    k.dma("sp", wg_[:], WGT.ap[f2], R=[WGT], W=[wg_])
                    wu_ = WguR.get()
                    k.dma("sp", wu_[:], WUT.ap[f2], R=[WUT], W=[wu_])
                    for c2 in range(2):
                        fc = f2 * 2 + c2
                        pg_ = pg2.get()
                        for kk in range(8):
                            mm(k, pg_[:, :n], wg_[:, kk, c2 * 128:(c2 + 1) * 128], u2[:, kk, :n], kk == 0, kk == 7, [wg_, u2], [pg_])
                        pu_ = pg2.get()
                        for kk in range(8):
                            mm(k, pu_[:, :n], wu_[:, kk, c2 * 128:(c2 + 1) * 128], u2[:, kk, :n], kk == 0, kk == 7, [wu_, u2], [pu_])
                        sg_ = sgl.get()
                        act(k, sg_[:, :n], pg_[:, :n], AF.Silu, [pg_], [sg_])
                        tt(k, "dve", hid[:, fc, :n], sg_[:, :n], pu_[:, :n], ALU.mult, [sg_, pu_], [hid] + (otbufs if (last and 8 <= fc < 16) else []))
                        yield
                for dc in range(8):
                    wd_ = WdR.get()
                    k.dma("sp", wd_[:], WDT.ap[dc], R=[WDT], W=[wd_])
                    py = pg2.get()
                    for fc in range(22):
                        mm(k, py[:, :n], wd_[:, fc, :], hid[:, fc, :n], fc == 0, fc == 21, [wd_, hid], [py])
                    stt(k, hb[:, dc, :n], py[:, :n], AB[:, l, 5, dc, j:j + 1], hb[:, dc, :n], ALU.mult, ALU.add, [py, AB, hb], [hb])
                    yield
                if not last:
                    k.dma("sp", hTv[:, :, t0:t0 + n], hb[:, :, :n], R=[hb], W=[hT])
                    yield
                else:
                    sq2 = hid
                    act(k, sq2[:, 0:8, :n], hb[:, :, :n], AF.Square, [hb], [sq2])
                    for kk in range(8):
                        mm(k, ppT[:].rearrange("p a b -> p (a b)")[:, :n], onesB[:], sq2[:, kk, :n], kk == 0, kk == 7, [onesB, sq2], [ppT])
                    act(k, rs2[:, :n], ppT[:].rearrange("p a b -> p (a b)")[:, :n], AF.Sqrt, [ppT, cst], [rs2], scale=1.0 / D, bias=EPS6)
                    k.op("dve", lambda g: g.reciprocal(out=rs2[:, :n], in_=rs2[:, :n]), R=[rs2], W=[rs2])
                    yield
                    for kk in range(8):
                        stt(k, hb[:, kk, :n], hb[:, kk, :n], nrmF[:, kk:kk + 1], rs2[:, :n], ALU.mult, ALU.mult, [hb, nrmF, rs2], [hb])
                    yield
                    for q in range(n // 128):
                        ot = otile.get()
                        for half in range(2):
                            for qq in range(4):
                                kk = half * 4 + qq
                                tr(k, ppT[:, qq, :], hb[:, kk, q * 128:(q + 1) * 128], identF[:], [hb, identF], [ppT])
                            cp(k, k.alt(), ot[:, half * 512:(half + 1) * 512], ppT[:].rearrange("p a b -> p (a b)"), [ppT], [ot, hid])
                        r0 = t0 - NCTX + q * 128
                        k.dma("sp", out_d.ap[r0:r0 + 128, :], ot[:], R=[ot], W=[out_d])
                        yield

            def run_pair(g1, g2):
                gens = [g for g in (g1, g2) if g is not None]
                alive = [True] * len(gens)
                while any(alive):
                    for gi in range(len(gens)):
                        if alive[gi]:
                            try:
                                next(gens[gi])
                            except StopIteration:
                                alive[gi] = False
            nbk = len(blocks)
            run_pair(stage1(0), None)
            for bi in range(nbk):
                run_pair(stage2(bi), stage1(bi + 1) if bi + 1 < nbk else None)
            k.barrier()
        k.es = es0

    done = False
    for l in range(DEPTH):
        last = (l == DEPTH - 1)
        for name, fn in (("A", lambda: phase_A(l)), ("C", lambda: phase_C(l)), ("B", lambda: phase_B(l)),
                         ("D", lambda: phase_D(l)), ("E", lambda: phase_E(l, last))):
            fn()
            if stop_after == "%s%d" % (name, l):
                done = True
                break
        if done:
            break
    k.barrier()
    es0.close()
    return nc, k


def _consts():
    c = {}
    c["c_ident"] = np.eye(128, dtype=np.float32)
    inv = (10000.0 ** (-np.arange(16, dtype=np.float32) / 16)).astype(np.float32)
    lt = np.arange(NLAT)
    row = (lt // 64).astype(np.float32)
    col = (lt % 64).astype(np.float32)
    cos = np.ones((128, T), np.float32)
    sin = np.zeros((128, T), np.float32)
    perm = np.zeros((128, 128), np.float32)
    for p in range(128):
        e = p % 64
        axis = e // 32
        half = (e % 32) // 16
        f = e % 16
        pos = row if axis == 0 else col
        ang = (pos * inv[f]).astype(np.float32)
        cos[p, NCTX:] = np.cos(ang)
        sn = np.sin(ang)
        sin[p, NCTX:] = -sn if half == 0 else sn
        partner = p + 16 if half == 0 else p - 16
        perm[partner, p] = 1.0
    c["c_rope"] = np.stack([cos, sin]).astype(np.float32)
    c["c_perm"] = perm
    jj, ii = np.meshgrid(np.arange(128), np.arange(128), indexing="ij")
    mb = np.stack([np.where(jj <= ii, 0.0, NEG), np.where(jj >= ii, 0.0, NEG)]).astype(np.float32)
    c["c_mb"] = mb
    c["c_nod"] = (-(1.0 - np.eye(128))).astype(np.float32)
    m01 = np.ones((4, 2, 512), np.float32)
    m01[:, 0, 0::128] = 0.0
    m01[:, 1, 127::128] = 0.0
    c["c_m01"] = m01
    sel = np.zeros((4, 4, 128), np.float32)
    for h in range(4):
        sel[h, h, :] = 1.0
    c["c_sel"] = sel
    bl = np.zeros((5, 128, 128), np.float32)
    bl[0] = (jj // 8 == ii // 8)
    for m_, b_ in enumerate((8, 16, 32, 64)):
        bl[m_ + 1] = (jj // (2 * b_) == ii // (2 * b_)) & (jj // b_ != ii // b_)
    c["c_blk"] = bl
    return c


def _prep_shared(inp):
    f = lambda a: np.ascontiguousarray(a, dtype=np.float32)
    s = {}
    s["w_mod"] = f(inp["w_mod"])
    s["b_modT"] = f(inp["b_mod"].reshape(DEPTH, 48, 128).transpose(0, 2, 1))
    nm = np.stack([inp["norm_mix"], inp["norm_ffn"]], 1)
    s["normsT"] = f(nm.reshape(DEPTH, 2, 8, 128).transpose(3, 0, 1, 2))
    s["normfT"] = f(inp["norm_final"].reshape(8, 128).T)
    s["w_in"] = f(inp["w_in"])
    s["dn_convT"] = f(inp["dn_conv"].reshape(DEPTH, 4, 12, 128).transpose(0, 3, 2, 1))
    s["dnab"] = f(np.concatenate([inp["dn_a_log"].transpose(0, 2, 1), inp["dn_dt_bias"].transpose(0, 2, 1)], -1))
    s["dn_normT"] = f(inp["dn_norm"].reshape(DEPTH, 128, 1))
    s["lru_cw"] = f(inp["lru_conv_w"].reshape(DEPTH, 4, 4, 128).transpose(0, 3, 2, 1))
    s["lru_cb"] = f(inp["lru_conv_b"].reshape(DEPTH, 4, 128).transpose(0, 2, 1))
    lv = np.stack([inp["lru_ba"], inp["lru_bi"], inp["lru_lambda"]], 1)
    s["lru_vec"] = f(lv.reshape(DEPTH, 3, 2, 4, 128).transpose(0, 4, 1, 2, 3))
    wb = np.zeros((DEPTH, 2, 2, 4, 128, 128), np.float32)
    for ai, w in enumerate([inp["lru_wa"], inp["lru_wi"]]):
        for cc in range(4):
            for g2 in range(2):
                wb[:, ai, :, cc, g2 * 64:(g2 + 1) * 64, g2 * 64:(g2 + 1) * 64] = w[:, :, cc * 2 + g2]
    s["lru_wblk"] = wb
    s["da_lam"] = f(inp["da_lambda"].reshape(DEPTH, 256))
    s["da_normT"] = f(inp["da_norm"].reshape(DEPTH, 128, 1))
    s["w_branch"] = f(inp["w_branch"])
    s["w_out"] = f(inp["w_out"])
    s["w_ffn_gate"] = f(inp["w_ffn_gate"])
    s["w_ffn_up"] = f(inp["w_ffn_up"])
    s["w_ffn_down"] = f(inp["w_ffn_down"])
    s.update(_consts())
    return s


def _prep_core(inp, b):
    m = {}
    m["xin"] = np.ascontiguousarray(np.concatenate([inp["ctx"][b], inp["x"][b]], 0), dtype=np.float32)
    cv = np.stack([inp["c_ctx"], inp["c"][b]], 0)
    m["cT"] = np.ascontiguousarray(cv.reshape(2, 8, 128).transpose(2, 1, 0), dtype=np.float32)
    return m


_CACHE = {}


def kernel(**inputs):
    inp = {k_: np.asarray(v) for k_, v in inputs.items()}
    if "nc" not in _CACHE:
        _CACHE["nc"] = build()[0]
    nc = _CACHE["nc"]
    shared = _prep_shared(inp)
    in_maps = []
    for b in range(8):
        m = dict(shared)
        m.update(_prep_core(inp, b))
        in_maps.append(m)
    res = run_bass_kernel_spmd(nc, in_maps, core_ids=list(range(8)))
    return np.stack([np.asarray(r["out"], dtype=np.float32) for r in res.results], 0)
```

```python
import math
import numpy as np
from contextlib import ExitStack
import concourse.bass as bass
import concourse.mybir as mybir
from concourse.bass_utils import run_bass_kernel_spmd
from concourse.alu_op_type import AluOpType as ALU

AF = mybir.ActivationFunctionType
F32 = mybir.dt.float32
BF16 = mybir.dt.bfloat16

D = 1024
NCTX = 256
NLAT = 4096
T = NCTX + NLAT
DEPTH = 2
DFF = 2816
INC = 7696
BLOCKS = [(0, 256)] + [(256 + 512 * j, 512) for j in range(8)]
NEG = -30000.0


class Buf:
    def __init__(self, ap, name, share=None):
        self.ap = ap
        self.name = name
        self.st = share.st if share is not None else [None, []]

    @property
    def w(self):
        return self.st[0]

    @w.setter
    def w(self, v):
        self.st[0] = v

    @property
    def r(self):
        return self.st[1]

    @r.setter
    def r(self, v):
        self.st[1] = v

    def __getitem__(self, k):
        return self.ap[k]


class K:
    NDMA = 8

    def __init__(self, nc, es):
        self.nc = nc
        self.es = es
        self.eng = {"pe": nc.tensor, "act": nc.scalar, "dve": nc.vector,
                    "pool": nc.gpsimd, "sp": nc.sync}
        self.sem = {}
        self.cnt = {}
        for e in self.eng:
            self.sem[e] = es.enter_context(nc.semaphore("s_" + e))
            self.cnt[e] = 0
        self.dring = {}
        for q in ("sp", "pool"):
            ring = []
            for i in range(self.NDMA):
                key = "d_%s%d" % (q, i)
                self.sem[key] = es.enter_context(nc.semaphore(key))
                self.cnt[key] = 0
                ring.append(key)
            self.dring[q] = ring
        self.dpos = {q: 0 for q in self.dring}
        self.seen = {e: {} for e in self.eng}
        self.nbuf = 0
        self.ninst = 0
        self.rr = 0

    def sb(self, shape, dt, name=None):
        self.nbuf += 1
        name = (name or "sb") + "_%d" % self.nbuf
        t = self.es.enter_context(self.nc.sbuf_tensor(name, list(shape), dt))
        return Buf(t, name)

    def ps(self, shape, dt, name=None):
        self.nbuf += 1
        name = (name or "ps") + "_%d" % self.nbuf
        t = self.es.enter_context(self.nc.psum_tensor(name, list(shape), dt))
        return Buf(t, name)

    def _deps(self, R, W):
        d = {}

        def add(ev):
            if ev is None:
                return
            k, v = ev
            if d.get(k, 0) < v:
                d[k] = v
        for b in R:
            add(b.w)
        for b in W:
            add(b.w)
            for ev in b.r:
                add(ev)
        return d

    def _wait(self, e, d):
        eng = self.eng[e]
        seen = self.seen[e]
        for k, v in d.items():
            if e == "pe" and k == "pe":
                continue
            if seen.get(k, 0) >= v:
                continue
            eng.wait_ge(self.sem[k], v)
            seen[k] = v

    def _mark(self, ev, R, W):
        for b in R:
            b.r = [x for x in b.r if x[0] != ev[0]] + [ev]
        for b in W:
            b.w = ev
            b.r = []

    def op(self, e, fn, R=(), W=()):
        d = self._deps(R, W)
        self._wait(e, d)
        inst = fn(self.eng[e])
        self.cnt[e] += 1
        inst.then_inc(self.sem[e], 1)
        self._mark((e, self.cnt[e]), R, W)
        self.ninst += 1

    def dma(self, q, out, in_, R=(), W=(), **kw):
        ring = self.dring[q]
        key = ring[self.dpos[q] % self.NDMA]
        self.dpos[q] += 1
        d = self._deps(R, W)
        if self.cnt[key] > 0 and d.get(key, 0) < self.cnt[key]:
            d[key] = self.cnt[key]
        self._wait(q, d)
        inst = self.eng[q].dma_start(out=out, in_=in_, **kw)
        self.cnt[key] += 16
        inst.then_inc(self.sem[key], 16)
        self._mark((key, self.cnt[key]), R, W)
        self.ninst += 1

    def barrier(self):
        d = {k: v for k, v in self.cnt.items() if v > 0}
        for e in self.eng:
            self._wait(e, dict(d))

    def alt(self):
        self.rr += 1
        return "act" if self.rr % 2 else "dve"


def tt(k, e, out, a, b, op, R, W):
    k.op(e, lambda g: g.tensor_tensor(out=out, in0=a, in1=b, op=op), R=R, W=W)


def ts(k, e, out, a, s1, s2, op0, op1, R, W):
    if op1 is None:
        k.op(e, lambda g: g.tensor_scalar(out=out, in0=a, scalar1=s1, scalar2=None, op0=op0), R=R, W=W)
    else:
        k.op(e, lambda g: g.tensor_scalar(out=out, in0=a, scalar1=s1, scalar2=s2, op0=op0, op1=op1), R=R, W=W)


def stt(k, out, a, s, b, op0, op1, R, W):
    k.op("dve", lambda g: g.scalar_tensor_tensor(out=out, in0=a, scalar=s, in1=b, op0=op0, op1=op1), R=R, W=W)


def act(k, out, a, func, R, W, scale=None, bias=None):
    kw = {}
    if scale is not None:
        kw["scale"] = scale
    if bias is not None:
        kw["bias"] = bias
    k.op("act", lambda g: g.activation(out=out, in_=a, func=func, **kw), R=R, W=W)


def cp(k, e, out, a, R, W):
    if e == "act":
        k.op("act", lambda g: g.activation(out=out, in_=a, func=AF.Copy), R=R, W=W)
    else:
        k.op(e, lambda g: g.tensor_copy(out=out, in_=a), R=R, W=W)


def mm(k, out, lhsT, rhs, start, stop, R, W):
    k.op("pe", lambda g: g.matmul(out, lhsT=lhsT, rhs=rhs, start=start, stop=stop), R=R, W=W)


def tr(k, out, in_, ident, R, W):
    k.op("pe", lambda g: g.transpose(out=out, in_=in_, identity=ident), R=R, W=W)


class Rot:
    def __init__(self, bufs):
        self.bufs = bufs
        self.i = 0

    def get(self):
        b = self.bufs[self.i % len(self.bufs)]
        self.i += 1
        return b


def build(stop_after=None, debug=False):
    nc = bass.Bass("TRN2", target_bir_lowering=False)
    SK = "ExternalOutput" if debug else "Internal"

    def din(name, shape, dt=F32):
        return Buf(nc.dram_tensor(name, list(shape), dt, kind="ExternalInput").ap(), name)

    def dsc(name, shape, dt):
        return Buf(nc.dram_tensor(name, list(shape), dt, kind=SK).ap(), name)

    xin = din("xin", [T, D])
    cT_d = din("cT", [128, 8, 2])
    w_mod = din("w_mod", [DEPTH, D, 6 * D])
    b_modT = din("b_modT", [DEPTH, 128, 48])
    normsT = din("normsT", [128, DEPTH, 2, 8])
    normfT = din("normfT", [128, 8])
    w_in = din("w_in", [DEPTH, D, INC])
    dn_convT = din("dn_convT", [DEPTH, 128, 12, 4])
    dnab = din("dnab", [DEPTH, 4, 4])
    dn_normT = din("dn_normT", [DEPTH, 128, 1])
    lru_cw = din("lru_cw", [DEPTH, 128, 4, 4])
    lru_cb = din("lru_cb", [DEPTH, 128, 4])
    lru_vec = din("lru_vec", [DEPTH, 128, 3, 2, 4])
    lru_wblk = din("lru_wblk", [DEPTH, 2, 2, 4, 128, 128])
    da_lam = din("da_lam", [DEPTH, 256])
    da_normT = din("da_normT", [DEPTH, 128, 1])
    w_branch = din("w_branch", [DEPTH, 3, 512, D])
    w_out = din("w_out", [DEPTH, D, D])
    w_g = din("w_ffn_gate", [DEPTH, D, DFF])
    w_u = din("w_ffn_up", [DEPTH, D, DFF])
    w_d = din("w_ffn_down", [DEPTH, DFF, D])
    c_ident = din("c_ident", [128, 128])
    c_rope = din("c_rope", [2, 128, T])
    c_perm = din("c_perm", [128, 128])
    c_mb = din("c_mb", [2, 128, 128])
    c_nod = din("c_nod", [128, 128])
    c_m01 = din("c_m01", [4, 2, 512])
    c_sel = din("c_sel", [4, 4, 128])
    c_blk = din("c_blk", [5, 128, 128])
    out_d = Buf(nc.dram_tensor("out", [NLAT, D], F32, kind="ExternalOutput").ap(), "out")

    hT = dsc("hT", [D, T], F32)
    DNQKV = dsc("DNQKV", [3, 512, T], BF16)
    ZS = dsc("ZS", [512, T], BF16)
    GCB = dsc("GCB", [4, 4, T], F32)
    LX = dsc("LX", [512, T], F32)
    LG = dsc("LG", [512, T], BF16)
    QR = dsc("QR", [512, T], BF16)
    KR = dsc("KR", [512, T], BF16)
    VT = dsc("VT", [T, 512], BF16)
    SG = dsc("SG", [3072, T], BF16)
    OT = dsc("OT", [2, 512, T], F32)
    YB = dsc("YB", [512, T], BF16)
    YC = dsc("YC", [512, T], BF16)
    hTv = hT.ap.rearrange("(k p) t -> p k t", p=128)
    WGT = dsc("WGT", [11, 128, 8, 256], BF16)
    WUT = dsc("WUT", [11, 128, 8, 256], BF16)
    WDT = dsc("WDT", [8, 128, 22, 128], BF16)

    def precast_ffn(l):
        wgv = w_g.ap[l].rearrange("(k p) c -> p k c", p=128)
        wuv = w_u.ap[l].rearrange("(k p) c -> p k c", p=128)
        wdv = w_d.ap[l].rearrange("(fc p) c -> p fc c", p=128)
        for f2 in range(11):
            k.dma("pool", WGT.ap[f2], wgv[:, :, f2 * 256:(f2 + 1) * 256], R=[w_g], W=[WGT])
            k.dma("pool", WUT.ap[f2], wuv[:, :, f2 * 256:(f2 + 1) * 256], R=[w_u], W=[WUT])
        for dc in range(8):
            k.dma("pool", WDT.ap[dc], wdv[:, :, dc * 128:(dc + 1) * 128], R=[w_d], W=[WDT])

    es0 = ExitStack()
    k = K(nc, es0)
    identF = k.sb([128, 128], F32, "identF")
    identB = k.sb([128, 128], BF16, "identB")
    onesB = k.sb([128, 128], BF16, "onesB")
    modT = k.sb([128, DEPTH, 48, 2], F32, "modT")
    AB = k.sb([128, DEPTH, 6, 8, 2], F32, "AB")
    nrmT = k.sb([128, DEPTH, 2, 8], F32, "nrmT")
    nrmF = k.sb([128, 8], F32, "nrmF")
    cst = k.sb([128, 4], F32, "cst")
    k.dma("sp", identF[:], c_ident.ap, R=[c_ident], W=[identF])
    k.dma("sp", nrmT[:], normsT.ap, R=[normsT], W=[nrmT])
    k.dma("sp", nrmF[:], normfT.ap, R=[normfT], W=[nrmF])
    cp(k, "dve", identB[:], identF[:], [identF], [identB])
    k.op("dve", lambda g: g.memset(onesB[:], 1.0), W=[onesB])
    k.op("dve", lambda g: g.memset(cst[:, 0:1], 1e-6), W=[cst])
    k.op("dve", lambda g: g.memset(cst[:, 1:2], 1e-5), W=[cst])
    k.op("dve", lambda g: g.memset(cst[:, 2:3], 1.0), W=[cst])
    k.op("dve", lambda g: g.memset(cst[:, 3:4], 0.0), W=[cst])
    EPS6, EPS5, ONE = cst[:, 0:1], cst[:, 1:2], cst[:, 2:3]

    def norm_block(l, which, hb, n, j, sq, pss, rs, tmp, ubuf, uout):
        act(k, sq[:, :, :n], hb[:, :, :n], AF.Square, [hb], [sq])
        for kk in range(8):
            mm(k, pss[:, :n], onesB[:], sq[:, kk, :n], kk == 0, kk == 7, [onesB, sq], [pss])
        act(k, rs[:, :n], pss[:, :n], AF.Sqrt, [pss, cst], [rs], scale=1.0 / D, bias=EPS6)
        k.op("dve", lambda g: g.reciprocal(out=rs[:, :n], in_=rs[:, :n]), R=[rs], W=[rs])
        for kk in range(8):
            tt(k, "dve", tmp[:, kk, :n], hb[:, kk, :n], rs[:, :n], ALU.mult, [hb, rs], [tmp])
            act(k, uout(kk), tmp[:, kk, :n], AF.Identity, [tmp, AB], [ubuf],
                scale=AB[:, l, 3 * which + 0, kk, j:j + 1], bias=AB[:, l, 3 * which + 1, kk, j:j + 1])

    with ExitStack() as es:
        k.es = es
        cTs = k.sb([128, 8, 2], F32, "cTs")
        sTs = k.sb([128, 8, 2], F32, "sTs")
        bm = k.sb([128, DEPTH, 48], F32, "bm")
        k.dma("sp", cTs[:], cT_d.ap, R=[cT_d], W=[cTs])
        for l in range(DEPTH):
            k.dma("sp", bm[:, l, :], b_modT.ap[l], R=[b_modT], W=[bm])
        act(k, sTs[:], cTs[:], AF.Silu, [cTs], [sTs])
        wm = Rot([k.sb([128, 8, 512], F32, "wm") for _ in range(2)])
        pmod = k.ps([128, 48, 2], F32, "pmod")
        for l in range(DEPTH):
            for g in range(12):
                wt = wm.get()
                k.dma("sp", wt[:], w_mod.ap[l].rearrange("(k p) c -> p k c", p=128)[:, :, g * 512:(g + 1) * 512], R=[w_mod], W=[wt])
                for c4 in range(4):
                    ch = g * 4 + c4
                    for kk in range(8):
                        mm(k, pmod[:, ch, :], wt[:, kk, c4 * 128:(c4 + 1) * 128], sTs[:, kk, :], kk == 0, kk == 7, [wt, sTs], [pmod])
            for j in range(2):
                tt(k, "dve", modT[:, l, :, j], pmod[:, :, j], bm[:, l, :], ALU.add, [pmod, bm], [modT])
            for s, (ish, isc, ig) in enumerate([(0, 1, 2), (3, 4, 5)]):
                for j in range(2):
                    stt(k, AB[:, l, 3 * s + 0, :, j], modT[:, l, isc * 8:(isc + 1) * 8, j], 1.0, nrmT[:, l, s, :], ALU.add, ALU.mult, [modT, nrmT], [AB])
                    cp(k, "dve", AB[:, l, 3 * s + 1, :, j], modT[:, l, ish * 8:(ish + 1) * 8, j], [modT], [AB])
                    cp(k, "dve", AB[:, l, 3 * s + 2, :, j], modT[:, l, ig * 8:(ig + 1) * 8, j], [modT], [AB])
        k.barrier()
    k.es = es0

    with ExitStack() as es:
        k.es = es
        xt = Rot([k.sb([128, D], F32, "xt") for _ in range(2)])
        ht = Rot([k.sb([128, 8, 128], F32, "ht") for _ in range(2)])
        pp = Rot([k.ps([128, 4, 128], F32, "ppT") for _ in range(4)])
        for t in range(T // 128):
            x_ = xt.get()
            k.dma("sp", x_[:], xin.ap[t * 128:(t + 1) * 128, :], R=[xin], W=[x_])
            h_ = ht.get()
            for half in range(2):
                p_ = pp.get()
                for q in range(4):
                    kk = half * 4 + q
                    tr(k, p_[:, q, :], x_[:, kk * 128:(kk + 1) * 128], identF[:], [x_, identF], [p_])
                cp(k, k.alt(), h_[:, half * 4:half * 4 + 4, :], p_[:], [p_], [h_])
            k.dma("sp", hTv[:, :, t * 128:(t + 1) * 128], h_[:], R=[h_], W=[hT])
        k.barrier()
    k.es = es0

    def off(t):
        return t + 1 if t < NCTX else t + 4

    def phase_A(l):
        with ExitStack() as esA:
            k.es = esA
            UT = k.sb([128, 8, T], BF16, "UT")
            with ExitStack() as es1:
                k.es = es1
                hbR = Rot([k.sb([128, 8, 512], F32, "hbA") for _ in range(2)])
                sq = k.sb([128, 8, 512], BF16, "sqA")
                tmp = k.sb([128, 8, 512], F32, "tmpA")
                rs = k.sb([128, 512], F32, "rsA")
                pss = k.ps([128, 512], F32, "pssA")
                for (t0, n) in BLOCKS:
                    j = 0 if t0 < NCTX else 1
                    h_ = hbR.get()
                    k.dma("sp", h_[:, :, :n], hTv[:, :, t0:t0 + n], R=[hT], W=[h_])
                    norm_block(l, 0, h_, n, j, sq, pss, rs, tmp, UT, lambda kk: UT[:, kk, t0:t0 + n])
                k.barrier()
            k.es = esA
            permB = k.sb([128, 128], BF16, "permB")
            cw = k.sb([128, 12, 4], F32, "cw")
            lcw = k.sb([128, 4, 4], F32, "lcw")
            lcb = k.sb([128, 4], F32, "lcb")
            dnab_s = k.sb([4, 4], F32, "dnab_s")
            nA = k.sb([4, 2], F32, "nA")
            m01 = k.sb([4, 2, 512], F32, "m01")
            k.dma("pool", permB[:], c_perm.ap, R=[c_perm], W=[permB])
            k.dma("sp", cw[:], dn_convT.ap[l], R=[dn_convT], W=[cw])
            k.dma("sp", lcw[:], lru_cw.ap[l], R=[lru_cw], W=[lcw])
            k.dma("sp", lcb[:], lru_cb.ap[l], R=[lru_cb], W=[lcb])
            k.dma("sp", dnab_s[:], dnab.ap[l], R=[dnab], W=[dnab_s])
            k.dma("sp", m01[:], c_m01.ap, R=[c_m01], W=[m01])
            act(k, nA[:], dnab_s[:, 0:2], AF.Exp, [dnab_s], [nA])
            ts(k, "dve", nA[:], nA[:], -1.0, None, ALU.mult, None, [nA], [nA])
            WG = Rot([k.sb([128, 8, 512], BF16, "WG") for _ in range(2)])
            OB = Rot([k.sb([128, T], BF16, "OB") for _ in range(2)])
            pg = Rot([k.ps([128, 512], F32, "pg") for _ in range(4)])
            p2 = Rot([k.ps([128, 512], F32, "p2") for _ in range(2)])

            def rot(shape, dt, name, n=2):
                return Rot([k.sb(shape, dt, name) for _ in range(n)])
            w_inv = w_in.ap[l].rearrange("(k p) c -> p k c", p=128)
            groups_conv = [("dnq", 0, 512), ("dnk", 512, 512), ("dnv", 1024, 512), ("lrux", 2064, 512)]
            groups_rest = [("z", 1536, 512), ("ab", 2048, 16), ("lrug", 2576, 512), ("daq", 3088, 512),
                           ("dak", 3600, 512), ("dav", 4112, 512)] + [("gate%d" % g, 4624 + 512 * g, 512) for g in range(6)]
            def process(groups):
                for (kind, c0, ncol) in groups:
                    wt = WG.get()
                    k.dma("pool", wt[:, :, :ncol], w_inv[:, :, c0:c0 + ncol], R=[w_in], W=[wt])
                    if kind == "dav":
                        for ti in range(T // 128):
                            p_ = pg.get()
                            for kk in range(8):
                                mm(k, p_[:, :], UT[:, kk, ti * 128:(ti + 1) * 128], wt[:, kk, :], kk == 0, kk == 7, [UT, wt], [p_])
                            v_ = vbR.get()
                            cp(k, k.alt(), v_[:], p_[:], [p_], [v_])
                            k.dma("sp", VT.ap[ti * 128:(ti + 1) * 128, :], v_[:], R=[v_], W=[VT])
                        continue
                    if kind == "ab":
                        for (t0, n) in BLOCKS:
                            abt = abR.get()
                            for d in range(2):
                                p_ = pg.get()
                                for kk in range(8):
                                    mm(k, p_[0:4, :n], wt[:, kk, 4 * d:4 * d + 4], UT[:, kk, t0:t0 + n], kk == 0, kk == 7, [UT, wt], [p_])
                                e_ = eR.get()
                                act(k, e_[:, :n], p_[0:4, :n], AF.Exp, [p_, dnab_s], [e_], bias=dnab_s[:, 2 + d:3 + d])
                                act(k, e_[:, :n], e_[:, :n], AF.Ln, [e_, cst], [e_], bias=cst[0:4, 2:3])
                                ts(k, "dve", e_[:, :n], e_[:, :n], nA[:, d:d + 1], None, ALU.mult, None, [e_, nA], [e_])
                                if d == 0:
                                    k.op("dve", lambda g: g.tensor_tensor_scan(out=abt[:, 0, :n], data0=m01[:, 0, :n], data1=e_[:, :n], initial=0.0, op0=ALU.mult, op1=ALU.add), R=[m01, e_], W=[abt])
                                else:
                                    k.op("dve", lambda g: g.tensor_tensor_scan(out=abt[:, 1, :n][:, ::-1], data0=m01[:, 1, :n][:, ::-1], data1=e_[:, :n][:, ::-1], initial=0.0, op0=ALU.mult, op1=ALU.add), R=[m01, e_], W=[abt])
                                p_ = pg.get()
                                for kk in range(8):
                                    mm(k, p_[0:4, :n], wt[:, kk, 8 + 4 * d:12 + 4 * d], UT[:, kk, t0:t0 + n], kk == 0, kk == 7, [UT, wt], [p_])
                                act(k, abt[:, 2 + d, :n], p_[0:4, :n], AF.Sigmoid, [p_], [abt])
                            k.dma("sp", GCB.ap[:, :, t0:t0 + n], abt[:, :, :n], R=[abt], W=[GCB])
                        continue
                    for c4 in range(4):
                        conv = kind in ("dnq", "dnk", "dnv", "lrux")
                        xp = XP.get() if conv else None
                        ob = OB.get() if kind != "lrux" else None
                        for (t0, n) in BLOCKS:
                            p_ = pg.get()
                            for kk in range(8):
                                mm(k, p_[:, :n], wt[:, kk, c4 * 128:(c4 + 1) * 128], UT[:, kk, t0:t0 + n], kk == 0, kk == 7, [UT, wt], [p_])
                            if conv:
                                cp(k, k.alt(), xp[:, off(t0):off(t0) + n], p_[:, :n], [p_], [xp])
                            elif kind == "z":
                                act(k, ob[:, t0:t0 + n], p_[:, :n], AF.Silu, [p_], [ob])
                            elif kind == "lrug":
                                act(k, ob[:, t0:t0 + n], p_[:, :n], AF.Gelu, [p_], [ob])
                            elif kind.startswith("gate"):
                                act(k, ob[:, t0:t0 + n], p_[:, :n], AF.Sigmoid, [p_], [ob])
                            else:
                                qr_ = qrawR.get()
                                cp(k, "act", qr_[:, :n], p_[:, :n], [p_], [qr_])
                                q2 = p2.get()
                                mm(k, q2[:, :n], permB[:], qr_[:, :n], True, True, [permB, qr_], [q2])
                                a1 = t1R.get()
                                tt(k, "pool", a1[:, :n], qr_[:, :n], cosT[:, t0:t0 + n], ALU.mult, [qr_, cosT], [a1])
                                a2 = t2R.get()
                                tt(k, "dve", a2[:, :n], q2[:, :n], sinT[:, t0:t0 + n], ALU.mult, [q2, sinT], [a2])
                                tt(k, "pool", ob[:, t0:t0 + n], a1[:, :n], a2[:, :n], ALU.add, [a1, a2], [ob])
                        if conv:
                            if kind == "lrux":
                                wv = lambda jj, c4=c4: lcw[:, c4, jj:jj + 1]
                            else:
                                ci = {"dnq": 0, "dnk": 4, "dnv": 8}[kind] + c4
                                wv = lambda jj, ci=ci: cw[:, ci, jj:jj + 1]
                            wbuf = lcw if kind == "lrux" else cw
                            Dg = DgR.get()
                            for jj in range(4):
                                ts(k, "dve", Dg[:, jj, :], identB[:], wv(jj), None, ALU.mult, None, [identB, wbuf], [Dg])

                            def post(kind=kind, c4=c4, xp=xp, ob=ob, Dg=Dg):
                                ar = accrow.get() if kind != "dnv" else None
                                for (t0, n) in BLOCKS:
                                    c = off(t0)
                                    pc = pg.get()
                                    for idx, (jj, sh) in enumerate(((0, -1), (1, 0), (2, 1), (3, 2))):
                                        mm(k, pc[:, :n], Dg[:, jj, :], xp[:, c + sh:c + sh + n], idx == 0, idx == 3, [Dg, xp], [pc])
                                    if kind == "lrux":
                                        act(k, ar[:, t0:t0 + n], pc[:, :n], AF.Identity, [pc, lcb], [ar], bias=lcb[:, c4:c4 + 1])
                                    elif kind == "dnv":
                                        act(k, ob[:, t0:t0 + n], pc[:, :n], AF.Silu, [pc], [ob])
                                    else:
                                        act(k, ar[:, t0:t0 + n], pc[:, :n], AF.Silu, [pc], [ar])
                                if kind == "lrux":
                                    k.dma("sp", LX.ap[c4 * 128:(c4 + 1) * 128, :], ar[:], R=[ar], W=[LX])
                                    return
                                if kind != "dnv":
                                    for (t0, n) in BLOCKS:
                                        tt(k, "pool", sqrow[:, t0:t0 + n], ar[:, t0:t0 + n], ar[:, t0:t0 + n], ALU.mult, [ar], [sqrow])
                                    for (t0, n) in BLOCKS:
                                        q2 = p2.get()
                                        mm(k, q2[:, :n], onesB[:], sqrow[:, t0:t0 + n], True, True, [onesB, sqrow], [q2])
                                        act(k, lnrow[:, t0:t0 + n], q2[:, :n], AF.Ln, [q2, cst], [lnrow], bias=EPS6)
                                    for (t0, n) in BLOCKS:
                                        act(k, lnrow[:, t0:t0 + n], lnrow[:, t0:t0 + n], AF.Exp, [lnrow], [lnrow], scale=-0.5)
                                    for (t0, n) in BLOCKS:
                                        if kind == "dnq":
                                            stt(k, ob[:, t0:t0 + n], ar[:, t0:t0 + n], 128.0 ** -0.5, lnrow[:, t0:t0 + n], ALU.mult, ALU.mult, [ar, lnrow], [ob])
                                        else:
                                            tt(k, "dve", ob[:, t0:t0 + n], ar[:, t0:t0 + n], lnrow[:, t0:t0 + n], ALU.mult, [ar, lnrow], [ob])
                                dst = DNQKV.ap[{"dnq": 0, "dnk": 1, "dnv": 2}[kind], c4 * 128:(c4 + 1) * 128, :]
                                k.dma("sp", dst, ob[:], R=[ob], W=[DNQKV])
                            while pending:
                                pending.pop(0)()
                            pending.append(post)
                            continue
                        if ob is not None:
                            if kind in ("dnq", "dnk", "dnv"):
                                dst = DNQKV.ap[{"dnq": 0, "dnk": 1, "dnv": 2}[kind], c4 * 128:(c4 + 1) * 128, :]
                                dbuf = DNQKV
                            elif kind == "z":
                                dst, dbuf = ZS.ap[c4 * 128:(c4 + 1) * 128, :], ZS
                            elif kind == "lrug":
                                dst, dbuf = LG.ap[c4 * 128:(c4 + 1) * 128, :], LG
                            elif kind == "daq":
                                dst, dbuf = QR.ap[c4 * 128:(c4 + 1) * 128, :], QR
                            elif kind == "dak":
                                dst, dbuf = KR.ap[c4 * 128:(c4 + 1) * 128, :], KR
                            else:
                                g = int(kind[4:])
                                r0 = (g * 4 + c4) * 128
                                dst, dbuf = SG.ap[r0:r0 + 128, :], SG
                            k.dma("sp", dst, ob[:], R=[ob], W=[dbuf])
            with ExitStack() as e2:
                k.es = e2
                XPs = [k.sb([128, T + 6], BF16, "XP") for _ in range(2)]
                for x_ in XPs:
                    k.op("pool", lambda g: g.memset(x_[:], 0.0), W=[x_])
                XP = Rot(XPs)
                accrow = rot([128, T], F32, "accrow", 2)
                DgR = rot([128, 4, 128], BF16, "Dg", 2)
                pending = []
                sqrow = k.sb([128, T], BF16, "sqrow")
                lnrow = k.sb([128, T], F32, "lnrow")
                process(groups_conv)
                while pending:
                    pending.pop(0)()
                k.barrier()
            with ExitStack() as e3:
                k.es = e3
                cosT = k.sb([128, T], BF16, "cosT")
                sinT = k.sb([128, T], BF16, "sinT")
                k.dma("pool", cosT[:], c_rope.ap[0], R=[c_rope], W=[cosT])
                k.dma("pool", sinT[:], c_rope.ap[1], R=[c_rope], W=[sinT])
                qrawR = rot([128, 512], BF16, "qraw")
                t1R = rot([128, 512], F32, "t1")
                t2R = rot([128, 512], F32, "t2")
                vbR = rot([128, 512], BF16, "vb")
                abR = rot([4, 4, 512], F32, "abt", 2)
                eR = rot([4, 512], F32, "eab")
                process(groups_rest)
                k.barrier()
            k.es = esA
            k.barrier()
        k.es = es0

    def phase_C(l):
        with ExitStack() as es:
            k.es = es
            lv = k.sb([128, 3, 2, 4], F32, "lv")
            cneg = k.sb([128, 2, 4], F32, "cneg")
            k.dma("sp", lv[:], lru_vec.ap[l], R=[lru_vec], W=[lv])
            act(k, cneg[:], lv[:, 2], AF.Exp, [lv], [cneg], scale=-1.0)
            act(k, cneg[:], cneg[:], AF.Ln, [cneg, cst], [cneg], bias=ONE)
            ts(k, "dve", cneg[:], cneg[:], -8.0, None, ALU.mult, None, [cneg], [cneg])
            xc = k.sb([128, T], F32, "xc")
            xcb = k.sb([128, T], BF16, "xcb")
            gg = k.sb([128, T], BF16, "gg")
            A_s = [k.sb([128, T], F32, "lruA%d" % d_) for d_ in range(2)]
            IG_s = [k.sb([128, T], F32, "lruIG%d" % d_) for d_ in range(2)]
            W_s = [k.sb([128, T], F32, "lruW%d" % d_) for d_ in range(2)]
            H = [k.sb([128, T], F32, "lruH%d" % d) for d in range(2)]
            yb = k.sb([128, T], BF16, "lruY")
            wb = k.sb([128, 2, 2, 128], BF16, "lruwb")
            pg = Rot([k.ps([128, 512], F32, "pgC") for _ in range(4)])
            for cc in range(4):
                k.dma("sp", xc[:], LX.ap[cc * 128:(cc + 1) * 128, :], R=[LX], W=[xc])
                k.dma("sp", gg[:], LG.ap[cc * 128:(cc + 1) * 128, :], R=[LG], W=[gg])
                k.dma("pool", wb[:], lru_wblk.ap[l, :, :, cc].rearrange("a d i j -> i a d j"), R=[lru_wblk], W=[wb])
                cp(k, "act", xcb[:], xc[:], [xc], [xcb])
                for d in range(2):
                    A_, IG = A_s[d], IG_s[d]
                    for (t0, n) in BLOCKS:
                        p_ = pg.get()
                        mm(k, p_[:, :n], wb[:, 0, d, :], xcb[:, t0:t0 + n], True, True, [wb, xcb], [p_])
                        act(k, A_[:, t0:t0 + n], p_[:, :n], AF.Sigmoid, [p_, lv], [A_], bias=lv[:, 0, d, cc:cc + 1])
                        p_ = pg.get()
                        mm(k, p_[:, :n], wb[:, 1, d, :], xcb[:, t0:t0 + n], True, True, [wb, xcb], [p_])
                        act(k, IG[:, t0:t0 + n], p_[:, :n], AF.Sigmoid, [p_, lv], [IG], bias=lv[:, 1, d, cc:cc + 1])
                for d in range(2):
                    act(k, A_s[d][:], A_s[d][:], AF.Exp, [A_s[d], cneg], [A_s[d]], scale=cneg[:, d, cc:cc + 1])
                    tt(k, "pool", IG_s[d][:], IG_s[d][:], xc[:], ALU.mult, [IG_s[d], xc], [IG_s[d]])
                for d in range(2):
                    act(k, W_s[d][:], A_s[d][:], AF.Square, [A_s[d]], [W_s[d]])
                for d in range(2):
                    ts(k, "dve", W_s[d][:], W_s[d][:], -1.0, 1.0, ALU.mult, ALU.add, [W_s[d]], [W_s[d]])
                for d in range(2):
                    act(k, W_s[d][:], W_s[d][:], AF.Sqrt, [W_s[d]], [W_s[d]])
                for d in range(2):
                    tt(k, "dve", IG_s[d][:], IG_s[d][:], W_s[d][:], ALU.mult, [IG_s[d], W_s[d]], [IG_s[d]])
                for d in range(2):
                    A_, IG, Hd = A_s[d], IG_s[d], H[d]
                    if d == 0:
                        k.op("dve", lambda g: g.tensor_tensor_scan(out=Hd[:], data0=A_[:], data1=IG[:], initial=0.0, op0=ALU.mult, op1=ALU.add), R=[A_, IG], W=[Hd])
                    else:
                        k.op("dve", lambda g: g.tensor_tensor_scan(out=Hd[:, 0:NCTX][:, ::-1], data0=A_[:, 0:NCTX][:, ::-1], data1=IG[:, 0:NCTX][:, ::-1], initial=0.0, op0=ALU.mult, op1=ALU.add), R=[A_, IG], W=[Hd])
                        k.op("dve", lambda g: g.tensor_tensor_scan(out=Hd[:, NCTX:T][:, ::-1], data0=A_[:, NCTX:T][:, ::-1], data1=IG[:, NCTX:T][:, ::-1], initial=Hd[:, 0:1], op0=ALU.mult, op1=ALU.add), R=[A_, IG, Hd], W=[Hd])
                tt(k, "pool", H[0][:], H[0][:], H[1][:], ALU.add, [H[0], H[1]], [H[0]])
                tt(k, "dve", yb[:], H[0][:], gg[:], ALU.mult, [H[0], gg], [yb])
                k.dma("sp", YB.ap[cc * 128:(cc + 1) * 128, :], yb[:], R=[yb], W=[YB])
            k.barrier()
        k.es = es0

    def phase_B(l):
        with ExitStack() as es:
            k.es = es
            mb = k.sb([128, 2, 128], F32, "mb")
            nod = k.sb([128, 128], F32, "nod")
            sel = k.sb([4, 4, 128], F32, "sel")
            blk = k.sb([128, 5, 128], F32, "blk")
            k.dma("sp", mb[:], c_mb.ap.rearrange("d j i -> j d i"), R=[c_mb], W=[mb])
            k.dma("sp", nod[:], c_nod.ap, R=[c_nod], W=[nod])
            k.dma("sp", sel[:], c_sel.ap, R=[c_sel], W=[sel])
            k.dma("sp", blk[:], c_blk.ap.rearrange("m j i -> j m i"), R=[c_blk], W=[blk])
            precast_ffn(l)
            qv = DNQKV.ap.rearrange("w (h f) t -> w f h t", h=4)
            otv = OT.ap.rearrange("d (h f) t -> d f h t", h=4)
            gcv = GCB.ap
            order = {0: list(range(len(BLOCKS))), 1: [0] + list(range(len(BLOCKS) - 1, 0, -1))}
            B4 = [128, 4, 128]
            bc_h = lambda ap2: ap2.unsqueeze(1).broadcast_to(B4)
            bc_i = lambda ap2: ap2.unsqueeze(2).broadcast_to(B4)

            def dn_dir(d):
                def rot(shape, dt, name, n=1):
                    return Rot([k.sb(shape, dt, name + "%d" % d) for _ in range(n)])
                QTr = rot([128, 4, 512], BF16, "QT")
                KTr = rot([128, 4, 512], BF16, "KT")
                VTr = rot([128, 4, 512], BF16, "VTt")
                GBr = rot([4, 2, 512], F32, "GB")
                OBr = rot([128, 4, 512], F32, "OTb", 2)
                S_f = k.sb(B4, F32, "S32_%d" % d)
                S_b = k.sb(B4, BF16, "Sb_%d" % d)
                k.op("pool", lambda g: g.memset(S_f[:], 0.0), W=[S_f])
                k.op("pool", lambda g: g.memset(S_b[:], 0.0), W=[S_b])
                pf = Rot([k.ps(B4, F32, "pfB%d" % d) for _ in range(2)])
                prc = Rot([k.ps(B4, F32, "prB%d" % d) for _ in range(1)])
                pb = Rot([k.ps(B4, BF16, "pbB%d" % d) for _ in range(1)])
                gbc = rot([128, 8], F32, "gbc", 2)
                sc4 = rot([128, 4, 4], F32, "sc4", 3)
                D1 = rot(B4, F32, "D1")
                E = rot(B4, F32, "E", 2)
                nE = rot(B4, F32, "nE")
                EG = rot(B4, F32, "EG")
                Nr = rot(B4, BF16, "N")
                L0r = rot(B4, BF16, "L0")
                Ndr = rot(B4, BF16, "Nd")
                Ldr = rot(B4, BF16, "Ld")
                N2r = rot(B4, BF16, "N2")
                L2r = rot(B4, BF16, "L2")
                L4r = rot(B4, BF16, "L4")
                Xr = rot(B4, BF16, "X", 3)
                Lbr = rot(B4, BF16, "Lb")
                XTr = rot(B4, BF16, "XT")
                Yr = rot(B4, BF16, "Y")
                Rr = rot(B4, BF16, "R", 2)
                QK = rot(B4, BF16, "QK", 3)
                kbg = rot(B4, BF16, "kbg", 2)
                kout = rot(B4, BF16, "kout", 3)
                vbt = rot(B4, BF16, "vbt", 2)
                wTr = rot(B4, BF16, "wT", 3)
                ur = rot(B4, F32, "u", 3)
                qin = rot(B4, BF16, "qin", 3)
                vnew = rot(B4, BF16, "vnew", 2)
                stmp = rot(B4, F32, "stmp")
                bm_ = lambda i_: bc_h(blk[:, i_, :])

                def mm4(p, lf, rf, R):
                    for h in range(4):
                        mm(k, p[:, h, :], lf(h), rf(h), True, True, R, [p])

                def tr4(p, src, ident, R):
                    for h in range(4):
                        tr(k, p[:, h, :], src(h), ident, R, [p])

                DEPTH = 2
                HQ = {}
                pdone = [0]
                rdone = [0]
                cidx = [0]

                def prep():
                  for step in range(len(BLOCKS)):
                      t0, n = BLOCKS[order[d][step]]
                      QT, KT, VTt, GB = QTr.get(), KTr.get(), VTr.get(), GBr.get()
                      k.dma("sp", QT[:, :, :n], qv[0, :, :, t0:t0 + n], R=[DNQKV], W=[QT])
                      k.dma("sp", KT[:, :, :n], qv[1, :, :, t0:t0 + n], R=[DNQKV], W=[KT])
                      k.dma("sp", VTt[:, :, :n], qv[2, :, :, t0:t0 + n], R=[DNQKV], W=[VTt])
                      k.dma("sp", GB[:, 0, :n], gcv[:, d, t0:t0 + n], R=[GCB], W=[GB])
                      k.dma("sp", GB[:, 1, :n], gcv[:, 2 + d, t0:t0 + n], R=[GCB], W=[GB])
                      yield
                      nch = n // 128
                      chs = list(range(nch)) if d == 0 else list(range(nch - 1, -1, -1))
                      lastc = 127 if d == 0 else 0
                      for ci_, c in enumerate(chs):
                          while cidx[0] - rdone[0] >= DEPTH:
                              yield
                          o = c * 128
                          sl = slice(o, o + 128)
                          p0 = pf.get()
                          tr(k, p0[:, 0, 0:4], GB[:, 0, sl], identF[0:4, 0:4], [GB, identF], [p0])
                          tr(k, p0[:, 0, 4:8], GB[:, 1, sl], identF[0:4, 0:4], [GB, identF], [p0])
                          g8 = gbc.get()
                          cp(k, "dve", g8[:], p0[:, 0, 0:8], [p0], [g8])
                          yield
                          GR = pf.get()
                          mm4(GR, lambda h: sel[:, h, :], lambda h: GB[:, 0, sl], [sel, GB])
                          d1 = D1.get()
                          tt(k, "dve", d1[:], GR[:], bc_i(g8[:, 0:4]), ALU.subtract, [GR, g8], [d1])
                          s4 = sc4.get()
                          act(k, s4[:, 0, :], g8[:, 0:4], AF.Exp, [g8], [s4])
                          yield
                          tt(k, "dve", s4[:, 1, :], s4[:, 0, :], g8[:, 4:8], ALU.mult, [s4, g8], [s4])
                          tt(k, "dve", s4[:, 2, :], GR[:, :, lastc], g8[:, 0:4], ALU.subtract, [GR, g8], [s4])
                          act(k, s4[:, 2, :], s4[:, 2, :], AF.Exp, [s4], [s4])
                          act(k, s4[:, 3, :], GR[:, :, lastc], AF.Exp, [GR], [s4])
                          eg_ = EG.get()
                          act(k, eg_[:], GR[:], AF.Exp, [GR], [eg_])
                          yield
                          tt(k, "pool", d1[:], d1[:], bc_h(mb[:, d, :]), ALU.add, [d1, mb], [d1])
                          e_ = E.get()
                          act(k, e_[:], d1[:], AF.Exp, [d1], [e_])
                          yield
                          ne_ = nE.get()
                          tt(k, "pool", ne_[:], e_[:], bc_h(nod[:]), ALU.mult, [e_, nod], [ne_])
                          BR = pf.get()
                          mm4(BR, lambda h: sel[:, h, :], lambda h: GB[:, 1, sl], [sel, GB])
                          tt(k, "dve", ne_[:], BR[:], ne_[:], ALU.mult, [BR, ne_], [ne_])
                          yield
                          KK = pf.get()
                          mm4(KK, lambda h: KT[:, h, sl], lambda h: KT[:, h, sl], [KT])
                          N_ = Nr.get()
                          tt(k, "dve", N_[:], KK[:], ne_[:], ALU.mult, [KK, ne_], [N_])
                          yield
                          KQ = pf.get()
                          mm4(KQ, lambda h: KT[:, h, sl], lambda h: QT[:, h, sl], [KT, QT])
                          qk = QK.get()
                          tt(k, "dve", qk[:], KQ[:], e_[:], ALU.mult, [KQ, e_], [qk])
                          yield
                          pL = pb.get()
                          tr4(pL, lambda h: N_[:, h, :], identB[:], [N_, identB])
                          L0 = L0r.get()
                          cp(k, "act", L0[:], pL[:], [pL], [L0])
                          Nd = Ndr.get()
                          tt(k, "pool", Nd[:], N_[:], bm_(0), ALU.mult, [N_, blk], [Nd])
                          yield
                          Ld = Ldr.get()
                          tt(k, "pool", Ld[:], L0[:], bm_(0), ALU.mult, [L0, blk], [Ld])
                          yield
                          pA = pf.get()
                          mm4(pA, lambda h: Ld[:, h, :], lambda h: Nd[:, h, :], [Ld, Nd])
                          N2 = N2r.get()
                          cp(k, "act", N2[:], pA[:], [pA], [N2])
                          pA = pf.get()
                          mm4(pA, lambda h: Nd[:, h, :], lambda h: Ld[:, h, :], [Ld, Nd])
                          L2 = L2r.get()
                          cp(k, "dve", L2[:], pA[:], [pA], [L2])
                          X = Xr.get()
                          tt(k, "pool", X[:], Nd[:], bc_h(identB[:]), ALU.add, [Nd, identB], [X])
                          yield
                          pA = pf.get()
                          mm4(pA, lambda h: N2[:, h, :], lambda h: L2[:, h, :], [N2, L2])
                          L4 = L4r.get()
                          cp(k, "act", L4[:], pA[:], [pA], [L4])
                          yield
                          for Lk in (L2, L4):
                              pA = pf.get()
                              mm4(pA, lambda h: Lk[:, h, :], lambda h: X[:, h, :], [Lk, X])
                              X2 = Xr.get()
                              tt(k, "dve", X2[:], pA[:], X[:], ALU.add, [pA, X], [X2])
                              X = X2
                              yield
                          for lev in range(1, 5):
                              Lb = Lbr.get()
                              tt(k, "pool", Lb[:], L0[:], bm_(lev), ALU.mult, [L0, blk], [Lb])
                              pT = pb.get()
                              tr4(pT, lambda h: X[:, h, :], identB[:], [X, identB])
                              XT = XTr.get()
                              cp(k, "act", XT[:], pT[:], [pT], [XT])
                              yield
                              pA = pf.get()
                              mm4(pA, lambda h: Lb[:, h, :], lambda h: X[:, h, :], [Lb, X])
                              Y = Yr.get()
                              cp(k, "dve", Y[:], pA[:], [pA], [Y])
                              yield
                              pA = pf.get()
                              mm4(pA, lambda h: XT[:, h, :], lambda h: Y[:, h, :], [XT, Y])
                              if lev < 4:
                                  X2 = Xr.get()
                                  tt(k, "dve", X2[:], pA[:], X[:], ALU.add, [pA, X], [X2])
                                  X = X2
                              else:
                                  R_ = Rr.get()
                                  tt(k, "dve", R_[:], pA[:], X[:], ALU.add, [pA, X], [R_])
                              yield
                          pK = pb.get()
                          tr4(pK, lambda h: KT[:, h, sl], identB[:], [KT, identB])
                          kb_ = kbg.get()
                          tt(k, "dve", kb_[:], pK[:], bc_i(s4[:, 1, :]), ALU.mult, [pK, s4], [kb_])
                          ko_ = kout.get()
                          tt(k, "dve", ko_[:], pK[:], bc_i(s4[:, 2, :]), ALU.mult, [pK, s4], [ko_])
                          yield
                          pV = pb.get()
                          tr4(pV, lambda h: VTt[:, h, sl], identB[:], [VTt, identB])
                          vb_ = vbt.get()
                          tt(k, "dve", vb_[:], pV[:], bc_i(g8[:, 4:8]), ALU.mult, [pV, g8], [vb_])
                          yield
                          pW = pf.get()
                          mm4(pW, lambda h: kb_[:, h, :], lambda h: R_[:, h, :], [kb_, R_])
                          wT = wTr.get()
                          cp(k, "act", wT[:], pW[:], [pW], [wT])
                          yield
                          pUu = pf.get()
                          mm4(pUu, lambda h: R_[:, h, :], lambda h: vb_[:, h, :], [vb_, R_])
                          u_ = ur.get()
                          cp(k, "dve", u_[:], pUu[:], [pUu], [u_])
                          qi = qin.get()
                          tt(k, "pool", qi[:], QT[:, :, sl], eg_[:], ALU.mult, [QT, eg_], [qi])
                          yield
                          HQ[cidx[0]] = dict(wT=wT, u_=u_, qi=qi, qk=qk, ko_=ko_, s4=s4, t0=t0, n=n, sl=sl, first=(ci_ == 0), last=(ci_ == len(chs) - 1))
                          cidx[0] += 1
                          pdone[0] = cidx[0]
                          yield

                def recur():
                    OTb = None
                    for idx in range(34):
                        while pdone[0] <= idx:
                            yield
                        hq = HQ.pop(idx)
                        wT, u_, qi, qk, ko_, s4, t0, n, sl = hq["wT"], hq["u_"], hq["qi"], hq["qk"], hq["ko_"], hq["s4"], hq["t0"], hq["n"], hq["sl"]
                        if hq["first"]:
                            OTb = OBr.get()
                        pWS = prc.get()
                        mm4(pWS, lambda h: wT[:, h, :], lambda h: S_b[:, h, :], [wT, S_b])
                        vn = vnew.get()
                        tt(k, "dve", vn[:], u_[:], pWS[:], ALU.subtract, [u_, pWS], [vn])
                        yield
                        pO = prc.get()
                        for h in range(4):
                            mm(k, pO[:, h, :], S_b[:, h, :], qi[:, h, :], True, False, [S_b, qi], [pO])
                            mm(k, pO[:, h, :], vn[:, h, :], qk[:, h, :], False, True, [vn, qk], [pO])
                        cp(k, "act", OTb[:, :, sl], pO[:], [pO], [OTb])
                        yield
                        pS = prc.get()
                        mm4(pS, lambda h: ko_[:, h, :], lambda h: vn[:, h, :], [ko_, vn])
                        st_ = stmp.get()
                        tt(k, "pool", st_[:], S_f[:], bc_i(s4[:, 3, :]), ALU.mult, [S_f, s4], [st_])
                        tt(k, "dve", S_f[:], st_[:], pS[:], ALU.add, [st_, pS], [S_f])
                        cp(k, "act", S_b[:], S_f[:], [S_f], [S_b])
                        yield
                        if hq["last"]:
                            k.dma("sp", otv[d, :, :, t0:t0 + n], OTb[:, :, :n], R=[OTb], W=[OT])
                        rdone[0] = idx + 1
                        yield

                return prep(), recur()

            gens = list(dn_dir(0)) + list(dn_dir(1))
            alive = [True] * len(gens)
            while any(alive):
                for gi in range(len(gens)):
                    if alive[gi]:
                        try:
                            next(gens[gi])
                        except StopIteration:
                            alive[gi] = False
            k.barrier()
        k.es = es0

    def phase_D(l):
        lam_init = 0.8 - 0.6 * math.exp(-0.3 * l)
        with ExitStack() as es:
            k.es = es
            krv = KR.ap.rearrange("(h f) t -> f h t", h=4)
            KS = [k.sb([128, 4, T], BF16, "KS%d" % s_) for s_ in range(2)]
            QRa = k.sb([128, 4, T], BF16, "QRa")
            Va = k.sb([128, T // 128, 512], BF16, "Va")
            k.op("pool", lambda g: g.memset(KS[0][64:128], 0.0), W=[KS[0]])
            k.op("pool", lambda g: g.memset(KS[1][0:64], 0.0), W=[KS[1]])
            k.dma("sp", KS[0][0:64], krv[0:64], R=[KR], W=[KS[0]])
            k.dma("sp", KS[1][64:128], krv[64:128], R=[KR], W=[KS[1]])
            k.dma("sp", QRa[:], QR.ap.rearrange("(h f) t -> f h t", h=4), R=[QR], W=[QRa])
            vtv = VT.ap.rearrange("(kt p) v -> p kt v", p=128)
            for g in range(0, T // 128, 6):
                g1 = min(g + 6, T // 128)
                k.dma("sp", Va[:, g:g1, :], vtv[:, g:g1, :], R=[VT], W=[Va])
            dl = k.sb([128, 2, 2, 64], F32, "dl")
            k.dma("sp", dl[:].rearrange("p a b f -> p (a b f)"), da_lam.ap[l:l + 1, :].partition_broadcast(128) if False else da_lam.ap[l].partition_broadcast(128), R=[da_lam], W=[dl])
            pr = k.sb([128, 2, 64], F32, "pr")
            sv = k.sb([128, 4], F32, "sv")
            dnw = k.sb([128, 1], F32, "dnw")
            k.dma("sp", dnw[:], da_normT.ap[l], R=[da_normT], W=[dnw])
            tt(k, "dve", pr[:], dl[:, :, 0, :], dl[:, :, 1, :], ALU.mult, [dl], [pr])
            k.op("dve", lambda g: g.tensor_reduce(out=sv[:, 0:2], in_=pr[:], axis=mybir.AxisListType.X, op=ALU.add), R=[pr], W=[sv])
            act(k, sv[:, 0:2], sv[:, 0:2], AF.Exp, [sv], [sv])
            tt(k, "dve", sv[:, 2:3], sv[:, 1:2], sv[:, 0:1], ALU.subtract, [sv], [sv])
            ts(k, "dve", sv[:, 2:3], sv[:, 2:3], -lam_init, None, ALU.add, None, [sv], [sv])
            ts(k, "dve", dnw[:], dnw[:], 1.0 - lam_init, None, ALU.mult, None, [dnw], [dnw])
            pst = Rot([k.ps([128, 512], F32, "pst") for _ in range(3)])
            poS = [k.ps([128, 512], F32, "po%d" % s_) for s_ in range(2)]
            plS = [k.ps([128, 512], F32, "pl%d" % s_) for s_ in range(2)]
            pn = k.ps([128, 512], F32, "pn")
            ptR = Rot([k.sb([128, 512], BF16, "pt") for _ in range(6)])
            oS = [Rot([k.sb([128, 512], F32, "oS%d" % s_) for _ in range(2)]) for s_ in range(2)]
            lS = [Rot([k.sb([128, 512], F32, "lS%d" % s_) for _ in range(2)]) for s_ in range(2)]
            accR = Rot([k.sb([128, 512], F32, "accD") for _ in range(2)])
            sqdR = Rot([k.sb([128, 512], BF16, "sqD") for _ in range(2)])
            rnR = Rot([k.sb([128, 512], F32, "rnD") for _ in range(2)])
            deferred = []
            ycR = Rot([k.sb([128, T], BF16, "ycrow") for _ in range(2)])
            LOOK = 2
            for h in range(4):
                yc = ycR.get()
                for (t0, n) in BLOCKS:
                    kts = [0, 1] if t0 < NCTX else list(range(T // 128))
                    items = [(s_, i_, kt) for i_, kt in enumerate(kts) for s_ in range(2)]
                    pend = []

                    def flush_one():
                        s_, i_, kt, pt_ = pend.pop(0)
                        mm(k, poS[s_][:, :n], Va[:, kt, h * 128:(h + 1) * 128], pt_[:, :n], i_ == 0, i_ == len(kts) - 1, [Va, pt_], [poS[s_]])
                        mm(k, plS[s_][:, :n], onesB[:], pt_[:, :n], i_ == 0, i_ == len(kts) - 1, [onesB, pt_], [plS[s_]])
                    for it_, (s_, i_, kt) in enumerate(items):
                        st = pst.get()
                        mm(k, st[:, :n], KS[s_][:, h, kt * 128:(kt + 1) * 128], QRa[:, h, t0:t0 + n], True, True, [KS[s_], QRa], [st])
                        pt_ = ptR.get()
                        act(k, pt_[:, :n], st[:, :n], AF.Exp, [st], [pt_], scale=0.125)
                        pend.append((s_, i_, kt, pt_))
                        if len(pend) > LOOK:
                            flush_one()
                        if it_ == 12:
                            while deferred:
                                deferred.pop(0)()
                    while pend:
                        flush_one()
                    while deferred:
                        deferred.pop(0)()
                    os_, ls_ = [], []
                    for s_ in range(2):
                        o_ = oS[s_].get()
                        l_ = lS[s_].get()
                        cp(k, "dve", o_[:, :n], poS[s_][:, :n], [poS[s_]], [o_])
                        cp(k, "dve", l_[:, :n], plS[s_][:, :n], [plS[s_]], [l_])
                        os_.append(o_)
                        ls_.append(l_)
                    for s_ in range(2):
                        k.op("dve", lambda g: g.reciprocal(out=ls_[s_][:, :n], in_=ls_[s_][:, :n]), R=[ls_[s_]], W=[ls_[s_]])
                    acc, sqd, rn = accR.get(), sqdR.get(), rnR.get()
                    tt(k, "dve", acc[:, :n], os_[0][:, :n], ls_[0][:, :n], ALU.mult, [os_[0], ls_[0]], [acc])
                    tt(k, "pool", os_[1][:, :n], os_[1][:, :n], ls_[1][:, :n], ALU.mult, [os_[1], ls_[1]], [os_[1]])
                    stt(k, acc[:, :n], os_[1][:, :n], sv[:, 2:3], acc[:, :n], ALU.mult, ALU.add, [os_[1], sv, acc], [acc])
                    k.op("pool", lambda g: g.tensor_tensor(out=sqd[:, :n], in0=acc[:, :n], in1=acc[:, :n], op=ALU.mult), R=[acc], W=[sqd])

                    def part2(acc=acc, sqd=sqd, rn=rn, yc=yc, t0=t0, n=n):
                        mm(k, pn[:, :n], onesB[:], sqd[:, :n], True, True, [onesB, sqd], [pn])
                        act(k, rn[:, :n], pn[:, :n], AF.Sqrt, [pn, cst], [rn], scale=1.0 / 128, bias=EPS5)
                        k.op("dve", lambda g: g.reciprocal(out=rn[:, :n], in_=rn[:, :n]), R=[rn], W=[rn])
                        stt(k, yc[:, t0:t0 + n], acc[:, :n], dnw[:, 0:1], rn[:, :n], ALU.mult, ALU.mult, [acc, dnw, rn], [yc])
                    deferred.append(part2)
                while deferred:
                    deferred.pop(0)()
                k.dma("sp", YC.ap[h * 128:(h + 1) * 128, :], yc[:], R=[yc], W=[YC])
            k.barrier()
        k.es = es0

    def phase_E(l, last):
        with ExitStack() as es:
            k.es = es
            Wbr = k.sb([128, 3, 4, D], BF16, "Wbr")
            Wo = k.sb([128, 8, D], BF16, "Wo")
            for nb in range(3):
                k.dma("pool", Wbr[:, nb], w_branch.ap[l, nb].rearrange("(k p) c -> p k c", p=128), R=[w_branch], W=[Wbr])
            for h2 in range(2):
                k.dma("pool", Wo[:, h2 * 4:h2 * 4 + 4, :], w_out.ap[l].rearrange("(k p) c -> p k c", p=128)[:, h2 * 4:h2 * 4 + 4, :], R=[w_out], W=[Wo])
            dnw = k.sb([128, 1], F32, "dnwE")
            k.dma("sp", dnw[:], dn_normT.ap[l], R=[dn_normT], W=[dnw])
            WguR = Rot([k.sb([128, 8, 256], BF16, "Wgu") for _ in range(4)])
            WdR = Rot([k.sb([128, 22, 128], BF16, "Wd") for _ in range(2)])
            tmp = k.sb([128, 8, 512], F32, "tmpE")
            sq = k.sb([128, 8, 512], BF16, "sqE")
            zs = k.sb([128, 4, 512], BF16, "zsE")
            ya = k.sb([128, 4, 512], BF16, "yaE")
            yb = k.sb([128, 4, 512], BF16, "ybE")
            yc = k.sb([128, 4, 512], BF16, "ycE")
            sgR = Rot([k.sb([128, 3, 512], BF16, "sgE") for _ in range(2)])
            merged = k.sb([128, 8, 512], BF16, "mergedE")
            HB = [k.sb([128, 8, 512], F32, "hbE%d" % i) for i in range(2)]
            U2 = [k.sb([128, 8, 512], BF16, "u2E%d" % i) for i in range(2)]
            hid = k.sb([128, 22, 512], BF16, "hidE")
            rs = k.sb([128, 512], F32, "rsE")
            rs2 = k.sb([128, 512], F32, "rs2E")
            macc = k.sb([128, 512], F32, "maccE")
            mt = k.sb([128, 512], F32, "mtE")
            sgl = Rot([k.sb([128, 512], F32, "sglE") for _ in range(2)])
            otbufs = [Buf(hid.ap[:, 8 + 4 * i:12 + 4 * i, :].bitcast(F32).rearrange("p a b -> p (a b)"), "otE%d" % i) for i in range(2)]
            otile = Rot(otbufs)
            pg1 = Rot([k.ps([128, 512], F32, "pgE1") for _ in range(3)])
            pg2 = Rot([k.ps([128, 512], F32, "pgE2") for _ in range(3)])
            pss = k.ps([128, 512], F32, "pssE")
            ppT = k.ps([128, 4, 128], F32, "ppTE")
            ytv = lambda Y: Y.ap.rearrange("(k p) t -> p k t", p=128)
            otv = OT.ap.rearrange("d (h f) t -> d f h t", h=4)
            sgv = SG.ap.rearrange("(nb dc p) t -> p nb dc t", nb=3, dc=8)
            wgv = w_g.ap[l].rearrange("(k p) c -> p k c", p=128)
            wuv = w_u.ap[l].rearrange("(k p) c -> p k c", p=128)
            wdv = w_d.ap[l].rearrange("(fc p) c -> p fc c", p=128)
            blocks = [bk for bk in BLOCKS if not (last and bk[0] < NCTX)]

            def stage1(bi):
                t0, n = blocks[bi]
                j = 0 if t0 < NCTX else 1
                hb, u2 = HB[bi % 2], U2[bi % 2]
                k.dma("sp", tmp[:, 0:4, :n], otv[0, :, :, t0:t0 + n], R=[OT], W=[tmp])
                k.dma("sp", tmp[:, 4:8, :n], otv[1, :, :, t0:t0 + n], R=[OT], W=[tmp])
                k.dma("sp", zs[:, :, :n], ytv(ZS)[:, :, t0:t0 + n], R=[ZS], W=[zs])
                k.dma("sp", yb[:, :, :n], ytv(YB)[:, :, t0:t0 + n], R=[YB], W=[yb])
                k.dma("sp", yc[:, :, :n], ytv(YC)[:, :, t0:t0 + n], R=[YC], W=[yc])
                k.dma("sp", hb[:, :, :n], hTv[:, :, t0:t0 + n], R=[hT], W=[hb])
                yield
                tt(k, "pool", tmp[:, 0:4, :n], tmp[:, 0:4, :n], tmp[:, 4:8, :n], ALU.add, [tmp], [tmp])
                act(k, sq[:, 0:4, :n], tmp[:, 0:4, :n], AF.Square, [tmp], [sq])
                yield
                for h in range(4):
                    mm(k, pss[:, :n], onesB[:], sq[:, h, :n], True, True, [onesB, sq], [pss])
                    act(k, rs[:, :n], pss[:, :n], AF.Sqrt, [pss, cst], [rs], scale=1.0 / 128, bias=EPS6)
                    k.op("dve", lambda g: g.reciprocal(out=rs[:, :n], in_=rs[:, :n]), R=[rs], W=[rs])
                    tt(k, "dve", tmp[:, 4 + h, :n], tmp[:, h, :n], rs[:, :n], ALU.mult, [tmp, rs], [tmp])
                    stt(k, ya[:, h, :n], tmp[:, 4 + h, :n], dnw[:, 0:1], zs[:, h, :n], ALU.mult, ALU.mult, [tmp, dnw, zs], [ya])
                    yield
                for _ in range(4):
                    yield
                ys = [ya, yb, yc]
                for dc in range(8):
                    sg = sgR.get()
                    k.dma("sp", sg[:, :, :n], sgv[:, :, dc, t0:t0 + n], R=[SG], W=[sg])
                    for nb in range(3):
                        pu = pg1.get()
                        for k4 in range(4):
                            mm(k, pu[:, :n], Wbr[:, nb, k4, dc * 128:(dc + 1) * 128], ys[nb][:, k4, :n], k4 == 0, k4 == 3, [Wbr, ys[nb]], [pu])
                        if nb == 0:
                            tt(k, "dve", macc[:, :n], pu[:, :n], sg[:, 0, :n], ALU.mult, [pu, sg], [macc])
                        else:
                            tt(k, "dve", mt[:, :n], pu[:, :n], sg[:, nb, :n], ALU.mult, [pu, sg], [mt])
                            if nb == 1:
                                tt(k, "pool", macc[:, :n], macc[:, :n], mt[:, :n], ALU.add, [macc, mt], [macc])
                            else:
                                tt(k, "pool", merged[:, dc, :n], macc[:, :n], mt[:, :n], ALU.add, [macc, mt], [merged])
                        yield
                for dc in range(8):
                    py = pg1.get()
                    for k8 in range(8):
                        mm(k, py[:, :n], Wo[:, k8, dc * 128:(dc + 1) * 128], merged[:, k8, :n], k8 == 0, k8 == 7, [Wo, merged], [py])
                    stt(k, hb[:, dc, :n], py[:, :n], AB[:, l, 2, dc, j:j + 1], hb[:, dc, :n], ALU.mult, ALU.add, [py, AB, hb], [hb])
                    yield
                act(k, sq[:, :, :n], hb[:, :, :n], AF.Square, [hb], [sq])
                yield
                for kk in range(8):
                    mm(k, pss[:, :n], onesB[:], sq[:, kk, :n], kk == 0, kk == 7, [onesB, sq], [pss])
                act(k, rs[:, :n], pss[:, :n], AF.Sqrt, [pss, cst], [rs], scale=1.0 / D, bias=EPS6)
                k.op("dve", lambda g: g.reciprocal(out=rs[:, :n], in_=rs[:, :n]), R=[rs], W=[rs])
                yield
                for kk in range(8):
                    tt(k, "dve", tmp[:, kk, :n], hb[:, kk, :n], rs[:, :n], ALU.mult, [hb, rs], [tmp])
                    act(k, u2[:, kk, :n], tmp[:, kk, :n], AF.Identity, [tmp, AB], [u2],
                        scale=AB[:, l, 3, kk, j:j + 1], bias=AB[:, l, 4, kk, j:j + 1])
                    if kk % 2 == 1:
                        yield

            def stage2(bi):
                t0, n = blocks[bi]
                j = 0 if t0 < NCTX else 1
                hb, u2 = HB[bi % 2], U2[bi % 2]
                for f2 in range(11):
                    wg_ = WguR.get()
                    k.dma("sp", wg_[:], WGT.ap[f2], R=[WGT], W=[wg_])
                    wu_ = WguR.get()
                    k.dma("sp", wu_[:], WUT.ap[f2], R=[WUT], W=[wu_])
                    for c2 in range(2):
                        fc = f2 * 2 + c2
                        pg_ = pg2.get()
                        for kk in range(8):
                            mm(k, pg_[:, :n], wg_[:, kk, c2 * 128:(c2 + 1) * 128], u2[:, kk, :n], kk == 0, kk == 7, [wg_, u2], [pg_])
                        pu_ = pg2.get()
                        for kk in range(8):
                            mm(k, pu_[:, :n], wu_[:, kk, c2 * 128:(c2 + 1) * 128], u2[:, kk, :n], kk == 0, kk == 7, [wu_, u2], [pu_])
                        sg_ = sgl.get()
                        act(k, sg_[:, :n], pg_[:, :n], AF.Silu, [pg_], [sg_])
                        tt(k, "dve", hid[:, fc, :n], sg_[:, :n], pu_[:, :n], ALU.mult, [sg_, pu_], [hid] + (otbufs if (last and 8 <= fc < 16) else []))
                        yield
                for dc in range(8):
                    wd_ = WdR.get()
                    k.dma("sp", wd_[:], WDT.ap[dc], R=[WDT], W=[wd_])
                    py = pg2.get()
                    for fc in range(22):
                        mm(k, py[:, :n], wd_[:, fc, :], hid[:, fc, :n], fc == 0, fc == 21, [wd_, hid], [py])
                    stt(k, hb[:, dc, :n], py[:, :n], AB[:, l, 5, dc, j:j + 1], hb[:, dc, :n], ALU.mult, ALU.add, [py, AB, hb], [hb])
                    yield
                if not last:
                    k.dma("sp", hTv[:, :, t0:t0 + n], hb[:, :, :n], R=[hb], W=[hT])
                    yield
                else:
                    sq2 = hid
                    act(k, sq2[:, 0:8, :n], hb[:, :, :n], AF.Square, [hb], [sq2])
                    for kk in range(8):
                        mm(k, ppT[:].rearrange("p a b -> p (a b)")[:, :n], onesB[:], sq2[:, kk, :n], kk == 0, kk == 7, [onesB, sq2], [ppT])
                    act(k, rs2[:, :n], ppT[:].rearrange("p a b -> p (a b)")[:, :n], AF.Sqrt, [ppT, cst], [rs2], scale=1.0 / D, bias=EPS6)
                    k.op("dve", lambda g: g.reciprocal(out=rs2[:, :n], in_=rs2[:, :n]), R=[rs2], W=[rs2])
                    yield
                    for kk in range(8):
                        stt(k, hb[:, kk, :n], hb[:, kk, :n], nrmF[:, kk:kk + 1], rs2[:, :n], ALU.mult, ALU.mult, [hb, nrmF, rs2], [hb])
                    yield
                    for q in range(n // 128):
                        ot = otile.get()
                        for half in range(2):
                            for qq in range(4):
                                kk = half * 4 + qq
                                tr(k, ppT[:, qq, :], hb[:, kk, q * 128:(q + 1) * 128], identF[:], [hb, identF], [ppT])
                            cp(k, k.alt(), ot[:, half * 512:(half + 1) * 512], ppT[:].rearrange("p a b -> p (a b)"), [ppT], [ot, hid])
                        r0 = t0 - NCTX + q * 128
                        k.dma("sp", out_d.ap[r0:r0 + 128, :], ot[:], R=[ot], W=[out_d])
                        yield

            def run_pair(g1, g2):
                gens = [g for g in (g1, g2) if g is not None]
                alive = [True] * len(gens)
                while any(alive):
                    for gi in range(len(gens)):
                        if alive[gi]:
                            try:
                                next(gens[gi])
                            except StopIteration:
                                alive[gi] = False
            nbk = len(blocks)
            run_pair(stage1(0), None)
            for bi in range(nbk):
                run_pair(stage2(bi), stage1(bi + 1) if bi + 1 < nbk else None)
            k.barrier()
        k.es = es0

    done = False
    for l in range(DEPTH):
        last = (l == DEPTH - 1)
        for name, fn in (("A", lambda: phase_A(l)), ("C", lambda: phase_C(l)), ("B", lambda: phase_B(l)),
                         ("D", lambda: phase_D(l)), ("E", lambda: phase_E(l, last))):
            fn()
            if stop_after == "%s%d" % (name, l):
                done = True
                break
        if done:
            break
    k.barrier()
    es0.close()
    return nc, k


def _consts():
    c = {}
    c["c_ident"] = np.eye(128, dtype=np.float32)
    inv = (10000.0 ** (-np.arange(16, dtype=np.float32) / 16)).astype(np.float32)
    lt = np.arange(NLAT)
    row = (lt // 64).astype(np.float32)
    col = (lt % 64).astype(np.float32)
    cos = np.ones((128, T), np.float32)
    sin = np.zeros((128, T), np.float32)
    perm = np.zeros((128, 128), np.float32)
    for p in range(128):
        e = p % 64
        axis = e // 32
        half = (e % 32) // 16
        f = e % 16
        pos = row if axis == 0 else col
        ang = (pos * inv[f]).astype(np.float32)
        cos[p, NCTX:] = np.cos(ang)
        sn = np.sin(ang)
        sin[p, NCTX:] = -sn if half == 0 else sn
        partner = p + 16 if half == 0 else p - 16
        perm[partner, p] = 1.0
    c["c_rope"] = np.stack([cos, sin]).astype(np.float32)
    c["c_perm"] = perm
    jj, ii = np.meshgrid(np.arange(128), np.arange(128), indexing="ij")
    mb = np.stack([np.where(jj <= ii, 0.0, NEG), np.where(jj >= ii, 0.0, NEG)]).astype(np.float32)
    c["c_mb"] = mb
    c["c_nod"] = (-(1.0 - np.eye(128))).astype(np.float32)
    m01 = np.ones((4, 2, 512), np.float32)
    m01[:, 0, 0::128] = 0.0
    m01[:, 1, 127::128] = 0.0
    c["c_m01"] = m01
    sel = np.zeros((4, 4, 128), np.float32)
    for h in range(4):
        sel[h, h, :] = 1.0
    c["c_sel"] = sel
    bl = np.zeros((5, 128, 128), np.float32)
    bl[0] = (jj // 8 == ii // 8)
    for m_, b_ in enumerate((8, 16, 32, 64)):
        bl[m_ + 1] = (jj // (2 * b_) == ii // (2 * b_)) & (jj // b_ != ii // b_)
    c["c_blk"] = bl
    return c


def _prep_shared(inp):
    f = lambda a: np.ascontiguousarray(a, dtype=np.float32)
    s = {}
    s["w_mod"] = f(inp["w_mod"])
    s["b_modT"] = f(inp["b_mod"].reshape(DEPTH, 48, 128).transpose(0, 2, 1))
    nm = np.stack([inp["norm_mix"], inp["norm_ffn"]], 1)
    s["normsT"] = f(nm.reshape(DEPTH, 2, 8, 128).transpose(3, 0, 1, 2))
    s["normfT"] = f(inp["norm_final"].reshape(8, 128).T)
    s["w_in"] = f(inp["w_in"])
    s["dn_convT"] = f(inp["dn_conv"].reshape(DEPTH, 4, 12, 128).transpose(0, 3, 2, 1))
    s["dnab"] = f(np.concatenate([inp["dn_a_log"].transpose(0, 2, 1), inp["dn_dt_bias"].transpose(0, 2, 1)], -1))
    s["dn_normT"] = f(inp["dn_norm"].reshape(DEPTH, 128, 1))
    s["lru_cw"] = f(inp["lru_conv_w"].reshape(DEPTH, 4, 4, 128).transpose(0, 3, 2, 1))
    s["lru_cb"] = f(inp["lru_conv_b"].reshape(DEPTH, 4, 128).transpose(0, 2, 1))
    lv = np.stack([inp["lru_ba"], inp["lru_bi"], inp["lru_lambda"]], 1)
    s["lru_vec"] = f(lv.reshape(DEPTH, 3, 2, 4, 128).transpose(0, 4, 1, 2, 3))
    wb = np.zeros((DEPTH, 2, 2, 4, 128, 128), np.float32)
    for ai, w in enumerate([inp["lru_wa"], inp["lru_wi"]]):
        for cc in range(4):
            for g2 in range(2):
                wb[:, ai, :, cc, g2 * 64:(g2 + 1) * 64, g2 * 64:(g2 + 1) * 64] = w[:, :, cc * 2 + g2]
    s["lru_wblk"] = wb
    s["da_lam"] = f(inp["da_lambda"].reshape(DEPTH, 256))
    s["da_normT"] = f(inp["da_norm"].reshape(DEPTH, 128, 1))
    s["w_branch"] = f(inp["w_branch"])
    s["w_out"] = f(inp["w_out"])
    s["w_ffn_gate"] = f(inp["w_ffn_gate"])
    s["w_ffn_up"] = f(inp["w_ffn_up"])
    s["w_ffn_down"] = f(inp["w_ffn_down"])
    s.update(_consts())
    return s


def _prep_core(inp, b):
    m = {}
    m["xin"] = np.ascontiguousarray(np.concatenate([inp["ctx"][b], inp["x"][b]], 0), dtype=np.float32)
    cv = np.stack([inp["c_ctx"], inp["c"][b]], 0)
    m["cT"] = np.ascontiguousarray(cv.reshape(2, 8, 128).transpose(2, 1, 0), dtype=np.float32)
    return m


_CACHE = {}


def kernel(**inputs):
    inp = {k_: np.asarray(v) for k_, v in inputs.items()}
    if "nc" not in _CACHE:
        _CACHE["nc"] = build()[0]
    nc = _CACHE["nc"]
    shared = _prep_shared(inp)
    in_maps = []
    for b in range(8):
        m = dict(shared)
        m.update(_prep_core(inp, b))
        in_maps.append(m)
    res = run_bass_kernel_spmd(nc, in_maps, core_ids=list(range(8)))
    return np.stack([np.asarray(r["out"], dtype=np.float32) for r in res.results], 0)
```
